# Optimizing a Trainium2 kernel written in Bass

```python
import jax, jax.numpy as jnp
from jax import lax
import numpy as np

D_MODEL = 1024
BATCH = 2
SEQ = 16384
DEPTH = 4
DEC_BATCH = 8
DEC_SEQ = 4096
PAST_LEN = 128

N_EVEN = (DEPTH + 1) // 2
N_ODD = DEPTH // 2
D_FF = 2816
CHUNK = 64
LN_EPS = 1e-5
ALPHA = (2 * DEPTH) ** 0.25
BETA = (8 * DEPTH) ** -0.25
NEG_BIG = -1e30
TINY = 1e-30

A_HEADS = 4
A_DH = 128
A_W = A_HEADS * A_DH
CONV_W = 5
B_HEADS = 4
B_DK = 64
B_DV = 128
B_KW = B_HEADS * B_DK
B_VW = B_HEADS * B_DV
B_RANK = 16
GLA_GATE_NORMALIZER = 16.0
C_HEADS = 4
C_DK = 128
C_DV = 128
C_KW = C_HEADS * C_DK
C_VW = C_HEADS * C_DV
R_HEADS = 4
R_DK = 128
R_DV = 128
R_KW = R_HEADS * R_DK
R_VW = R_HEADS * R_DV
ROPE_BASE = 10000.0

EV_SIZES = (A_W, A_W, A_W, A_W, 4 * A_HEADS, B_KW, B_KW, B_VW, B_VW, 2 * B_RANK)
EV_COLS = 4 * A_W + 4 * A_HEADS + 2 * B_KW + 2 * B_VW + 2 * B_RANK
EV_MIX = A_W + B_VW
OD_SIZES = (C_KW, C_KW, C_KW, C_VW, C_VW, R_KW, R_KW, R_VW, R_VW)
OD_COLS = 3 * C_KW + 2 * C_VW + 2 * R_KW + 2 * R_VW
OD_MIX = C_VW + R_VW

kernel_name = "hybrid_bidir_mlstm_gla_hgrn2_retnet_encoder"


def _split(y, sizes):
    out, off = [], 0
    for s in sizes:
        out.append(y[..., off:off + s])
        off += s
    return out


def _heads(t, n_heads):
    b, n, c = t.shape
    return t.reshape(b, n, n_heads, c // n_heads).transpose(0, 2, 1, 3)


def _merge_heads(t):
    b, h, n, d = t.shape
    return t.transpose(0, 2, 1, 3).reshape(b, n, h * d)


def _dir_stack(fwd, bwd):
    return jnp.concatenate([fwd, jnp.flip(bwd, axis=2)], axis=0)


def _dir_merge(y):
    b = y.shape[0] // 2
    return y[:b] + jnp.flip(y[b:], axis=2)


def _to_chunks(t):
    b, h, n, d = t.shape
    return jnp.moveaxis(t.reshape(b, h, n // CHUNK, CHUNK, d), 2, 0)


def _from_chunks(t):
    nc, b, h, l, d = t.shape
    return jnp.moveaxis(t, 0, 2).reshape(b, h, nc * l, d)


def _layer_norm(x, g, b):
    xf = x.astype(jnp.float32)
    mu = jnp.mean(xf, -1, keepdims=True)
    var = jnp.mean(jnp.square(xf - mu), -1, keepdims=True)
    return ((xf - mu) * lax.rsqrt(var + LN_EPS) * g.astype(jnp.float32) + b.astype(jnp.float32)).astype(x.dtype)


def _head_layernorm(x):
    xf = x.astype(jnp.float32)
    mu = jnp.mean(xf, -1, keepdims=True)
    var = jnp.mean(jnp.square(xf - mu), -1, keepdims=True)
    return ((xf - mu) * lax.rsqrt(var + LN_EPS)).astype(x.dtype)


def _head_rmsnorm(x):
    xf = x.astype(jnp.float32)
    return (xf * lax.rsqrt(jnp.mean(jnp.square(xf), -1, keepdims=True) + LN_EPS)).astype(x.dtype)


def _swiglu(x, w_in, w_out):
    g, u = jnp.split(x @ w_in, 2, axis=-1)
    return (jax.nn.silu(g) * u) @ w_out


def _centred_dwconv(x, w):
    pad = CONV_W // 2
    n = x.shape[1]
    xp = jnp.pad(x, ((0, 0), (pad, pad), (0, 0)))
    acc = xp[:, 0:n, :] * w[0]
    for j in range(1, CONV_W):
        acc = acc + xp[:, j:j + n, :] * w[j]
    return acc


def _rotary(x):
    n, d = x.shape[2], x.shape[3]
    inv = 1.0 / (ROPE_BASE ** (jnp.arange(0, d, 2, dtype=jnp.float32) / d))
    ang = jnp.arange(n, dtype=jnp.float32)[:, None] * inv[None, :]
    cos, sin = jnp.cos(ang), jnp.sin(ang)
    xf = x.astype(jnp.float32)
    x1, x2 = xf[..., : d // 2], xf[..., d // 2:]
    return jnp.concatenate([x1 * cos - x2 * sin, x1 * sin + x2 * cos], -1).astype(x.dtype)


def _causal_mask():
    return jnp.tril(jnp.ones((CHUNK, CHUNK), dtype=bool))


def gated_linear_chunked(q, k, v, log_f):
    b, h, n, dk = q.shape
    dv = v.shape[-1]
    xs = tuple(_to_chunks(t.astype(jnp.float32)) for t in (q, k, v, log_f))
    mask = _causal_mask()
    vector_decay = log_f.shape[-1] > 1

    def step(state, inp):
        qc, kc, vc, gc = inp
        bcum = jnp.cumsum(gc, axis=-2)
        b_end = bcum[..., -1:, :]
        o = jnp.einsum('bhtd,bhde->bhte', qc * jnp.exp(bcum), state)
        if vector_decay:
            rel = jnp.where(mask[:, :, None], bcum[..., :, None, :] - bcum[..., None, :, :], NEG_BIG)
            scores = jnp.einsum('bhtd,bhsd,bhtsd->bhts', qc, kc, jnp.exp(rel))
        else:
            rel = jnp.where(mask, bcum[..., :, None, 0] - bcum[..., None, :, 0], NEG_BIG)
            scores = jnp.einsum('bhtd,bhsd->bhts', qc, kc) * jnp.exp(rel)
        o = o + jnp.einsum('bhts,bhse->bhte', scores, vc)
        state = jnp.exp(jnp.swapaxes(b_end, -1, -2)) * state + jnp.einsum('bhsd,bhse->bhde', kc * jnp.exp(b_end - bcum), vc)
        return state, o

    s0 = jnp.zeros((b, h, dk, dv), jnp.float32)
    _, o = lax.scan(step, s0, xs)
    return _from_chunks(o).astype(v.dtype)


def mlstm_chunked(q, k, v, log_i, log_f):
    b, h, n, dk = q.shape
    dv = v.shape[-1]
    xs = tuple(_to_chunks(t.astype(jnp.float32)) for t in (q, k, v, log_i, log_f))
    mask = _causal_mask()

    def step(carry, inp):
        c_mat, n_vec, m = carry
        qc, kc, vc, ic, fc = inp
        ic, fc = ic[..., 0], fc[..., 0]
        bcum = jnp.cumsum(fc, axis=-1)
        dmat = jnp.where(mask, bcum[..., :, None] - bcum[..., None, :] + ic[..., None, :], NEG_BIG)
        m_inter = bcum + m[..., None]
        m_t = jnp.maximum(m_inter, jnp.max(dmat, axis=-1))
        att = jnp.einsum('bhtd,bhsd->bhts', qc, kc) * jnp.exp(dmat - m_t[..., None])
        sc = jnp.exp(m_inter - m_t)
        num = sc[..., None] * jnp.einsum('bhtd,bhde->bhte', qc, c_mat) + jnp.einsum('bhts,bhse->bhte', att, vc)
        den = sc * jnp.einsum('bhtd,bhd->bht', qc, n_vec) + jnp.sum(att, -1)
        out = num / jnp.maximum(jnp.abs(den), jnp.exp(-m_t))[..., None]
        g_end = bcum[..., -1:] - bcum + ic
        m_new = jnp.maximum(bcum[..., -1] + m, jnp.max(g_end, -1))
        decay = jnp.exp(bcum[..., -1] + m - m_new)
        wk = kc * jnp.exp(g_end - m_new[..., None])[..., None]
        c_mat = decay[..., None, None] * c_mat + jnp.einsum('bhsd,bhse->bhde', wk, vc)
        n_vec = decay[..., None] * n_vec + jnp.sum(wk, -2)
        return (c_mat, n_vec, m_new), out

    init = (jnp.zeros((b, h, dk, dv), jnp.float32), jnp.zeros((b, h, dk), jnp.float32), jnp.zeros((b, h), jnp.float32))
    _, o = lax.scan(step, init, xs)
    return _from_chunks(o).astype(v.dtype)


def _even_mixer(x, w_in, gate_b, conv_w, a2_w, a2_b, norm_w, w_out):
    bz = x.shape[0]
    y = x @ w_in
    aq, ak, av, ao, ag, bq, bk, bv, bg, ba = _split(y, EV_SIZES)
    qk = jax.nn.silu(_centred_dwconv(jnp.concatenate([aq, ak], -1), conv_w))
    q = _heads(qk[..., :A_W], A_HEADS)
    k = _heads(qk[..., A_W:], A_HEADS) * A_DH ** -0.5
    v = _heads(av, A_HEADS)
    gates = (ag + gate_b).astype(jnp.float32)
    gates = gates.reshape(bz, -1, 4, A_HEADS).transpose(2, 0, 3, 1)[..., None]
    log_i = _dir_stack(gates[0], gates[2])
    log_f = jax.nn.log_sigmoid(_dir_stack(gates[1], gates[3]))
    h_a = _dir_merge(mlstm_chunked(_dir_stack(q, q), _dir_stack(k, k), _dir_stack(v, v), log_i, log_f))
    h_a = _merge_heads(_head_layernorm(h_a)) * norm_w[:A_W] * jax.nn.sigmoid(ao)
    q = _heads(bq, B_HEADS) * B_DK ** -0.5
    k = _heads(bk, B_HEADS)
    v = _heads(bv, B_HEADS)
    a_pre = jnp.einsum('bnjr,jrc->jbnc', ba.reshape(bz, -1, 2, B_RANK), a2_w) + a2_b[:, None, None, :]
    log_a = jax.nn.log_sigmoid(a_pre.astype(jnp.float32)) / GLA_GATE_NORMALIZER
    log_a = _dir_stack(_heads(log_a[0], B_HEADS), _heads(log_a[1], B_HEADS))
    h_b = _dir_merge(gated_linear_chunked(_dir_stack(q, q), _dir_stack(k, k), _dir_stack(v, v), log_a))
    h_b = _merge_heads(_head_rmsnorm(h_b)) * norm_w[A_W:] * jax.nn.silu(bg)
    return jnp.concatenate([h_a, h_b], -1) @ w_out


def _odd_mixer(x, w_in, lb_logits, layer_idx, norm_w, w_out):
    bz = x.shape[0]
    y = x @ w_in
    cq, cf_fwd, cf_bwd, ci, cg, rq, rk, rv, rg = _split(y, OD_SIZES)
    p = jax.nn.softmax(lb_logits.astype(jnp.float32), axis=0)
    lb = (jnp.cumsum(p, axis=0) - p[0])[layer_idx]

    def forget(z, lb_dir):
        z = z.astype(jnp.float32)
        f = lb_dir + (1.0 - lb_dir) * jax.nn.sigmoid(z)
        log_f = jnp.log(jnp.maximum(f, TINY))
        key = (1.0 - lb_dir) * jax.nn.sigmoid(-z)
        return _heads(log_f, C_HEADS), _heads(key, C_HEADS)

    lf_f, k_f = forget(cf_fwd, lb[0])
    lf_b, k_b = forget(cf_bwd, lb[1])
    q = _heads(jax.nn.silu(cq), C_HEADS) * C_DK ** -0.5
    v = _heads(ci, C_HEADS)
    h_c = _dir_merge(gated_linear_chunked(_dir_stack(q, q), _dir_stack(k_f, k_b), _dir_stack(v, v), _dir_stack(lf_f, lf_b)))
    h_c = _merge_heads(_head_rmsnorm(h_c)) * norm_w * jax.nn.silu(cg)
    q = _rotary(_heads(rq, R_HEADS))
    k = _rotary(_heads(rk, R_HEADS)) * R_DK ** -0.5
    v = _heads(rv, R_HEADS)
    n = x.shape[1]
    log_gamma = jnp.log1p(-(2.0 ** (-5.0 - jnp.arange(R_HEADS, dtype=jnp.float32))))
    lg = jnp.broadcast_to(log_gamma[None, :, None, None], (2 * bz, R_HEADS, n, 1))
    h_d = _dir_merge(gated_linear_chunked(_dir_stack(q, q), _dir_stack(k, k), _dir_stack(v, v), lg))
    h_d = _merge_heads(_head_layernorm(h_d)) * jax.nn.silu(rg)
    return jnp.concatenate([h_c, h_d], -1) @ w_out


def _trunk(x, ffn1_w_in, ffn1_w_out, ffn2_w_in, ffn2_w_out, ln_g, ln_b,
           ev_w_in, ev_gate_b, ev_conv_w, ev_gla_a2_w, ev_gla_a2_b, ev_norm_w, ev_w_out,
           od_w_in, od_lb_logits, od_norm_w, od_w_out):
    for l in range(DEPTH):
        j = l // 2
        x = _layer_norm(ALPHA * x + 0.5 * _swiglu(x, ffn1_w_in[l], ffn1_w_out[l]), ln_g[l, 0], ln_b[l, 0])
        if l % 2 == 0:
            mix = _even_mixer(x, ev_w_in[j], ev_gate_b[j], ev_conv_w[j], ev_gla_a2_w[j], ev_gla_a2_b[j], ev_norm_w[j], ev_w_out[j])
        else:
            mix = _odd_mixer(x, od_w_in[j], od_lb_logits, j, od_norm_w[j], od_w_out[j])
        x = _layer_norm(ALPHA * x + mix, ln_g[l, 1], ln_b[l, 1])
        x = _layer_norm(ALPHA * x + 0.5 * _swiglu(x, ffn2_w_in[l], ffn2_w_out[l]), ln_g[l, 2], ln_b[l, 2])
    return x


def setup_inputs(seed: int = 0) -> dict:
    key = jax.random.key(seed)
    ks = jax.random.split(key, 20)

    def nrm(k, shape, scale):
        return jax.random.normal(k, shape, jnp.float32) * scale

    f_bias = jnp.linspace(3.0, 6.0, A_HEADS, dtype=jnp.float32)
    i_bias = jnp.zeros((A_HEADS,), jnp.float32)
    gate_base = jnp.concatenate([i_bias, f_bias, i_bias, f_bias])
    return {
        "x_prompt": nrm(ks[0], (BATCH, SEQ, D_MODEL), 1.0),
        "x_sample": nrm(ks[1], (DEC_BATCH, DEC_SEQ, D_MODEL), 1.0),
        "ffn1_w_in": nrm(ks[2], (DEPTH, D_MODEL, 2 * D_FF), D_MODEL ** -0.5),
        "ffn1_w_out": nrm(ks[3], (DEPTH, D_FF, D_MODEL), BETA * D_FF ** -0.5),
        "ffn2_w_in": nrm(ks[4], (DEPTH, D_MODEL, 2 * D_FF), D_MODEL ** -0.5),
        "ffn2_w_out": nrm(ks[5], (DEPTH, D_FF, D_MODEL), BETA * D_FF ** -0.5),
        "ln_g": 1.0 + nrm(ks[6], (DEPTH, 3, D_MODEL), 0.02),
        "ln_b": nrm(ks[7], (DEPTH, 3, D_MODEL), 0.02),
        "ev_w_in": nrm(ks[8], (N_EVEN, D_MODEL, EV_COLS), D_MODEL ** -0.5),
        "ev_gate_b": gate_base[None, :] + nrm(ks[9], (N_EVEN, 4 * A_HEADS), 0.1),
        "ev_conv_w": nrm(ks[10], (N_EVEN, CONV_W, 2 * A_W), CONV_W ** -0.5),
        "ev_gla_a2_w": nrm(ks[11], (N_EVEN, 2, B_RANK, B_KW), B_RANK ** -0.5),
        "ev_gla_a2_b": nrm(ks[12], (N_EVEN, 2, B_KW), 0.02),
        "ev_norm_w": 1.0 + nrm(ks[13], (N_EVEN, EV_MIX), 0.02),
        "ev_w_out": nrm(ks[14], (N_EVEN, EV_MIX, D_MODEL), BETA * EV_MIX ** -0.5),
        "od_w_in": nrm(ks[15], (N_ODD, D_MODEL, OD_COLS), D_MODEL ** -0.5),
        "od_lb_logits": nrm(ks[16], (N_ODD, 2, C_KW), 0.5),
        "od_norm_w": 1.0 + nrm(ks[17], (N_ODD, C_VW), 0.02),
        "od_w_out": nrm(ks[18], (N_ODD, OD_MIX, D_MODEL), BETA * OD_MIX ** -0.5),
    }


def reference(x_prompt, x_sample, ffn1_w_in, ffn1_w_out, ffn2_w_in, ffn2_w_out, ln_g, ln_b,
              ev_w_in, ev_gate_b, ev_conv_w, ev_gla_a2_w, ev_gla_a2_b, ev_norm_w, ev_w_out,
              od_w_in, od_lb_logits, od_norm_w, od_w_out):
    y_prompt = _trunk(x_prompt, ffn1_w_in, ffn1_w_out, ffn2_w_in, ffn2_w_out, ln_g, ln_b,
                      ev_w_in, ev_gate_b, ev_conv_w, ev_gla_a2_w, ev_gla_a2_b, ev_norm_w, ev_w_out,
                      od_w_in, od_lb_logits, od_norm_w, od_w_out)
    y_sample = _trunk(x_sample, ffn1_w_in, ffn1_w_out, ffn2_w_in, ffn2_w_out, ln_g, ln_b,
                      ev_w_in, ev_gate_b, ev_conv_w, ev_gla_a2_w, ev_gla_a2_b, ev_norm_w, ev_w_out,
                      od_w_in, od_lb_logits, od_norm_w, od_w_out)
    return (y_prompt, y_sample)
```

```python
import numpy as np
from contextlib import ExitStack
import concourse.bass as bass
import concourse.mybir as mybir
from concourse.bass_utils import run_bass_kernel_spmd

F32 = mybir.dt.float32
BF16 = mybir.dt.bfloat16
AF = mybir.ActivationFunctionType
ALU = mybir.AluOpType
AX = mybir.AxisListType

D_MODEL = 1024
DEPTH = 4
D_FF = 2816
LN_EPS = 1e-5
ALPHA = (2 * DEPTH) ** 0.25
N_CORES = 8


class _Op:
    __slots__ = ("eng", "fn", "deps", "is_dma", "key", "cnt", "signal", "sigval", "dma_all")

    def __init__(self, eng, fn, is_dma=False, key=None):
        self.eng = eng
        self.fn = fn
        self.deps = []
        self.is_dma = is_dma
        self.key = key
        self.cnt = 0
        self.signal = False
        self.sigval = 0
        self.dma_all = ()


class Sched:
    CENG = ("pe", "act", "dve", "pool")
    ALLENG = ("pe", "act", "dve", "pool", "sp")

    def __init__(self, nc, es):
        self.nc = nc
        self.es = es
        self.sem = {e: es.enter_context(nc.semaphore("sem_" + e)) for e in self.CENG}
        self.sig_cnt = {e: 0 for e in self.CENG}
        self.dma_sem = {}
        self.dma_cnt = {}
        self.waited = {e: {} for e in self.ALLENG}
        self.tok = {}
        self.ops = []
        self.last = {e: None for e in self.ALLENG}
        self.n_inst = 0

    STRICT = True

    def _need(self, x, w):
        return not (w.eng == "pe" and x.eng == "pe")

    def _add(self, op, reads, writes):
        deps = op.deps
        for t in reads:
            st = self.tok.get(t)
            if st is None:
                st = self.tok[t] = [None, []]
            w = st[0]
            if w is not None and self._need(op, w):
                deps.append(w)
            st[1].append(op)
        for t in writes:
            st = self.tok.get(t)
            if st is None:
                st = self.tok[t] = [None, []]
            w = st[0]
            if w is not None and self._need(op, w):
                deps.append(w)
            for r in st[1]:
                if r is not op and self._need(op, r):
                    deps.append(r)
            st[0] = op
            st[1] = []
        for d in deps:
            d.signal = True
        self.ops.append(op)
        self.last[op.eng] = op

    def op(self, eng, fn, reads=(), writes=()):
        o = _Op(eng, fn)
        self._add(o, reads, writes)
        return o

    def dma(self, eng, key, out, in_, reads=(), writes=()):
        if key not in self.dma_sem:
            self.dma_sem[key] = self.es.enter_context(
                self.nc.semaphore("dsem%d" % len(self.dma_sem)))
            self.dma_cnt[key] = 0
        o = _Op(eng, lambda e: e.dma_start(out=out, in_=in_), is_dma=True, key=key)
        self._add(o, reads, writes)
        self.dma_cnt[key] += 16
        o.cnt = self.dma_cnt[key]
        o.signal = True
        return o

    def barrier(self):
        lasts = [self.last[e] for e in self.ALLENG if self.last[e] is not None]
        dma_all = [(k, c) for k, c in self.dma_cnt.items()]
        for e in self.ALLENG:
            o = _Op(e, None)
            for l in lasts:
                if l.eng != e or l.is_dma:
                    if not l.is_dma:
                        o.deps.append(l)
                        l.signal = True
            o.dma_all = dma_all
            self.ops.append(o)
        self.tok = {}

    def flush(self):
        self.barrier()
        ops = self.ops
        self.ops = []
        for o in ops:
            if o.signal and not o.is_dma:
                self.sig_cnt[o.eng] += 1
                o.sigval = self.sig_cnt[o.eng]
        per = {e: [o for o in ops if o.eng == e] for e in self.ALLENG}
        nc = self.nc

        def emit(eng_name, engine):
            waited = self.waited[eng_name]
            for o in per[eng_name]:
                need = {}
                for d in o.deps:
                    if d.is_dma:
                        k = ("d", d.key)
                        v = d.cnt
                    else:
                        k = ("c", d.eng)
                        v = d.sigval
                    if v > need.get(k, 0):
                        need[k] = v
                for (k, c) in o.dma_all:
                    kk = ("d", k)
                    if c > need.get(kk, 0):
                        need[kk] = c
                for k, v in need.items():
                    if v > waited.get(k, 0):
                        waited[k] = v
                        s = self.dma_sem[k[1]] if k[0] == "d" else self.sem[k[1]]
                        engine.wait_ge(s, v)
                        self.n_inst += 1
                if o.fn is None:
                    continue
                ins = o.fn(engine)
                self.n_inst += 1
                if o.is_dma:
                    ins.then_inc(self.dma_sem[o.key], 16)
                elif o.signal:
                    ins.then_inc(self.sem[o.eng], 1)

        with nc.Block() as blk:
            @blk.tensor
            def _(e):
                emit("pe", e)

            @blk.scalar
            def _(e):
                emit("act", e)

            @blk.vector
            def _(e):
                emit("dve", e)

            @blk.gpsimd
            def _(e):
                emit("pool", e)

            @blk.sync
            def _(e):
                emit("sp", e)
        self.last = {e: None for e in self.ALLENG}


class Builder_:
    def __init__(self, T, seq_len, depth=DEPTH):
        self.T = T
        self.seq_len = seq_len
        self.depth = depth
        self.nc = bass.Bass("TRN2", target_bir_lowering=False)
        self.es = ExitStack()
        self.S = Sched(self.nc, self.es)
        self.uid = 0

    def dram_in(self, name, shape):
        return self.nc.dram_tensor(name, list(shape), F32, kind="ExternalInput").ap()

    def dram_out(self, name, shape):
        return self.nc.dram_tensor(name, list(shape), F32, kind="ExternalOutput").ap()

    DEBUG = False

    def dram_tmp(self, name, shape, dtype=F32):
        if self.DEBUG:
            return self.nc.dram_tensor(name, list(shape), dtype, kind="ExternalOutput").ap()
        return self.nc.dram_tensor(name, list(shape), dtype).ap()

    def sb(self, es, name, shape, dtype):
        self.uid += 1
        return es.enter_context(self.nc.sbuf_tensor("%s_%d" % (name, self.uid), list(shape), dtype))

    def ps(self, es, name, shape, dtype=F32):
        self.uid += 1
        return es.enter_context(self.nc.psum_tensor("%s_%d" % (name, self.uid), list(shape), dtype))

    def make_identity(self, es):
        S = self.S
        ident = self.sb(es, "ident", [128, 128], F32)
        S.dma("sp", "ident", ident[:], self.c_ident, writes=["ident"])
        return ident

    def load_weight_bf16(self, w_dram, wt, K, N, stage_tiles, tokname, scale=None):
        S = self.S
        SW = stage_tiles[0].shape[-1]
        kc = K // 128
        wv = w_dram.rearrange("(k p) n -> p k n", p=128)
        i = 0
        for k in range(kc):
            for n0 in range(0, N, SW):
                n1 = min(N, n0 + SW)
                si = i % len(stage_tiles)
                st = stage_tiles[si]
                stok = "%s_stage%d" % (tokname, si)
                S.dma("sp", stok, st[:, 0:n1 - n0], wv[:, k, n0:n1], writes=[stok])
                eng = ("dve", "pool", "act")[i % 3]
                dst = wt[:, k, n0:n1]
                src = st[:, 0:n1 - n0]
                if eng == "act":
                    S.op("act", lambda e, dst=dst, src=src: e.copy(out=dst, in_=src),
                         reads=[stok], writes=[(tokname, i)])
                else:
                    S.op(eng, lambda e, dst=dst, src=src: e.tensor_copy(out=dst, in_=src),
                         reads=[stok], writes=[(tokname, i)])
                i += 1

    def ffn_phase(self, x_in, x_out, w_in, w_out, g_vec, b_vec):
        S = self.S
        nc = self.nc
        T = self.T
        TT = 256
        NS = TT // 128
        ntiles = T // TT
        FC = D_FF // 128
        with ExitStack() as es:
            ident = self.make_identity(es)
            win = self.sb(es, "win", [128, 8, 2 * D_FF], BF16)
            wout = self.sb(es, "wout", [128, FC, D_MODEL], BF16)
            xs = [self.sb(es, "x%d" % i, [128, NS, D_MODEL], F32) for i in range(2)]
            xT = self.sb(es, "xT", [128, 8, TT], BF16)
            aT = self.sb(es, "aT", [128, FC, TT], BF16)
            sg = [self.sb(es, "sg%d" % i, [128, TT], F32) for i in range(2)]
            gb = self.sb(es, "gb", [128, D_MODEL], F32)
            bb = self.sb(es, "bb", [128, D_MODEL], F32)
            stats = self.sb(es, "stats", [128, 2, 6], F32)
            mv = self.sb(es, "mv", [128, 2], F32)
            rstd = self.sb(es, "rstd", [128, 1], F32)
            nmr = self.sb(es, "nmr", [128, 1], F32)
            pt = [self.ps(es, "pt%d" % i, [128, 512]) for i in range(2)]
            pg = [self.ps(es, "pg%d" % i, [128, 512]) for i in range(2)]
            pu = [self.ps(es, "pu%d" % i, [128, 512]) for i in range(2)]
            py = [self.ps(es, "py%d" % i, [128, 512]) for i in range(2)]

            S.dma("sp", "gb", gb[:], g_vec.partition_broadcast(128), writes=["gb"])
            S.dma("sp", "bb", bb[:], b_vec.partition_broadcast(128), writes=["bb"])
            stage = [xs[0][:, i, :] for i in range(NS)] + [xs[1][:, i, :] for i in range(NS)]
            stage_w = [_View(v) for v in stage]
            self.load_weight_bf16(w_in, win, D_MODEL, 2 * D_FF, stage_w, "win")
            self.load_weight_bf16(w_out, wout, D_FF, D_MODEL, stage_w, "wout")
            S.flush()
            xin_v = x_in.rearrange("(n s p) d -> n p s d", p=128, s=NS)
            xout_v = x_out.rearrange("(n s p) d -> n p s d", p=128, s=NS)

            def load_x(i):
                sl = i % 2
                S.dma("sp", "xld%d" % sl, xs[sl][:], xin_v[i], writes=["x%d" % sl])

            load_x(0)
            tcount = 0
            for i in range(ntiles):
                sl = i % 2
                xt = xs[sl]
                xtok = "x%d" % sl
                if i + 1 < ntiles:
                    load_x(i + 1)
                for k in range(8):
                    p = pt[tcount % 2]
                    ptok = "pt%d" % (tcount % 2)
                    tcount += 1
                    for s in range(NS):
                        S.op("pe", lambda e, p=p, s=s, k=k, xt=xt: e.transpose(
                            out=p[:, s * 128:(s + 1) * 128], in_=xt[:, s, k * 128:(k + 1) * 128],
                            identity=ident[:]), reads=[xtok, "ident"], writes=[ptok])
                    eng = "dve" if k % 2 == 0 else "act"
                    if eng == "dve":
                        S.op("dve", lambda e, p=p, k=k: e.tensor_copy(out=xT[:, k, :], in_=p[:, 0:TT]),
                             reads=[ptok], writes=["xT"])
                    else:
                        S.op("act", lambda e, p=p, k=k: e.copy(out=xT[:, k, :], in_=p[:, 0:TT]),
                             reads=[ptok], writes=["xT"])
                for j in range(FC):
                    b = j % 2
                    g_ps, u_ps = pg[b], pu[b]
                    for k in range(8):
                        S.op("pe", lambda e, g_ps=g_ps, k=k, j=j: e.matmul(
                            g_ps[:, 0:TT], lhsT=win[:, k, j * 128:(j + 1) * 128], rhs=xT[:, k, :],
                            start=(k == 0), stop=(k == 7)), reads=["win", "xT"], writes=["pg%d" % b])
                    for k in range(8):
                        S.op("pe", lambda e, u_ps=u_ps, k=k, j=j: e.matmul(
                            u_ps[:, 0:TT], lhsT=win[:, k, D_FF + j * 128:D_FF + (j + 1) * 128],
                            rhs=xT[:, k, :], start=(k == 0), stop=(k == 7)),
                            reads=["win", "xT"], writes=["pu%d" % b])
                    S.op("act", lambda e, g_ps=g_ps, b=b: e.activation(
                        out=sg[b][:], in_=g_ps[:, 0:TT], func=AF.Silu),
                        reads=["pg%d" % b], writes=["sg%d" % b])
                    S.op("dve", lambda e, u_ps=u_ps, b=b, j=j: e.tensor_tensor(
                        out=aT[:, j, :], in0=sg[b][:], in1=u_ps[:, 0:TT], op=ALU.mult),
                        reads=["sg%d" % b, "pu%d" % b], writes=["aT"])
                for s in range(NS):
                    for c in range(2):
                        y_ps = py[c]
                        for j in range(FC):
                            S.op("pe", lambda e, y_ps=y_ps, j=j, s=s, c=c: e.matmul(
                                y_ps[:], lhsT=aT[:, j, s * 128:(s + 1) * 128],
                                rhs=wout[:, j, c * 512:(c + 1) * 512],
                                start=(j == 0), stop=(j == FC - 1)),
                                reads=["aT", "wout"], writes=["py%d" % c])
                        S.op("dve", lambda e, y_ps=y_ps, s=s, c=c, xt=xt: e.scalar_tensor_tensor(
                            out=xt[:, s, c * 512:(c + 1) * 512], in0=xt[:, s, c * 512:(c + 1) * 512],
                            scalar=2.0 * ALPHA, in1=y_ps[:], op0=ALU.mult, op1=ALU.add),
                            reads=["py%d" % c, xtok], writes=[xtok])
                    if getattr(self, "dbg", None) and i == 0 and s == 0:
                        S.dma("sp", "dbg", self.dbg["z"], xt[:, 0, :], reads=[xtok], writes=["dbgz"])
                        S.dma("sp", "dbg", self.dbg["xT"], xT[:], reads=["xT"], writes=["dbgxT"])
                        S.dma("sp", "dbg", self.dbg["aT"], aT[:], reads=["aT"], writes=["dbgaT"])
                    self.layer_norm_rows(xt[:, s, :], xtok, gb, bb, stats, mv, rstd, nmr, 4.0 * LN_EPS)
                S.dma("sp", "xst%d" % sl, xout_v[i], xs[sl][:], reads=[xtok],
                      writes=[("xout", i)])
            S.flush()

    def layer_norm_rows(self, z, ztok, gb, bb, stats, mv, rstd, nmr, eps, pfx=""):
        S = self.S
        for c in range(2):
            S.op("dve", lambda e, c=c: e.bn_stats(out=stats[:, c, :], in_=z[:, c * 512:(c + 1) * 512]),
                 reads=[ztok], writes=["stats"])
        S.op("dve", lambda e: e.bn_aggr(out=mv[:], in_=stats[:]), reads=["stats"], writes=["mv"])
        S.op("act", lambda e: e.activation(out=rstd[:], in_=mv[:, 1:2], func=AF.Sqrt, bias=float(eps), scale=1.0),
             reads=["mv"], writes=["rstd"])
        S.op("dve", lambda e: e.reciprocal(out=rstd[:], in_=rstd[:]), reads=["rstd"], writes=["rstd"])
        S.op("dve", lambda e: e.scalar_tensor_tensor(out=nmr[:], in0=mv[:, 0:1], scalar=-1.0, in1=rstd[:],
                                                     op0=ALU.mult, op1=ALU.mult),
             reads=["mv", "rstd"], writes=["nmr"])
        S.op("act", lambda e: e.activation(out=z, in_=z, func=AF.Identity, bias=nmr[:], scale=rstd[:]),
             reads=[ztok, "rstd", "nmr"], writes=[ztok])
        S.op("pool", lambda e: e.tensor_tensor(out=z, in0=z, in1=gb[:], op=ALU.mult),
             reads=[ztok, "gb"], writes=[ztok])
        S.op("pool", lambda e: e.tensor_tensor(out=z, in0=z, in1=bb[:], op=ALU.add),
             reads=[ztok, "bb"], writes=[ztok])


class _View:
    def __init__(self, ap):
        self.ap = ap
        self.shape = ap.shape

    def __getitem__(self, k):
        return self.ap[k]


EV_COLS = 3632
OD_COLS = 4608
LOG_GAMMA = [float(np.log1p(-(2.0 ** (-5.0 - h)))) for h in range(4)]
TINY = 1e-30


def _op(S, eng, fn, r, w):
    S.op(eng, fn, reads=r, writes=w)


class MixerMixin:
    def alloc_scratch(self):
        T = self.T
        t = self.dram_tmp
        self.QT = t("s_QT", [8, 128, T], BF16)
        self.KT = [t("s_KTf", [8, 128, T], BF16), t("s_KTb", [8, 128, T], BF16)]
        self.Kt = [t("s_Kf", [T, 8, 128], BF16), t("s_Kb", [T, 8, 128], BF16)]
        self.Vt = t("s_V", [T, 8, 128], BF16)
        self.Gt = [t("s_Gf", [T, 8, 128], F32), t("s_Gb", [T, 8, 128], F32)]
        self.GATE = t("s_GATE", [T, 1024], F32)
        self.OD = [t("s_OF", [T, 1024], F32), t("s_OB", [T, 1024], F32)]

    def proj_phase(self, x_in, even, P):
        S = self.S
        T = self.T
        NT = T // 128
        ncol = EV_COLS if even else OD_COLS
        BND = self.BND // 128
        with ExitStack() as es:
            ident = self.make_identity(es)
            identb = self.sb(es, "identb", [128, 128], BF16)
            win = self.sb(es, "pwin", [128, 8, ncol], BF16)
            xs = [self.sb(es, "px%d" % i, [128, 1024], F32) for i in range(2)]
            xT = self.sb(es, "pxT", [128, 8, 128], BF16)
            Y = [self.sb(es, "Y%d" % i, [128, ncol], F32) for i in range(2)]
            Qk = self.sb(es, "Qk", [128, 8, 128], BF16)
            Kf = self.sb(es, "Kf", [128, 8, 128], BF16)
            Kb = self.sb(es, "Kb", [128, 8, 128], BF16)
            Vk = self.sb(es, "Vk", [128, 8, 128], BF16)
            Gf = self.sb(es, "Gf", [128, 8, 128], F32)
            Gb = self.sb(es, "Gb", [128, 8, 128], F32)
            GA = self.sb(es, "GA", [128, 1024], F32)
            TT = self.sb(es, "TTs", [128, 3, 4, 128], BF16)
            nw = self.sb(es, "nw", [128, 1024], F32)
            tmp1 = self.sb(es, "tmp1", [128, 1024], F32)
            tmp2 = self.sb(es, "tmp2", [128, 1024], F32)
            pp = [self.ps(es, "pp%d" % i, [128, 512]) for i in range(3)]
            ptb = [self.ps(es, "ptb%d" % i, [128, 4, 128], BF16) for i in range(2)]
            psm = self.ps(es, "psm", [128, 512])

            S.op("dve", lambda e: e.tensor_copy(out=identb[:], in_=ident[:]), reads=["ident"], writes=["identb"])
            for tl in (Qk, Kf, Kb, Gf, Gb):
                S.op("pool", lambda e, tl=tl: e.memset(tl[:], 0.0), writes=[tl.name])
            if even:
                A3 = [self.sb(es, "A3_%d" % i, [128, 1024], BF16) for i in range(3)]
                acc = self.sb(es, "cacc", [128, 1024], F32)
                cw = self.sb(es, "cw", [128, 5, 1024], F32)
                sh = self.sb(es, "sh", [128, 8, 128], BF16)
                shk = self.sb(es, "shk", [128, 4, 128], BF16)
                gbb = self.sb(es, "gbb", [128, 16], F32)
                gt = self.sb(es, "gt", [128, 16], F32)
                E1 = self.sb(es, "E1", [128, 16], F32)
                E2 = self.sb(es, "E2", [128, 16], F32)
                a2w = self.sb(es, "a2w", [16, 2, 256], F32)
                a2b = self.sb(es, "a2b", [1, 2, 256], F32)
                ones1 = self.sb(es, "ones1", [1, 128], F32)
                baT = [self.sb(es, "baT%d" % j, [16, 128], F32) for j in range(2)]
                S.dma("sp", "cst1", cw[:], P["conv_w"].partition_broadcast(128), writes=["cw"])
                S.dma("sp", "cst2", sh[:], self.c_shift, writes=["sh"])
                S.dma("sp", "cst3", gbb[:], P["gate_b"].partition_broadcast(128), writes=["gbb"])
                S.dma("sp", "cst4", a2w[:], P["a2_w"].rearrange("j r c -> r j c"), writes=["a2w"])
                S.dma("sp", "cst5", a2b[:], P["a2_b"].rearrange("(o j) c -> o j c", o=1), writes=["a2b"])
                S.dma("sp", "cst6", nw[:], P["norm_w"].partition_broadcast(128), writes=["nw"])
                S.op("pool", lambda e: e.memset(ones1[:], 1.0), writes=["ones1"])
            else:
                lbl = self.sb(es, "lbl", [128, 2, 1024], F32)
                lb = self.sb(es, "lb", [128, 1024], F32)
                oml = self.sb(es, "oml", [128, 1024], F32)
                rot = self.sb(es, "rot", [128, 2, 128], F32)
                S.dma("sp", "cst1", lbl[:], P["lb_logits"].rearrange("l j c -> l (j c)").partition_broadcast(128),
                      writes=["lbl"])
                S.dma("sp", "cst6", nw[:, 0:512], P["norm_w"].partition_broadcast(128), writes=["nw"])
                if P["layer_idx"] == 0:
                    S.op("pool", lambda e: e.memset(lb[:], 0.0), writes=["lb"])
                else:
                    S.op("act", lambda e: e.activation(out=lbl[:], in_=lbl[:], func=AF.Exp), reads=["lbl"], writes=["lbl"])
                    S.op("dve", lambda e: e.tensor_tensor(out=lb[:], in0=lbl[:, 0, :], in1=lbl[:, 1, :], op=ALU.add),
                         reads=["lbl"], writes=["lb"])
                    S.op("dve", lambda e: e.reciprocal(out=lb[:], in_=lb[:]), reads=["lb"], writes=["lb"])
                    S.op("dve", lambda e: e.tensor_tensor(out=lb[:], in0=lb[:], in1=lbl[:, 1, :], op=ALU.mult),
                         reads=["lb", "lbl"], writes=["lb"])
                S.op("dve", lambda e: e.tensor_scalar(out=oml[:], in0=lb[:], scalar1=-1.0, scalar2=1.0,
                                                      op0=ALU.mult, op1=ALU.add), reads=["lb"], writes=["oml"])
                for h in range(4):
                    S.op("pool", lambda e, h=h: e.memset(Gf[:, 4 + h, :], LOG_GAMMA[h]), reads=[Gf.name], writes=[Gf.name])
                    S.op("pool", lambda e, h=h: e.memset(Gb[:, 4 + h, :], LOG_GAMMA[h]), reads=[Gb.name], writes=[Gb.name])
            stage = [(_View(Y[0][:, 0:2048]), ), (_View(Y[1][:, 0:2048]),)]
            self.load_weight_bf16(P["w_in"], win, D_MODEL, ncol, [s[0] for s in stage], "pwin")
            S.flush()
            if even:
                nb = (NT - 1) // BND
                shkb = []
                for b in range(1, nb + 1):
                    tl = self.sb(es, "shkb%d" % b, [128, 4, 128], BF16)
                    S.op("dve", lambda e, tl=tl, b=b: e.tensor_scalar(
                        out=tl[:], in0=sh[:, 4:8, :], scalar1=self.kf[:, b:b + 1], scalar2=None, op0=ALU.mult),
                        reads=["sh", "kf"], writes=[tl.name])
                    shkb.append(tl)

            xin_v = x_in.rearrange("(n p) d -> n p d", p=128)

            def load_x(i):
                S.dma("sp", "pxld%d" % (i % 2), xs[i % 2][:], xin_v[i], writes=["px%d" % (i % 2)])

            def stage1(n):
                xt = xs[n % 2]
                xtok = "px%d" % (n % 2)
                y = Y[n % 2]
                ytok = "Y%d" % (n % 2)
                if n + 1 < NT:
                    load_x(n + 1)
                for k in range(8):
                    p = pp[k % 3]
                    ptok = "pp%d" % (k % 3)
                    S.op("pe", lambda e, p=p, k=k: e.transpose(out=p[:, 0:128], in_=xt[:, k * 128:(k + 1) * 128],
                                                              identity=ident[:]), reads=[xtok, "ident"], writes=[ptok])
                    eng = "dve" if k % 2 == 0 else "act"
                    if eng == "dve":
                        S.op("dve", lambda e, p=p, k=k: e.tensor_copy(out=xT[:, k, :], in_=p[:, 0:128]),
                             reads=[ptok], writes=["pxT"])
                    else:
                        S.op("act", lambda e, p=p, k=k: e.copy(out=xT[:, k, :], in_=p[:, 0:128]),
                             reads=[ptok], writes=["pxT"])
                ci = 0
                for c0 in range(0, ncol, 512):
                    c1 = min(ncol, c0 + 512)
                    p = pp[ci % 3]
                    ptok = "pp%d" % (ci % 3)
                    for k in range(8):
                        S.op("pe", lambda e, p=p, k=k, c0=c0, c1=c1: e.matmul(
                            p[:, 0:c1 - c0], lhsT=xT[:, k, :], rhs=win[:, k, c0:c1], start=(k == 0), stop=(k == 7)),
                            reads=["pxT", "pwin"], writes=[ptok])
                    if ci % 2 == 0:
                        S.op("dve", lambda e, p=p, c0=c0, c1=c1: e.tensor_copy(out=y[:, c0:c1], in_=p[:, 0:c1 - c0]),
                             reads=[ptok], writes=[(ytok, ci)])
                    else:
                        S.op("act", lambda e, p=p, c0=c0, c1=c1: e.copy(out=y[:, c0:c1], in_=p[:, 0:c1 - c0]),
                             reads=[ptok], writes=[(ytok, ci)])
                    ci += 1
                if even:
                    S.op("pool", lambda e: e.tensor_copy(out=A3[n % 3][:], in_=y[:, 0:1024]),
                         reads=[(ytok, 0), (ytok, 1)], writes=["A3_%d" % (n % 3)])
                return ci

            def ytoks(m, c0, c1):
                return [("Y%d" % (m % 2), ci) for ci in range(c0 // 512, (c1 - 1) // 512 + 1)]

            def write_out(m):
                t0 = m * 128
                for ai, (src, dst) in enumerate(((Qk, self.QT), (Kf, self.KT[0]), (Kb, self.KT[1]))):
                    for hg in range(2):
                        pb = ptb[(ai * 2 + hg) % 2]
                        pbt = "ptb%d" % ((ai * 2 + hg) % 2)
                        for h in range(4):
                            S.op("pe", lambda e, pb=pb, h=h, hg=hg, src=src: e.transpose(
                                out=pb[:, h, :], in_=src[:, hg * 4 + h, :], identity=identb[:]),
                                reads=[src.name, "identb"], writes=[pbt])
                        ttok = ("TT", ai, hg)
                        eng = "act" if hg == 0 else "dve"
                        if eng == "act":
                            S.op("act", lambda e, pb=pb, ai=ai: e.copy(out=TT[:, ai, :, :], in_=pb[:]),
                                 reads=[pbt], writes=[("TT", ai)])
                        else:
                            S.op("dve", lambda e, pb=pb, ai=ai: e.tensor_copy(out=TT[:, ai, :, :], in_=pb[:]),
                                 reads=[pbt], writes=[("TT", ai)])
                        S.dma("sp", "wo_tt%d" % ai, dst[hg * 4:hg * 4 + 4, :, t0:t0 + 128].rearrange("h d t -> d h t"),
                              TT[:, ai, :, :], reads=[("TT", ai)], writes=[("dQT", ai, hg, m)])
                S.dma("sp", "wo_kf", self.Kt[0][t0:t0 + 128], Kf[:], reads=[Kf.name], writes=[("dKf", m)])
                S.dma("sp", "wo_kb", self.Kt[1][t0:t0 + 128], Kb[:], reads=[Kb.name], writes=[("dKb", m)])
                S.dma("sp", "wo_v", self.Vt[t0:t0 + 128], Vk[:], reads=[Vk.name], writes=[("dV", m)])
                S.dma("sp", "wo_gf", self.Gt[0][t0:t0 + 128], Gf[:], reads=[Gf.name], writes=[("dGf", m)])
                S.dma("sp", "wo_gb", self.Gt[1][t0:t0 + 128], Gb[:], reads=[Gb.name], writes=[("dGb", m)])
                S.dma("sp", "wo_ga", self.GATE[t0:t0 + 128], GA[:], reads=["GA"], writes=[("dGA", m)])

            def stage2_even(m):
                y = Y[m % 2]
                yn = "Y%d" % (m % 2)
                A = A3[m % 3]
                S.op("dve", lambda e: e.tensor_tensor(out=acc[:], in0=y[:, 0:1024], in1=cw[:, 2, :], op=ALU.mult),
                     reads=ytoks(m, 0, 1024) + ["cw"], writes=["cacc"])
                for oi, o in enumerate((-2, -1, 1, 2)):
                    j = o + 2
                    for half in range(2):
                        p = pp[(oi * 2 + half) % 3]
                        ptok = "pp%d" % ((oi * 2 + half) % 3)
                        mms = [(sh[:, oi, :], A, "A3_%d" % (m % 3), "sh")]
                        if o < 0 and m > 0:
                            if m % BND == 0:
                                mms.append((shkb[m // BND - 1][:, oi, :], A3[(m - 1) % 3], "A3_%d" % ((m - 1) % 3),
                                            shkb[m // BND - 1].name))
                            else:
                                mms.append((sh[:, 4 + oi, :], A3[(m - 1) % 3], "A3_%d" % ((m - 1) % 3), "sh"))
                        if o > 0 and m < NT - 1:
                            if (m + 1) % BND == 0:
                                mms.append((shkb[(m + 1) // BND - 1][:, oi, :], A3[(m + 1) % 3],
                                            "A3_%d" % ((m + 1) % 3), shkb[(m + 1) // BND - 1].name))
                            else:
                                mms.append((sh[:, 4 + oi, :], A3[(m + 1) % 3], "A3_%d" % ((m + 1) % 3), "sh"))
                        for qi, (l, r, rt, lt) in enumerate(mms):
                            S.op("pe", lambda e, p=p, l=l, r=r, half=half, qi=qi, nq=len(mms): e.matmul(
                                p[:], lhsT=l, rhs=r[:, half * 512:(half + 1) * 512], start=(qi == 0), stop=(qi == nq - 1)),
                                reads=[rt, lt], writes=[ptok])
                        S.op("dve", lambda e, p=p, half=half, j=j: e.tensor_tensor(
                            out=tmp1[:, half * 512:(half + 1) * 512], in0=p[:], in1=cw[:, j, half * 512:(half + 1) * 512],
                            op=ALU.mult), reads=[ptok, "cw"], writes=["tmp1"])
                        S.op("pool", lambda e, half=half: e.tensor_tensor(
                            out=acc[:, half * 512:(half + 1) * 512], in0=acc[:, half * 512:(half + 1) * 512],
                            in1=tmp1[:, half * 512:(half + 1) * 512], op=ALU.add), reads=["tmp1", "cacc"], writes=["cacc"])
                S.op("act", lambda e: e.activation(out=tmp2[:], in_=acc[:], func=AF.Silu), reads=["cacc"], writes=["tmp2"])
                S.op("dve", lambda e: e.tensor_copy(out=Qk[:, 0:4, :], in_=tmp2[:, 0:512].rearrange("p (h d) -> p h d", h=4)),
                     reads=["tmp2"], writes=[Qk.name])
                S.op("dve", lambda e: e.tensor_tensor(out=gt[:], in0=y[:, 2048:2064], in1=gbb[:], op=ALU.add),
                     reads=ytoks(m, 2048, 2064) + ["gbb"], writes=["gt"])
                S.op("act", lambda e: e.activation(out=E1[:], in_=gt[:], func=AF.Exp), reads=["gt"], writes=["E1"])
                S.op("act", lambda e: e.activation(out=E2[:], in_=gt[:], func=AF.Exp, scale=-1.0), reads=["gt"], writes=["E2"])
                S.op("act", lambda e: e.activation(out=E2[:], in_=E2[:], func=AF.Ln, bias=1.0), reads=["E2"], writes=["E2"])
                for d, (Kd, Gd) in enumerate(((Kf, Gf), (Kb, Gb))):
                    S.op("dve", lambda e, Kd=Kd, d=d: e.scalar_tensor_tensor(
                        out=Kd[:, 0:4, :], in0=tmp2[:, 512:1024].rearrange("p (h d) -> p h d", h=4), scalar=128.0 ** -0.5,
                        in1=E1[:, 8 * d:8 * d + 4].unsqueeze(2).to_broadcast([128, 4, 128]), op0=ALU.mult, op1=ALU.mult),
                        reads=["tmp2", "E1"], writes=[Kd.name])
                    S.op("pool", lambda e, Gd=Gd, d=d: e.tensor_scalar(
                        out=Gd[:, 0:4, :], in0=E2[:, 8 * d + 4:8 * d + 8].unsqueeze(2).to_broadcast([128, 4, 128]),
                        scalar1=-1.0, scalar2=None, op0=ALU.mult), reads=["E2"], writes=[Gd.name])
                S.op("act", lambda e: e.copy(out=Vk[:, 0:4, :], in_=y[:, 1024:1536].rearrange("p (h d) -> p h d", h=4)),
                     reads=ytoks(m, 1024, 1536), writes=[Vk.name])
                S.op("act", lambda e: e.activation(out=GA[:, 0:512], in_=y[:, 1536:2048], func=AF.Sigmoid),
                     reads=ytoks(m, 1536, 2048), writes=["GA"])
                S.op("dve", lambda e: e.tensor_scalar(
                    out=Qk[:, 4:8, 0:64], in0=y[:, 2064:2320].rearrange("p (h d) -> p h d", h=4), scalar1=64.0 ** -0.5,
                    scalar2=None, op0=ALU.mult), reads=ytoks(m, 2064, 2320), writes=[Qk.name])
                S.op("dve", lambda e: e.tensor_copy(out=Kf[:, 4:8, 0:64], in_=y[:, 2320:2576].rearrange("p (h d) -> p h d", h=4)),
                     reads=ytoks(m, 2320, 2576), writes=[Kf.name])
                S.op("pool", lambda e: e.tensor_copy(out=Kb[:, 4:8, 0:64], in_=y[:, 2320:2576].rearrange("p (h d) -> p h d", h=4)),
                     reads=ytoks(m, 2320, 2576), writes=[Kb.name])
                S.op("act", lambda e: e.copy(out=Vk[:, 4:8, :], in_=y[:, 2576:3088].rearrange("p (h d) -> p h d", h=4)),
                     reads=ytoks(m, 2576, 3088), writes=[Vk.name])
                S.op("act", lambda e: e.activation(out=GA[:, 512:1024], in_=y[:, 3088:3600], func=AF.Silu),
                     reads=ytoks(m, 3088, 3600), writes=["GA"])
                S.op("dve", lambda e: e.tensor_tensor(out=GA[:], in0=GA[:], in1=nw[:], op=ALU.mult),
                     reads=["GA", "nw"], writes=["GA"])
                for j, Gd in enumerate((Gf, Gb)):
                    S.op("pe", lambda e, j=j: e.transpose(out=psm[0:16, 0:128], in_=y[:, 3600 + 16 * j:3616 + 16 * j],
                                                         identity=ident[:]), reads=ytoks(m, 3600, 3632) + ["ident"], writes=["psm"])
                    S.op("dve", lambda e, j=j: e.tensor_copy(out=baT[j][:], in_=psm[0:16, 0:128]), reads=["psm"],
                         writes=["baT%d" % j])
                    S.op("pe", lambda e, j=j: e.matmul(psm[:, 0:256], lhsT=baT[j][:], rhs=a2w[:, j, :], start=True, stop=False),
                         reads=["baT%d" % j, "a2w"], writes=["psm"])
                    S.op("pe", lambda e, j=j: e.matmul(psm[:, 0:256], lhsT=ones1[:], rhs=a2b[:, j, :], start=False, stop=True),
                         reads=["ones1", "a2b"], writes=["psm"])
                    S.op("act", lambda e: e.activation(out=tmp1[:, 0:256], in_=psm[:, 0:256], func=AF.Exp, scale=-1.0),
                         reads=["psm"], writes=["tmp1"])
                    S.op("act", lambda e: e.activation(out=tmp1[:, 0:256], in_=tmp1[:, 0:256], func=AF.Ln, bias=1.0),
                         reads=["tmp1"], writes=["tmp1"])
                    S.op("dve", lambda e, Gd=Gd: e.tensor_scalar(
                        out=Gd[:, 4:8, 0:64], in0=tmp1[:, 0:256].rearrange("p (h d) -> p h d", h=4), scalar1=-1.0 / 16.0,
                        scalar2=None, op0=ALU.mult), reads=["tmp1"], writes=[Gd.name])
                write_out(m)

            def stage2_odd(m):
                y = Y[m % 2]
                t0 = m * 128
                S.dma("sp", "rotld", rot[:], self.c_rot[t0:t0 + 128].rearrange("t (a c) -> t a c", a=2), writes=["rot"])
                S.op("act", lambda e: e.activation(out=tmp1[:, 0:512], in_=y[:, 0:512], func=AF.Silu),
                     reads=ytoks(m, 0, 512), writes=["tmp1"])
                S.op("dve", lambda e: e.tensor_scalar(out=Qk[:, 0:4, :], in0=tmp1[:, 0:512].rearrange("p (h d) -> p h d", h=4),
                                                      scalar1=128.0 ** -0.5, scalar2=None, op0=ALU.mult),
                     reads=["tmp1"], writes=[Qk.name])
                S.op("act", lambda e: e.activation(out=tmp2[:], in_=y[:, 512:1536], func=AF.Sigmoid),
                     reads=ytoks(m, 512, 1536), writes=["tmp2"])
                S.op("dve", lambda e: e.tensor_tensor(out=tmp2[:], in0=tmp2[:], in1=oml[:], op=ALU.mult),
                     reads=["tmp2", "oml"], writes=["tmp2"])
                S.op("dve", lambda e: e.tensor_tensor(out=tmp2[:], in0=tmp2[:], in1=lb[:], op=ALU.add),
                     reads=["tmp2", "lb"], writes=["tmp2"])
                for d, (Kd, Gd) in enumerate(((Kf, Gf), (Kb, Gb))):
                    fv = tmp2[:, d * 512:(d + 1) * 512].rearrange("p (h d) -> p h d", h=4)
                    S.op("pool", lambda e, Kd=Kd, fv=fv: e.tensor_scalar(out=Kd[:, 0:4, :], in0=fv, scalar1=-1.0, scalar2=1.0,
                                                                        op0=ALU.mult, op1=ALU.add), reads=["tmp2"], writes=[Kd.name])
                    S.op("dve", lambda e, fv=fv, d=d: e.tensor_scalar(out=tmp1[:, d * 512:(d + 1) * 512].rearrange("p (h d) -> p h d", h=4),
                                                                    in0=fv, scalar1=TINY, scalar2=None, op0=ALU.max),
                         reads=["tmp2"], writes=["tmp1"])
                    S.op("act", lambda e, Gd=Gd, d=d: e.activation(out=Gd[:, 0:4, :],
                                                                  in_=tmp1[:, d * 512:(d + 1) * 512].rearrange("p (h d) -> p h d", h=4),
                                                                  func=AF.Ln), reads=["tmp1"], writes=[Gd.name])
                S.op("act", lambda e: e.copy(out=Vk[:, 0:4, :], in_=y[:, 1536:2048].rearrange("p (h d) -> p h d", h=4)),
                     reads=ytoks(m, 1536, 2048), writes=[Vk.name])
                S.op("act", lambda e: e.activation(out=GA[:, 0:512], in_=y[:, 2048:2560], func=AF.Silu),
                     reads=ytoks(m, 2048, 2560), writes=["GA"])
                S.op("dve", lambda e: e.tensor_tensor(out=GA[:, 0:512], in0=GA[:, 0:512], in1=nw[:, 0:512], op=ALU.mult),
                     reads=["GA", "nw"], writes=["GA"])
                cosb = rot[:, 0, 0:64].unsqueeze(1).to_broadcast([128, 4, 64])
                sinb = rot[:, 0, 64:128].unsqueeze(1).to_broadcast([128, 4, 64])
                cosk = rot[:, 1, 0:64].unsqueeze(1).to_broadcast([128, 4, 64])
                sink = rot[:, 1, 64:128].unsqueeze(1).to_broadcast([128, 4, 64])
                for (c0, cb, sb_, dsts) in ((2560, cosb, sinb, (Qk,)), (3072, cosk, sink, (Kf, Kb))):
                    xv = y[:, c0:c0 + 512].rearrange("p (h d) -> p h d", h=4)
                    x1, x2 = xv[:, :, 0:64], xv[:, :, 64:128]
                    t1 = tmp1[:, 0:256].rearrange("p (h d) -> p h d", h=4)
                    t2 = tmp1[:, 256:512].rearrange("p (h d) -> p h d", h=4)
                    t3 = tmp1[:, 512:768].rearrange("p (h d) -> p h d", h=4)
                    t4 = tmp1[:, 768:1024].rearrange("p (h d) -> p h d", h=4)
                    yt = ytoks(m, c0, c0 + 512)
                    S.op("dve", lambda e, t1=t1, x1=x1, cb=cb: e.tensor_tensor(out=t1, in0=x1, in1=cb, op=ALU.mult),
                         reads=yt + ["rot"], writes=["tmp1"])
                    S.op("pool", lambda e, t2=t2, x2=x2, sb_=sb_: e.tensor_tensor(out=t2, in0=x2, in1=sb_, op=ALU.mult),
                         reads=yt + ["rot"], writes=["tmp1"])
                    S.op("dve", lambda e, t3=t3, x1=x1, sb_=sb_: e.tensor_tensor(out=t3, in0=x1, in1=sb_, op=ALU.mult),
                         reads=yt + ["rot"], writes=["tmp1"])
                    S.op("pool", lambda e, t4=t4, x2=x2, cb=cb: e.tensor_tensor(out=t4, in0=x2, in1=cb, op=ALU.mult),
                         reads=yt + ["rot"], writes=["tmp1"])
                    for dst in dsts:
                        S.op("dve", lambda e, dst=dst, t1=t1, t2=t2: e.tensor_tensor(out=dst[:, 4:8, 0:64], in0=t1, in1=t2,
                                                                                    op=ALU.subtract), reads=["tmp1"], writes=[dst.name])
                        S.op("pool", lambda e, dst=dst, t3=t3, t4=t4: e.tensor_tensor(out=dst[:, 4:8, 64:128], in0=t3, in1=t4,
                                                                                     op=ALU.add), reads=["tmp1"], writes=[dst.name])
                S.op("act", lambda e: e.copy(out=Vk[:, 4:8, :], in_=y[:, 3584:4096].rearrange("p (h d) -> p h d", h=4)),
                     reads=ytoks(m, 3584, 4096), writes=[Vk.name])
                S.op("act", lambda e: e.activation(out=GA[:, 512:1024], in_=y[:, 4096:4608], func=AF.Silu),
                     reads=ytoks(m, 4096, 4608), writes=["GA"])
                write_out(m)

            load_x(0)
            if even:
                for n in range(NT + 1):
                    if n < NT:
                        stage1(n)
                    if n >= 1:
                        stage2_even(n - 1)
            else:
                for n in range(NT):
                    stage1(n)
                    stage2_odd(n)
            S.flush()

    def scan_phase(self, even):
        S = self.S
        T = self.T
        L = 64
        BT = 256
        CB = BT // L
        NB = T // BT
        BNDC = self.BND // L
        nden = 4 if even else 0
        with ExitStack() as es:
            tri = self.sb(es, "tri", [64, 8, 64], F32)
            S.dma("sp", "cst1", tri[:], self.c_tri, writes=["tri"])
            QTs = [self.sb(es, "sQT%d" % i, [128, 8, BT], BF16) for i in range(2)]
            KTs = [self.sb(es, "sKT%d" % i, [128, 8, BT], BF16) for i in range(2)]
            Ks = [self.sb(es, "sK%d" % i, [64, CB, 8, 128], BF16) for i in range(2)]
            Vs = [self.sb(es, "sV%d" % i, [64, CB, 8, 129], BF16) for i in range(2)]
            Gs = [self.sb(es, "sG%d" % i, [64, CB, 8, 128], F32) for i in range(2)]
            Os = [self.sb(es, "sO%d" % i, [64, CB, 8, 128], F32) for i in range(2)]
            E = self.sb(es, "E", [128, 8, 64], F32)
            Em = self.sb(es, "Em", [128, 8, 64], F32)
            Ek = self.sb(es, "Ek", [128, 8, 64], F32)
            Es = self.sb(es, "Es", [64, 8, 128], F32)
            Q1 = self.sb(es, "Q1", [128, 8, 64], BF16)
            K1 = self.sb(es, "K1", [128, 8, 64], BF16)
            Q2 = self.sb(es, "Q2", [128, 8, 64], BF16)
            K2 = self.sb(es, "K2", [64, 8, 128], BF16)
            PT = [self.sb(es, "PT%d" % i, [64, 64], BF16) for i in range(2)]
            S32 = self.sb(es, "S32", [128, 8, 129], F32)
            Sb = self.sb(es, "Sb", [128, 8, 129], BF16)
            den = self.sb(es, "den", [64, 2], F32)
            p_suf = self.ps(es, "p_suf", [64, 1024])
            p_bc = self.ps(es, "p_bc", [128, 8, 64])
            p_mid = self.ps(es, "p_mid", [128, 8, 64])
            p_st = [self.ps(es, "p_st%d" % i, [64, 64]) for i in range(2)]
            p_o = self.ps(es, "p_o", [64, 129])
            p_u = self.ps(es, "p_u", [128, 129])
            for i in range(2):
                S.op("pool", lambda e, i=i: e.memset(Vs[i][:], 1.0), writes=["sV%d" % i])
            S.flush()

            for d in range(2):
                rev = d == 1
                tinc, tmid, tsuf, msk = (tri[:, 0, :], tri[:, 1, :], tri[:, 2, :], tri[:, 3, :]) if not rev else \
                                        (tri[:, 4, :], tri[:, 5, :], tri[:, 6, :], tri[:, 7, :])
                deccol = 0 if rev else 63
                S32all = [("S32", h) for h in range(8)]
                Sball = [("Sb", h) for h in range(8)]
                S.op("pool", lambda e: e.memset(S32[:], 0.0), reads=S32all, writes=S32all)
                S.op("pool", lambda e: e.memset(Sb[:], 0.0), reads=Sball, writes=Sball)
                blocks = list(range(NB))
                if rev:
                    blocks = blocks[::-1]

                def load_block(bi, slot):
                    t0 = bi * BT
                    S.dma("sp", "sld_q%d" % slot, QTs[slot][:], self.QT[:, :, t0:t0 + BT].rearrange("h d t -> d h t"),
                          reads=[("dQT", 0, hg, t0 // 128 + s_) for hg in range(2) for s_ in range(BT // 128)],
                          writes=["sQT%d" % slot])
                    S.dma("sp", "sld_k%d" % slot, KTs[slot][:], self.KT[d][:, :, t0:t0 + BT].rearrange("h d t -> d h t"),
                          reads=[("dQT", 1 + d, hg, t0 // 128 + s_) for hg in range(2) for s_ in range(BT // 128)],
                          writes=["sKT%d" % slot])
                    S.dma("sp", "sld_kt%d" % slot, Ks[slot][:],
                          self.Kt[d][t0:t0 + BT].rearrange("(c p) h e -> p c h e", p=L),
                          reads=[("dKf" if d == 0 else "dKb", t0 // 128 + s_) for s_ in range(BT // 128)], writes=["sK%d" % slot])
                    for c_ in range(CB):
                        S.dma("sp", "sld_v%d" % slot, Vs[slot][:, c_, :, 0:128],
                              self.Vt[t0 + c_ * L:t0 + (c_ + 1) * L], writes=[("sV", slot, c_)])
                    S.dma("sp", "sld_g%d" % slot, Gs[slot][:],
                          self.Gt[d][t0:t0 + BT].rearrange("(c p) h e -> p c h e", p=L),
                          reads=[("dGf" if d == 0 else "dGb", t0 // 128 + s_) for s_ in range(BT // 128)], writes=["sG%d" % slot])

                load_block(blocks[0], 0)
                for ib, bi in enumerate(blocks):
                    slot = ib % 2
                    if ib + 1 < NB:
                        load_block(blocks[ib + 1], 1 - slot)
                    qn, kn, ktn, gn, on = ("sQT%d" % slot, "sKT%d" % slot, "sK%d" % slot,
                                           "sG%d" % slot, "sO%d" % slot)
                    chunks = list(range(CB))
                    if rev:
                        chunks = chunks[::-1]
                    for c in chunks:
                        gc = bi * CB + c
                        bidx = None
                        if not rev and gc > 0 and gc % BNDC == 0:
                            bidx = gc // BNDC
                        if rev and (gc + 1) % BNDC == 0 and gc + 1 < T // L:
                            bidx = (gc + 1) // BNDC
                        if bidx is not None:
                            S.op("dve", lambda e, bidx=bidx: e.tensor_scalar(
                                out=S32[:], in0=S32[:], scalar1=self.kf[:, bidx:bidx + 1], scalar2=None, op0=ALU.mult),
                                reads=S32all + ["kf"], writes=S32all)
                            S.op("pool", lambda e: e.tensor_copy(out=Sb[:], in_=S32[:]), reads=S32all, writes=Sball)
                        g = Gs[slot][:, c, :, :]
                        vn = ("sV", slot, c)
                        tsl = slice(c * L, (c + 1) * L)
                        for half in range(2):
                            S.op("pe", lambda e, half=half, g=g: e.matmul(
                                p_suf[:, half * 512:(half + 1) * 512], lhsT=tsuf,
                                rhs=g[:, half * 4:half * 4 + 4, :], start=True, stop=True),
                                reads=[gn, "tri"], writes=["p_suf"])
                        for h in range(8):
                            S.op("pe", lambda e, h=h, g=g: e.matmul(p_bc[:, h, :], lhsT=g[:, h, :], rhs=tinc, start=True, stop=True),
                                 reads=[gn, "tri"], writes=["p_bc"])
                        for h in range(8):
                            S.op("pe", lambda e, h=h, g=g: e.matmul(p_mid[:, h, :], lhsT=g[:, h, :], rhs=tmid, start=True, stop=True),
                                 reads=[gn, "tri"], writes=["p_mid"])
                        S.op("act", lambda e: e.activation(out=Es[:], in_=p_suf[:].rearrange("p (h e) -> p h e", h=8), func=AF.Exp),
                             reads=["p_suf"], writes=["Es"])
                        S.op("act", lambda e: e.activation(out=E[:], in_=p_bc[:], func=AF.Exp), reads=["p_bc"], writes=["E"])
                        S.op("act", lambda e: e.activation(out=Em[:], in_=p_mid[:], func=AF.Exp), reads=["p_mid"], writes=["Em"])
                        S.op("act", lambda e: e.activation(out=Ek[:], in_=p_mid[:], func=AF.Exp, scale=-1.0),
                             reads=["p_mid"], writes=["Ek"])
                        S.op("dve", lambda e, tsl=tsl, slot=slot: e.tensor_tensor(out=Q1[:], in0=QTs[slot][:, :, tsl], in1=Em[:], op=ALU.mult),
                             reads=[qn, "Em"], writes=["Q1"])
                        S.op("pool", lambda e, tsl=tsl, slot=slot: e.tensor_tensor(out=K1[:], in0=KTs[slot][:, :, tsl], in1=Ek[:], op=ALU.mult),
                             reads=[kn, "Ek"], writes=["K1"])
                        S.op("pool", lambda e, tsl=tsl, slot=slot: e.tensor_tensor(out=Q2[:], in0=QTs[slot][:, :, tsl], in1=E[:], op=ALU.mult),
                             reads=[qn, "E"], writes=["Q2"])
                        S.op("dve", lambda e, c=c, slot=slot: e.tensor_tensor(out=K2[:], in0=Ks[slot][:, c, :, :], in1=Es[:], op=ALU.mult),
                             reads=[ktn, "Es"], writes=["K2"])
                        if getattr(self, "DEBUG2", False) and d == 0 and ib == 0 and c == 0:
                            for nm, tl, shp, dt in (("Em", Em, [128, 8, 64], F32), ("Ek", Ek, [128, 8, 64], F32), ("E", E, [128, 8, 64], F32),
                                                    ("Es", Es, [64, 8, 128], F32), ("Q1", Q1, [128, 8, 64], BF16), ("K1", K1, [128, 8, 64], BF16),
                                                    ("Q2", Q2, [128, 8, 64], BF16), ("K2", K2, [64, 8, 128], BF16),
                                                    (qn, QTs[slot], [128, 8, BT], BF16), (kn, KTs[slot], [128, 8, BT], BF16)):
                                dd = self.nc.dram_tensor("dbg_" + nm, shp, dt, kind="ExternalOutput").ap()
                                S.dma("sp", "dbg", dd, tl[:], reads=[nm], writes=[("dbg", nm)])
                        for h in range(8):
                            pst = p_st[h % 2]
                            pstn = "p_st%d" % (h % 2)
                            ptt = PT[h % 2]
                            ptn = "PT%d" % (h % 2)
                            v1 = Vs[slot][:, c, h, :]
                            S.op("pe", lambda e, h=h, pst=pst: e.matmul(pst[:], lhsT=K1[:, h, :], rhs=Q1[:, h, :], start=True, stop=True),
                                 reads=["K1", "Q1"], writes=[pstn])
                            S.op("dve", lambda e, pst=pst, ptt=ptt: e.tensor_tensor(out=ptt[:], in0=pst[:], in1=msk, op=ALU.mult),
                                 reads=[pstn, "tri"], writes=[ptn])
                            if getattr(self, "DEBUG2", False) and d == 0 and ib == 0 and c == 0 and h in (0, 4):
                                dd = self.nc.dram_tensor("dbg_PT%d" % h, [64, 64], BF16, kind="ExternalOutput").ap()
                                S.dma("sp", "dbg", dd, ptt[:], reads=[ptn], writes=[("dbg", "PT", h)])
                            S.op("pe", lambda e, h=h: e.matmul(p_o[:], lhsT=Q2[:, h, :], rhs=Sb[:, h, :], start=True, stop=False),
                                 reads=["Q2", ("Sb", h)], writes=["p_o"])
                            S.op("pe", lambda e, ptt=ptt, v1=v1: e.matmul(p_o[:], lhsT=ptt[:], rhs=v1, start=False, stop=True),
                                 reads=[ptn, vn], writes=["p_o"])
                            S.op("pe", lambda e, h=h, v1=v1: e.matmul(p_u[:], lhsT=K2[:, h, :], rhs=v1, start=True, stop=True),
                                 reads=["K2", vn], writes=["p_u"])
                            oc = Os[slot][:, c, h, :]
                            if h < nden:
                                S.op("dve", lambda e: e.tensor_copy(out=den[:, 1:2], in_=p_o[:, 128:129]), reads=["p_o"], writes=["den"])
                                S.op("dve", lambda e: e.scalar_tensor_tensor(out=den[:, 0:1], in0=den[:, 1:2], scalar=-1.0, in1=den[:, 1:2],
                                                                             op0=ALU.mult, op1=ALU.max), reads=["den"], writes=["den"])
                                S.op("dve", lambda e: e.tensor_scalar(out=den[:, 0:1], in0=den[:, 0:1], scalar1=1.0, scalar2=None,
                                                                      op0=ALU.max), reads=["den"], writes=["den"])
                                S.op("dve", lambda e: e.reciprocal(out=den[:, 1:2], in_=den[:, 0:1]), reads=["den"], writes=["den"])
                                S.op("act", lambda e, oc=oc: e.activation(out=oc, in_=p_o[:, 0:128], func=AF.Identity, scale=den[:, 1:2]),
                                     reads=["p_o", "den"], writes=[(on, c)])
                            else:
                                S.op("act", lambda e, oc=oc: e.copy(out=oc, in_=p_o[:, 0:128]), reads=["p_o"], writes=[(on, c)])
                            S.op("dve", lambda e, h=h: e.scalar_tensor_tensor(
                                out=S32[:, h, :], in0=S32[:, h, :], scalar=E[:, h, deccol:deccol + 1], in1=p_u[:],
                                op0=ALU.mult, op1=ALU.add), reads=[("S32", h), "E", "p_u"], writes=[("S32", h)])
                            S.op("pool", lambda e, h=h: e.tensor_copy(out=Sb[:, h, :], in_=S32[:, h, :]),
                                 reads=[("S32", h)], writes=[("Sb", h), ])
                    t0 = bi * BT
                    S.dma("sp", "sst%d" % slot, self.OD[d][t0:t0 + BT].rearrange("(c p) (h e) -> p c h e", p=L, h=8),
                          Os[slot][:], reads=[(on, c) for c in range(CB)], writes=[("dO", d, t0 // 128 + s_) for s_ in range(BT // 128)])
                S.flush()

    def epi_phase(self, x_in, x_out, even, w_out, g_vec, b_vec):
        S = self.S
        T = self.T
        NT = T // 128
        with ExitStack() as es:
            ident = self.make_identity(es)
            wo = self.sb(es, "ewo", [128, 8, 1024], BF16)
            xs = [self.sb(es, "ex%d" % i, [128, 1024], F32) for i in range(2)]
            of = [self.sb(es, "eof%d" % i, [128, 1024], F32) for i in range(2)]
            ob = [self.sb(es, "eob%d" % i, [128, 1024], F32) for i in range(2)]
            ga = [self.sb(es, "ega%d" % i, [128, 1024], F32) for i in range(2)]
            sq = self.sb(es, "esq", [128, 1024], F32)
            onT = self.sb(es, "eonT", [128, 8, 128], BF16)
            s1 = self.sb(es, "es1", [128, 8], F32)
            s2 = self.sb(es, "es2", [128, 8], F32)
            cfl = self.sb(es, "ecfl", [128, 8], F32)
            gb = self.sb(es, "egb", [128, 1024], F32)
            bb = self.sb(es, "ebb", [128, 1024], F32)
            stats = self.sb(es, "estats", [128, 2, 6], F32)
            mv = self.sb(es, "emv", [128, 2], F32)
            rstd = self.sb(es, "erstd", [128, 1], F32)
            nmr = self.sb(es, "enmr", [128, 1], F32)
            pt = [self.ps(es, "ept%d" % i, [128, 512]) for i in range(2)]
            py = [self.ps(es, "epy%d" % i, [128, 512]) for i in range(2)]
            S.dma("sp", "cst1", gb[:], g_vec.partition_broadcast(128), writes=["gb"])
            S.dma("sp", "cst2", bb[:], b_vec.partition_broadcast(128), writes=["bb"])
            lnh = (0, 4) if even else (4, 8)
            S.op("pool", lambda e: e.memset(cfl[:], 0.0), writes=["ecfl"])
            S.op("pool", lambda e: e.memset(cfl[:, lnh[0]:lnh[1]], 1.0 / 128.0), reads=["ecfl"], writes=["ecfl"])
            self.load_weight_bf16(w_out, wo, 1024, 1024, [_View(of[0][:]), _View(of[1][:])], "ewo")
            S.flush()
            xin_v = x_in.rearrange("(n p) d -> n p d", p=128)
            xout_v = x_out.rearrange("(n p) d -> n p d", p=128)

            def load(i):
                sl = i % 2
                S.dma("sp", "eld_x%d" % sl, xs[sl][:], xin_v[i], writes=["ex%d" % sl])
                S.dma("sp", "eld_f%d" % sl, of[sl][:], self.OD[0][i * 128:(i + 1) * 128], reads=[("dO", 0, i)], writes=["eof%d" % sl])
                S.dma("sp", "eld_b%d" % sl, ob[sl][:], self.OD[1][i * 128:(i + 1) * 128], reads=[("dO", 1, i)], writes=["eob%d" % sl])
                S.dma("sp", "eld_g%d" % sl, ga[sl][:], self.GATE[i * 128:(i + 1) * 128], reads=[("dGA", i)], writes=["ega%d" % sl])

            load(0)
            for i in range(NT):
                sl = i % 2
                if i + 1 < NT:
                    load(i + 1)
                o = of[sl]
                on_ = "eof%d" % sl
                o3 = o[:].rearrange("p (h e) -> p h e", h=8)
                S.op("dve", lambda e, o=o, sl=sl: e.tensor_tensor(out=o[:], in0=o[:], in1=ob[sl][:], op=ALU.add),
                     reads=[on_, "eob%d" % sl], writes=[on_])
                S.op("act", lambda e, o=o: e.activation(out=sq[:], in_=o[:], func=AF.Square), reads=[on_], writes=["esq"])
                S.op("dve", lambda e, o3=o3: e.tensor_reduce(out=s1[:], in_=o3, axis=AX.X, op=ALU.add), reads=[on_], writes=["es1"])
                S.op("dve", lambda e: e.tensor_reduce(out=s2[:], in_=sq[:].rearrange("p (h e) -> p h e", h=8), axis=AX.X, op=ALU.add),
                     reads=["esq"], writes=["es2"])
                S.op("dve", lambda e: e.tensor_tensor(out=s1[:], in0=s1[:], in1=cfl[:], op=ALU.mult), reads=["es1", "ecfl"], writes=["es1"])
                S.op("dve", lambda e: e.tensor_tensor(out=sq[:, 0:8], in0=s1[:], in1=s1[:], op=ALU.mult), reads=["es1"], writes=["esq"])
                S.op("dve", lambda e: e.scalar_tensor_tensor(out=s2[:], in0=s2[:], scalar=1.0 / 128.0, in1=sq[:, 0:8],
                                                             op0=ALU.mult, op1=ALU.subtract), reads=["es2", "esq"], writes=["es2"])
                S.op("act", lambda e: e.activation(out=s2[:], in_=s2[:], func=AF.Sqrt, bias=float(LN_EPS), scale=1.0),
                     reads=["es2"], writes=["es2"])
                S.op("dve", lambda e: e.reciprocal(out=s2[:], in_=s2[:]), reads=["es2"], writes=["es2"])
                S.op("dve", lambda e, o3=o3: e.tensor_tensor(out=o3, in0=o3, in1=s1[:].unsqueeze(2).to_broadcast([128, 8, 128]),
                                                            op=ALU.subtract), reads=[on_, "es1"], writes=[on_])
                S.op("pool", lambda e, o3=o3: e.tensor_tensor(out=o3, in0=o3, in1=s2[:].unsqueeze(2).to_broadcast([128, 8, 128]),
                                                             op=ALU.mult), reads=[on_, "es2"], writes=[on_])
                S.op("dve", lambda e, o=o, sl=sl: e.tensor_tensor(out=o[:], in0=o[:], in1=ga[sl][:], op=ALU.mult),
                     reads=[on_, "ega%d" % sl], writes=[on_])
                for half in range(2):
                    p = pt[half]
                    for k4 in range(4):
                        k = half * 4 + k4
                        S.op("pe", lambda e, p=p, k=k, k4=k4, o=o: e.transpose(out=p[:, k4 * 128:(k4 + 1) * 128],
                                                                             in_=o[:, k * 128:(k + 1) * 128], identity=ident[:]),
                             reads=[on_, "ident"], writes=["ept%d" % half])
                    if half == 0:
                        S.op("dve", lambda e, p=p: e.tensor_copy(out=onT[:, 0:4, :], in_=p[:].rearrange("p (k t) -> p k t", k=4)),
                             reads=["ept0"], writes=[("eonT", 0)])
                    else:
                        S.op("act", lambda e, p=p: e.copy(out=onT[:, 4:8, :], in_=p[:].rearrange("p (k t) -> p k t", k=4)),
                             reads=["ept1"], writes=[("eonT", 1)])
                xt = xs[sl]
                xtok = "ex%d" % sl
                for c in range(2):
                    for k in range(8):
                        S.op("pe", lambda e, c=c, k=k: e.matmul(py[c][:], lhsT=onT[:, k, :], rhs=wo[:, k, c * 512:(c + 1) * 512],
                                                                start=(k == 0), stop=(k == 7)),
                             reads=[("eonT", 0), ("eonT", 1), "ewo"], writes=["epy%d" % c])
                    S.op("dve", lambda e, c=c, xt=xt: e.scalar_tensor_tensor(
                        out=xt[:, c * 512:(c + 1) * 512], in0=xt[:, c * 512:(c + 1) * 512], scalar=ALPHA, in1=py[c][:],
                        op0=ALU.mult, op1=ALU.add), reads=["epy%d" % c, xtok], writes=[xtok])
                self.layer_norm_rows(xt[:], xtok, gb, bb, stats, mv, rstd, nmr, LN_EPS, pfx="e")
                S.dma("sp", "est%d" % sl, xout_v[i], xt[:], reads=[xtok], writes=[("xout", i)])
            S.flush()


class Builder(Builder_, MixerMixin):
    def setup_globals(self):
        nc = self.nc
        self.c_ident = self.dram_in("c_ident", [128, 128])
        self.c_shift = nc.dram_tensor("c_shift", [128, 8, 128], BF16, kind="ExternalInput").ap()
        self.c_tri = self.dram_in("c_tri", [64, 8, 64])
        self.c_rot = self.dram_in("c_rot", [self.T, 256])
        self.c_flags = self.dram_in("c_flags", [128, 4])
        self.kf = self.sb(self.es, "kf", [128, 4], F32)
        self.S.dma("sp", "kf", self.kf[:], self.c_flags, writes=["kf"])
        self.S.flush()
        self.alloc_scratch()


def host_consts():
    import ml_dtypes
    c = {}
    c["c_ident"] = np.eye(128, dtype=np.float32)
    sh = np.zeros((128, 8, 128), np.float32)
    u = np.arange(128)[:, None]
    t = np.arange(128)[None, :]
    for oi, o in enumerate((-2, -1, 1, 2)):
        sh[:, oi, :] = (u == t + o)
        if o < 0:
            sh[:, 4 + oi, :] = (u == 128 + t + o)
        else:
            sh[:, 4 + oi, :] = (u == t + o - 128)
    c["c_shift"] = sh.astype(ml_dtypes.bfloat16)
    tri = np.zeros((64, 8, 64), np.float32)
    u = np.arange(64)[:, None]
    t = np.arange(64)[None, :]
    tri[:, 0, :] = (u <= t)
    tri[:, 1, :] = ((u >= 32) & (u <= t)) * 1.0 - ((u > t) & (u <= 31)) * 1.0
    tri[:, 2, :] = (u > t)
    tri[:, 3, :] = (u <= t)
    tri[:, 4, :] = (u >= t)
    tri[:, 5, :] = ((u >= t) & (u <= 31)) * 1.0 - ((u >= 32) & (u < t)) * 1.0
    tri[:, 6, :] = (u < t)
    tri[:, 7, :] = (u >= t)
    c["c_tri"] = tri
    return c


def rot_table(T, seq_len):
    pos = (np.arange(T) % seq_len).astype(np.float32)
    inv = (1.0 / (10000.0 ** (np.arange(0, 128, 2, dtype=np.float32) / np.float32(128)))).astype(np.float32)
    ang = (pos[:, None] * inv[None, :]).astype(np.float32)
    cos, sin = np.cos(ang).astype(np.float32), np.sin(ang).astype(np.float32)
    sc = np.float32(128.0 ** -0.5)
    return np.concatenate([cos, sin, cos * sc, sin * sc], axis=1).astype(np.float32)


WNAMES = ["ffn1_w_in", "ffn1_w_out", "ffn2_w_in", "ffn2_w_out", "ln_g", "ln_b", "ev_w_in", "ev_gate_b", "ev_conv_w",
          "ev_gla_a2_w", "ev_gla_a2_b", "ev_norm_w", "ev_w_out", "od_w_in", "od_lb_logits", "od_norm_w", "od_w_out"]


def build_program(T, bnd, shapes, depth=DEPTH):
    B = Builder(T, bnd, depth)
    B.BND = bnd
    B.setup_globals()
    W = {n: B.dram_in(n, shapes[n]) for n in WNAMES}
    x = B.dram_in("x", [T, 1024])
    y = B.dram_out("y", [T, 1024])
    xa = B.dram_tmp("s_xa", [T, 1024])
    xb = B.dram_tmp("s_xb", [T, 1024])
    cur = x
    for l in range(depth):
        j = l // 2
        B.ffn_phase(cur, xa, W["ffn1_w_in"][l], W["ffn1_w_out"][l], W["ln_g"][l, 0], W["ln_b"][l, 0])
        if l % 2 == 0:
            P = dict(w_in=W["ev_w_in"][j], gate_b=W["ev_gate_b"][j], conv_w=W["ev_conv_w"][j], a2_w=W["ev_gla_a2_w"][j],
                     a2_b=W["ev_gla_a2_b"][j], norm_w=W["ev_norm_w"][j])
            wo = W["ev_w_out"][j]
        else:
            P = dict(w_in=W["od_w_in"][j], lb_logits=W["od_lb_logits"], norm_w=W["od_norm_w"][j], layer_idx=j)
            wo = W["od_w_out"][j]
        B.proj_phase(xa, l % 2 == 0, P)
        B.scan_phase(l % 2 == 0)
        B.epi_phase(xa, xb, l % 2 == 0, wo, W["ln_g"][l, 1], W["ln_b"][l, 1])
        dst = y if l == depth - 1 else xa
        B.ffn_phase(xb, dst, W["ffn2_w_in"][l], W["ffn2_w_out"][l], W["ln_g"][l, 2], W["ln_b"][l, 2])
        cur = xa
    B.es.close()
    return B


def kernel(**inputs):
    xp = np.asarray(inputs["x_prompt"], np.float32)
    xs = np.asarray(inputs["x_sample"], np.float32)
    T = 16384
    BND = 4096
    weights = {n: np.ascontiguousarray(np.asarray(inputs[n], np.float32)) for n in WNAMES}
    shapes = {n: weights[n].shape for n in WNAMES}
    B = build_program(T, BND, shapes)
    consts = host_consts()
    rot_p = rot_table(T, 16384)
    rot_s = rot_table(T, 4096)
    fl_p = np.ones((128, 4), np.float32)
    fl_s = np.zeros((128, 4), np.float32)
    xcore = [xp[0], xp[1], xs[0:4].reshape(T, 1024), xs[4:8].reshape(T, 1024)]
    maps = []
    for c in range(N_CORES):
        src = c if c < 4 else 2 + (c % 2)
        m = dict(weights)
        m.update(consts)
        m["x"] = np.ascontiguousarray(xcore[src])
        m["c_rot"] = rot_p if src < 2 else rot_s
        m["c_flags"] = fl_p if src < 2 else fl_s
        maps.append(m)
    res = run_bass_kernel_spmd(B.nc, maps, core_ids=list(range(N_CORES)))
    r = res.results
    y_prompt = np.stack([np.asarray(r[0]["y"]), np.asarray(r[1]["y"])], 0).astype(np.float32)
    y_sample = np.concatenate([np.asarray(r[2]["y"]).reshape(4, 4096, 1024),
                               np.asarray(r[3]["y"]).reshape(4, 4096, 1024)], 0).astype(np.float32)
    return (y_prompt, y_sample)
```

```python
import numpy as np
from contextlib import ExitStack
import concourse.bass as bass
import concourse.mybir as mybir
from concourse.bass_utils import run_bass_kernel_spmd

F32 = mybir.dt.float32
BF16 = mybir.dt.bfloat16
AF = mybir.ActivationFunctionType
ALU = mybir.AluOpType
AX = mybir.AxisListType

D_MODEL = 1024
DEPTH = 4
D_FF = 2816
LN_EPS = 1e-5
ALPHA = (2 * DEPTH) ** 0.25
N_CORES = 8


class _Op:
    __slots__ = ("eng", "fn", "deps", "is_dma", "key", "cnt", "signal", "sigval", "dma_all")

    def __init__(self, eng, fn, is_dma=False, key=None):
        self.eng = eng
        self.fn = fn
        self.deps = []
        self.is_dma = is_dma
        self.key = key
        self.cnt = 0
        self.signal = False
        self.sigval = 0
        self.dma_all = ()


class Sched:
    CENG = ("pe", "act", "dve", "pool")
    ALLENG = ("pe", "act", "dve", "pool", "sp")

    def __init__(self, nc, es):
        self.nc = nc
        self.es = es
        self.sem = {e: es.enter_context(nc.semaphore("sem_" + e)) for e in self.CENG}
        self.sig_cnt = {e: 0 for e in self.CENG}
        self.dma_sem = {}
        self.dma_cnt = {}
        self.waited = {e: {} for e in self.ALLENG}
        self.tok = {}
        self.ops = []
        self.last = {e: None for e in self.ALLENG}
        self.n_inst = 0

    STRICT = True

    def _need(self, x, w):
        return not (w.eng == "pe" and x.eng == "pe")

    def _add(self, op, reads, writes):
        deps = op.deps
        for t in reads:
            st = self.tok.get(t)
            if st is None:
                st = self.tok[t] = [None, []]
            w = st[0]
            if w is not None and self._need(op, w):
                deps.append(w)
            rl = st[1]
            if not op.is_dma:
                for i_, r_ in enumerate(rl):
                    if r_.eng == op.eng and not r_.is_dma:
                        rl[i_] = op
                        break
                else:
                    rl.append(op)
            else:
                rl.append(op)
        for t in writes:
            st = self.tok.get(t)
            if st is None:
                st = self.tok[t] = [None, []]
            w = st[0]
            if w is not None and self._need(op, w):
                deps.append(w)
            for r in st[1]:
                if r is not op and self._need(op, r):
                    deps.append(r)
            st[0] = op
            st[1] = []
        for d in deps:
            d.signal = True
        self.ops.append(op)
        self.last[op.eng] = op

    def op(self, eng, fn, reads=(), writes=()):
        o = _Op(eng, fn)
        self._add(o, reads, writes)
        return o

    def dma(self, eng, key, out, in_, reads=(), writes=()):
        if key not in self.dma_sem:
            self.dma_sem[key] = self.es.enter_context(
                self.nc.semaphore("dsem%d" % len(self.dma_sem)))
            self.dma_cnt[key] = 0
        o = _Op(eng, lambda e: e.dma_start(out=out, in_=in_), is_dma=True, key=key)
        self._add(o, reads, writes)
        self.dma_cnt[key] += 16
        o.cnt = self.dma_cnt[key]
        o.signal = True
        return o

    def barrier(self):
        lasts = [self.last[e] for e in self.ALLENG if self.last[e] is not None]
        dma_all = [(k, c) for k, c in self.dma_cnt.items()]
        for e in self.ALLENG:
            o = _Op(e, None)
            for l in lasts:
                if l.eng != e or l.is_dma:
                    if not l.is_dma:
                        o.deps.append(l)
                        l.signal = True
            o.dma_all = dma_all
            self.ops.append(o)
        self.tok = {}

    def flush(self):
        self.barrier()
        ops = self.ops
        self.ops = []
        for o in ops:
            if o.signal and not o.is_dma:
                self.sig_cnt[o.eng] += 1
                o.sigval = self.sig_cnt[o.eng]
        per = {e: [o for o in ops if o.eng == e] for e in self.ALLENG}
        nc = self.nc

        def emit(eng_name, engine):
            waited = self.waited[eng_name]
            for o in per[eng_name]:
                need = {}
                for d in o.deps:
                    if d.is_dma:
                        k = ("d", d.key)
                        v = d.cnt
                    else:
                        k = ("c", d.eng)
                        v = d.sigval
                    if v > need.get(k, 0):
                        need[k] = v
                for (k, c) in o.dma_all:
                    kk = ("d", k)
                    if c > need.get(kk, 0):
                        need[kk] = c
                for k, v in need.items():
                    if v > waited.get(k, 0):
                        waited[k] = v
                        s = self.dma_sem[k[1]] if k[0] == "d" else self.sem[k[1]]
                        engine.wait_ge(s, v)
                        self.n_inst += 1
                if o.fn is None:
                    continue
                ins = o.fn(engine)
                self.n_inst += 1
                if o.is_dma:
                    ins.then_inc(self.dma_sem[o.key], 16)
                elif o.signal:
                    ins.then_inc(self.sem[o.eng], 1)

        with nc.Block() as blk:
            @blk.tensor
            def _(e):
                emit("pe", e)

            @blk.scalar
            def _(e):
                emit("act", e)

            @blk.vector
            def _(e):
                emit("dve", e)

            @blk.gpsimd
            def _(e):
                emit("pool", e)

            @blk.sync
            def _(e):
                emit("sp", e)
        self.last = {e: None for e in self.ALLENG}


class Builder_:
    def __init__(self, T, seq_len, depth=DEPTH):
        self.T = T
        self.seq_len = seq_len
        self.depth = depth
        self.nc = bass.Bass("TRN2", target_bir_lowering=False)
        self.es = ExitStack()
        self.S = Sched(self.nc, self.es)
        self.uid = 0

    def dram_in(self, name, shape):
        return self.nc.dram_tensor(name, list(shape), F32, kind="ExternalInput").ap()

    def dram_out(self, name, shape):
        return self.nc.dram_tensor(name, list(shape), F32, kind="ExternalOutput").ap()

    DEBUG = False

    def dram_tmp(self, name, shape, dtype=F32):
        if self.DEBUG:
            return self.nc.dram_tensor(name, list(shape), dtype, kind="ExternalOutput").ap()
        return self.nc.dram_tensor(name, list(shape), dtype).ap()

    def sb(self, es, name, shape, dtype):
        self.uid += 1
        return es.enter_context(self.nc.sbuf_tensor("%s_%d" % (name, self.uid), list(shape), dtype))

    def ps(self, es, name, shape, dtype=F32):
        self.uid += 1
        return es.enter_context(self.nc.psum_tensor("%s_%d" % (name, self.uid), list(shape), dtype))

    def make_identity(self, es):
        S = self.S
        ident = self.sb(es, "ident", [128, 128], F32)
        S.dma("sp", "ident", ident[:], self.c_ident, writes=["ident"])
        return ident

    def load_weight_bf16(self, w_dram, wt, K, N, stage_tiles, tokname, scale=None):
        S = self.S
        SW = stage_tiles[0].shape[-1]
        kc = K // 128
        wv = w_dram.rearrange("(k p) n -> p k n", p=128)
        i = 0
        for k in range(kc):
            for n0 in range(0, N, SW):
                n1 = min(N, n0 + SW)
                si = i % len(stage_tiles)
                st = stage_tiles[si]
                stok = "wstage%d" % si
                S.dma("sp", stok, st[:, 0:n1 - n0], wv[:, k, n0:n1], writes=[stok])
                eng = ("dve", "pool", "act")[i % 3]
                dst = wt[:, k, n0:n1]
                src = st[:, 0:n1 - n0]
                if eng == "act":
                    S.op("act", lambda e, dst=dst, src=src: e.copy(out=dst, in_=src),
                         reads=[stok], writes=[(tokname, i)])
                else:
                    S.op(eng, lambda e, dst=dst, src=src: e.tensor_copy(out=dst, in_=src),
                         reads=[stok], writes=[(tokname, i)])
                i += 1

    def ffn_phase(self, x_in, x_out, w_in, w_out, g_vec, b_vec):
        S = self.S
        nc = self.nc
        T = self.T
        TT = 256
        NS = TT // 128
        ntiles = T // TT
        FC = D_FF // 128
        with ExitStack() as es:
            ident = self.make_identity(es)
            win = self.sb(es, "win", [128, 8, 2 * D_FF], BF16)
            wout = self.sb(es, "wout", [128, FC, D_MODEL], BF16)
            xs = [self.sb(es, "x%d" % i, [128, NS, D_MODEL], F32) for i in range(2)]
            xT = self.sb(es, "xT", [128, 8, TT], BF16)
            aT = self.sb(es, "aT", [128, FC, TT], BF16)
            sg = [self.sb(es, "sg%d" % i, [128, TT], F32) for i in range(2)]
            gb = self.sb(es, "gb", [128, D_MODEL], F32)
            bb = self.sb(es, "bb", [128, D_MODEL], F32)
            stats = self.sb(es, "stats", [128, 2, 6], F32)
            mv = self.sb(es, "mv", [128, 2], F32)
            rstd = self.sb(es, "rstd", [128, 1], F32)
            nmr = self.sb(es, "nmr", [128, 1], F32)
            pt = [self.ps(es, "pt%d" % i, [128, 512]) for i in range(2)]
            pg = [self.ps(es, "pg%d" % i, [128, 512]) for i in range(2)]
            pu = [self.ps(es, "pu%d" % i, [128, 512]) for i in range(2)]
            py = [self.ps(es, "py%d" % i, [128, 512]) for i in range(2)]

            S.dma("sp", "gb", gb[:], g_vec.partition_broadcast(128), writes=["gb"])
            S.dma("sp", "bb", bb[:], b_vec.partition_broadcast(128), writes=["bb"])
            stage = [xs[0][:, i, :] for i in range(NS)] + [xs[1][:, i, :] for i in range(NS)]
            stage_w = [_View(v) for v in stage]
            self.load_weight_bf16(w_in, win, D_MODEL, 2 * D_FF, stage_w, "win")
            self.load_weight_bf16(w_out, wout, D_FF, D_MODEL, stage_w, "wout")
            S.flush()
            xin_v = x_in.rearrange("(n s p) d -> n p s d", p=128, s=NS)
            xout_v = x_out.rearrange("(n s p) d -> n p s d", p=128, s=NS)

            def load_x(i):
                sl = i % 2
                S.dma("sp", "xld%d" % sl, xs[sl][:], xin_v[i], writes=["x%d" % sl])

            load_x(0)
            tcount = 0
            for i in range(ntiles):
                sl = i % 2
                xt = xs[sl]
                xtok = "x%d" % sl
                if i + 1 < ntiles:
                    load_x(i + 1)
                for k in range(8):
                    p = pt[tcount % 2]
                    ptok = "pt%d" % (tcount % 2)
                    tcount += 1
                    for s in range(NS):
                        S.op("pe", lambda e, p=p, s=s, k=k, xt=xt: e.transpose(
                            out=p[:, s * 128:(s + 1) * 128], in_=xt[:, s, k * 128:(k + 1) * 128],
                            identity=ident[:]), reads=[xtok, "ident"], writes=[ptok])
                    eng = "dve" if k % 2 == 0 else "act"
                    if eng == "dve":
                        S.op("dve", lambda e, p=p, k=k: e.tensor_copy(out=xT[:, k, :], in_=p[:, 0:TT]),
                             reads=[ptok], writes=["xT"])
                    else:
                        S.op("act", lambda e, p=p, k=k: e.copy(out=xT[:, k, :], in_=p[:, 0:TT]),
                             reads=[ptok], writes=["xT"])
                for j in range(FC):
                    b = j % 2
                    g_ps, u_ps = pg[b], pu[b]
                    for k in range(8):
                        S.op("pe", lambda e, g_ps=g_ps, k=k, j=j: e.matmul(
                            g_ps[:, 0:TT], lhsT=win[:, k, j * 128:(j + 1) * 128], rhs=xT[:, k, :],
                            start=(k == 0), stop=(k == 7)), reads=["win", "xT"], writes=["pg%d" % b])
                    for k in range(8):
                        S.op("pe", lambda e, u_ps=u_ps, k=k, j=j: e.matmul(
                            u_ps[:, 0:TT], lhsT=win[:, k, D_FF + j * 128:D_FF + (j + 1) * 128],
                            rhs=xT[:, k, :], start=(k == 0), stop=(k == 7)),
                            reads=["win", "xT"], writes=["pu%d" % b])
                    S.op("act", lambda e, g_ps=g_ps, b=b: e.activation(
                        out=sg[b][:], in_=g_ps[:, 0:TT], func=AF.Silu),
                        reads=["pg%d" % b], writes=["sg%d" % b])
                    S.op("dve", lambda e, u_ps=u_ps, b=b, j=j: e.tensor_tensor(
                        out=aT[:, j, :], in0=sg[b][:], in1=u_ps[:, 0:TT], op=ALU.mult),
                        reads=["sg%d" % b, "pu%d" % b], writes=["aT"])
                for s in range(NS):
                    for c in range(2):
                        y_ps = py[c]
                        for j in range(FC):
                            S.op("pe", lambda e, y_ps=y_ps, j=j, s=s, c=c: e.matmul(
                                y_ps[:], lhsT=aT[:, j, s * 128:(s + 1) * 128],
                                rhs=wout[:, j, c * 512:(c + 1) * 512],
                                start=(j == 0), stop=(j == FC - 1)),
                                reads=["aT", "wout"], writes=["py%d" % c])
                        S.op("dve", lambda e, y_ps=y_ps, s=s, c=c, xt=xt: e.scalar_tensor_tensor(
                            out=xt[:, s, c * 512:(c + 1) * 512], in0=xt[:, s, c * 512:(c + 1) * 512],
                            scalar=2.0 * ALPHA, in1=y_ps[:], op0=ALU.mult, op1=ALU.add),
                            reads=["py%d" % c, xtok], writes=[xtok])
                    if getattr(self, "dbg", None) and i == 0 and s == 0:
                        S.dma("sp", "dbg", self.dbg["z"], xt[:, 0, :], reads=[xtok], writes=["dbgz"])
                        S.dma("sp", "dbg", self.dbg["xT"], xT[:], reads=["xT"], writes=["dbgxT"])
                        S.dma("sp", "dbg", self.dbg["aT"], aT[:], reads=["aT"], writes=["dbgaT"])
                    self.layer_norm_rows(xt[:, s, :], xtok, gb, bb, stats, mv, rstd, nmr, 4.0 * LN_EPS)
                S.dma("sp", "xst%d" % sl, xout_v[i], xs[sl][:], reads=[xtok],
                      writes=[("xout", i)])
            S.flush()

    def layer_norm_rows(self, z, ztok, gb, bb, stats, mv, rstd, nmr, eps, pfx=""):
        S = self.S
        for c in range(2):
            S.op("dve", lambda e, c=c: e.bn_stats(out=stats[:, c, :], in_=z[:, c * 512:(c + 1) * 512]),
                 reads=[ztok], writes=["stats"])
        S.op("dve", lambda e: e.bn_aggr(out=mv[:], in_=stats[:]), reads=["stats"], writes=["mv"])
        S.op("act", lambda e: e.activation(out=rstd[:], in_=mv[:, 1:2], func=AF.Sqrt, bias=float(eps), scale=1.0),
             reads=["mv"], writes=["rstd"])
        S.op("dve", lambda e: e.reciprocal(out=rstd[:], in_=rstd[:]), reads=["rstd"], writes=["rstd"])
        S.op("dve", lambda e: e.scalar_tensor_tensor(out=nmr[:], in0=mv[:, 0:1], scalar=-1.0, in1=rstd[:],
                                                     op0=ALU.mult, op1=ALU.mult),
             reads=["mv", "rstd"], writes=["nmr"])
        S.op("act", lambda e: e.activation(out=z, in_=z, func=AF.Identity, bias=nmr[:], scale=rstd[:]),
             reads=[ztok, "rstd", "nmr"], writes=[ztok])
        S.op("pool", lambda e: e.tensor_tensor(out=z, in0=z, in1=gb[:], op=ALU.mult),
             reads=[ztok, "gb"], writes=[ztok])
        S.op("pool", lambda e: e.tensor_tensor(out=z, in0=z, in1=bb[:], op=ALU.add),
             reads=[ztok, "bb"], writes=[ztok])


class _View:
    def __init__(self, ap):
        self.ap = ap
        self.shape = ap.shape

    def __getitem__(self, k):
        return self.ap[k]


EV_COLS = 3632
OD_COLS = 4608
LOG_GAMMA = [float(np.log1p(-(2.0 ** (-5.0 - h)))) for h in range(4)]
TINY = 1e-30


def _op(S, eng, fn, r, w):
    S.op(eng, fn, reads=r, writes=w)


class MixerMixin:
    def alloc_scratch(self):
        T = self.T
        t = self.dram_tmp
        self.QT = t("s_QT", [8, 128, T], BF16)
        self.KT = [t("s_KTf", [8, 128, T], BF16), t("s_KTb", [8, 128, T], BF16)]
        self.Kt = [t("s_Kf", [T, 8, 128], BF16), t("s_Kb", [T, 8, 128], BF16)]
        self.Vt = t("s_V", [T, 8, 128], BF16)
        self.Gt = [t("s_Gf", [T, 8, 128], F32), t("s_Gb", [T, 8, 128], F32)]
        self.GATE = t("s_GATE", [T, 1024], F32)
        self.OD = [t("s_OF", [T, 1024], F32), t("s_OB", [T, 1024], F32)]

    def proj_phase(self, x_in, even, P):
        S = self.S
        T = self.T
        NT = T // 128
        ncol = EV_COLS if even else OD_COLS
        BND = self.BND // 128
        with ExitStack() as es:
            ident = self.make_identity(es)
            identb = self.sb(es, "identb", [128, 128], BF16)
            win = self.sb(es, "pwin", [128, 8, ncol], BF16)
            xs = [self.sb(es, "px%d" % i, [128, 1024], F32) for i in range(2)]
            xT = self.sb(es, "pxT", [128, 8, 128], BF16)
            Y = [self.sb(es, "Y%d" % i, [128, ncol], F32) for i in range(2)]
            Qk = self.sb(es, "Qk", [128, 8, 128], BF16)
            Kf = self.sb(es, "Kf", [128, 8, 128], BF16)
            Kb = self.sb(es, "Kb", [128, 8, 128], BF16)
            Vk = self.sb(es, "Vk", [128, 8, 128], BF16)
            Gf = self.sb(es, "Gf", [128, 8, 128], F32)
            Gb = self.sb(es, "Gb", [128, 8, 128], F32)
            GA = self.sb(es, "GA", [128, 1024], F32)
            TT = self.sb(es, "TTs", [128, 3, 4, 128], BF16)
            nw = self.sb(es, "nw", [128, 1024], F32)
            tmp1 = self.sb(es, "tmp1", [128, 1024], F32)
            tmp2 = self.sb(es, "tmp2", [128, 1024], F32)
            pp = [self.ps(es, "pp%d" % i, [128, 512]) for i in range(3)]
            ptb = [self.ps(es, "ptb%d" % i, [128, 4, 128], BF16) for i in range(2)]
            psm = self.ps(es, "psm", [128, 512])

            S.op("dve", lambda e: e.tensor_copy(out=identb[:], in_=ident[:]), reads=["ident"], writes=["identb"])
            for tl in (Qk, Kf, Kb, Gf, Gb):
                S.op("pool", lambda e, tl=tl: e.memset(tl[:], 0.0), writes=[tl.name])
            if even:
                A3 = [self.sb(es, "A3_%d" % i, [128, 1024], BF16) for i in range(3)]
                acc = self.sb(es, "cacc", [128, 1024], F32)
                cw = self.sb(es, "cw", [128, 5, 1024], F32)
                sh = self.sb(es, "sh", [128, 8, 128], BF16)
                shk = self.sb(es, "shk", [128, 4, 128], BF16)
                gbb = self.sb(es, "gbb", [128, 16], F32)
                gt = self.sb(es, "gt", [128, 16], F32)
                E1 = self.sb(es, "E1", [128, 16], F32)
                E2 = self.sb(es, "E2", [128, 16], F32)
                a2w = self.sb(es, "a2w", [16, 2, 256], F32)
                a2b = self.sb(es, "a2b", [1, 2, 256], F32)
                ones1 = self.sb(es, "ones1", [1, 128], F32)
                baT = [self.sb(es, "baT%d" % j, [16, 128], F32) for j in range(2)]
                S.dma("sp", "cst1", cw[:], P["conv_w"].partition_broadcast(128), writes=["cw"])
                S.dma("sp", "cst2", sh[:], self.c_shift, writes=["sh"])
                S.dma("sp", "cst3", gbb[:], P["gate_b"].partition_broadcast(128), writes=["gbb"])
                S.dma("sp", "cst4", a2w[:], P["a2_w"].rearrange("j r c -> r j c"), writes=["a2w"])
                S.dma("sp", "cst5", a2b[:], P["a2_b"].rearrange("(o j) c -> o j c", o=1), writes=["a2b"])
                S.dma("sp", "cst6", nw[:], P["norm_w"].partition_broadcast(128), writes=["nw"])
                S.op("pool", lambda e: e.memset(ones1[:], 1.0), writes=["ones1"])
            else:
                lbl = self.sb(es, "lbl", [128, 2, 1024], F32)
                lb = self.sb(es, "lb", [128, 1024], F32)
                oml = self.sb(es, "oml", [128, 1024], F32)
                rot = self.sb(es, "rot", [128, 2, 128], F32)
                S.dma("sp", "cst1", lbl[:], P["lb_logits"].rearrange("l j c -> l (j c)").partition_broadcast(128),
                      writes=["lbl"])
                S.dma("sp", "cst6", nw[:, 0:512], P["norm_w"].partition_broadcast(128), writes=["nw"])
                if P["layer_idx"] == 0:
                    S.op("pool", lambda e: e.memset(lb[:], 0.0), writes=["lb"])
                else:
                    S.op("act", lambda e: e.activation(out=lbl[:], in_=lbl[:], func=AF.Exp), reads=["lbl"], writes=["lbl"])
                    S.op("dve", lambda e: e.tensor_tensor(out=lb[:], in0=lbl[:, 0, :], in1=lbl[:, 1, :], op=ALU.add),
                         reads=["lbl"], writes=["lb"])
                    S.op("dve", lambda e: e.reciprocal(out=lb[:], in_=lb[:]), reads=["lb"], writes=["lb"])
                    S.op("dve", lambda e: e.tensor_tensor(out=lb[:], in0=lb[:], in1=lbl[:, 1, :], op=ALU.mult),
                         reads=["lb", "lbl"], writes=["lb"])
                S.op("dve", lambda e: e.tensor_scalar(out=oml[:], in0=lb[:], scalar1=-1.0, scalar2=1.0,
                                                      op0=ALU.mult, op1=ALU.add), reads=["lb"], writes=["oml"])
                for h in range(4):
                    S.op("pool", lambda e, h=h: e.memset(Gf[:, 4 + h, :], LOG_GAMMA[h]), reads=[Gf.name], writes=[Gf.name])
                    S.op("pool", lambda e, h=h: e.memset(Gb[:, 4 + h, :], LOG_GAMMA[h]), reads=[Gb.name], writes=[Gb.name])
            stage = [(_View(Y[0][:, 0:2048]), ), (_View(Y[1][:, 0:2048]),)]
            self.load_weight_bf16(P["w_in"], win, D_MODEL, ncol, [s[0] for s in stage], "pwin")
            S.flush()
            if even:
                nb = (NT - 1) // BND
                shkb = []
                for b in range(1, nb + 1):
                    tl = self.sb(es, "shkb%d" % b, [128, 4, 128], BF16)
                    S.op("dve", lambda e, tl=tl, b=b: e.tensor_scalar(
                        out=tl[:], in0=sh[:, 4:8, :], scalar1=self.kf[:, b:b + 1], scalar2=None, op0=ALU.mult),
                        reads=["sh", "kf"], writes=[tl.name])
                    shkb.append(tl)

            xin_v = x_in.rearrange("(n p) d -> n p d", p=128)

            def load_x(i):
                S.dma("sp", "pxld%d" % (i % 2), xs[i % 2][:], xin_v[i], writes=["px%d" % (i % 2)])

            def stage1(n):
                xt = xs[n % 2]
                xtok = "px%d" % (n % 2)
                y = Y[n % 2]
                ytok = "Y%d" % (n % 2)
                if n + 1 < NT:
                    load_x(n + 1)
                for k in range(8):
                    p = pp[k % 3]
                    ptok = "pp%d" % (k % 3)
                    S.op("pe", lambda e, p=p, k=k: e.transpose(out=p[:, 0:128], in_=xt[:, k * 128:(k + 1) * 128],
                                                              identity=ident[:]), reads=[xtok, "ident"], writes=[ptok])
                    eng = "dve" if k % 2 == 0 else "act"
                    if eng == "dve":
                        S.op("dve", lambda e, p=p, k=k: e.tensor_copy(out=xT[:, k, :], in_=p[:, 0:128]),
                             reads=[ptok], writes=["pxT"])
                    else:
                        S.op("act", lambda e, p=p, k=k: e.copy(out=xT[:, k, :], in_=p[:, 0:128]),
                             reads=[ptok], writes=["pxT"])
                ci = 0
                for c0 in range(0, ncol, 512):
                    c1 = min(ncol, c0 + 512)
                    p = pp[ci % 3]
                    ptok = "pp%d" % (ci % 3)
                    for k in range(8):
                        S.op("pe", lambda e, p=p, k=k, c0=c0, c1=c1: e.matmul(
                            p[:, 0:c1 - c0], lhsT=xT[:, k, :], rhs=win[:, k, c0:c1], start=(k == 0), stop=(k == 7)),
                            reads=["pxT", "pwin"], writes=[ptok])
                    if ci % 2 == 0:
                        S.op("dve", lambda e, p=p, c0=c0, c1=c1: e.tensor_copy(out=y[:, c0:c1], in_=p[:, 0:c1 - c0]),
                             reads=[ptok], writes=[(ytok, ci)])
                    else:
                        S.op("act", lambda e, p=p, c0=c0, c1=c1: e.copy(out=y[:, c0:c1], in_=p[:, 0:c1 - c0]),
                             reads=[ptok], writes=[(ytok, ci)])
                    ci += 1
                if even:
                    S.op("pool", lambda e: e.tensor_copy(out=A3[n % 3][:], in_=y[:, 0:1024]),
                         reads=[(ytok, 0), (ytok, 1)], writes=["A3_%d" % (n % 3)])
                return ci

            def ytoks(m, c0, c1):
                return [("Y%d" % (m % 2), ci) for ci in range(c0 // 512, (c1 - 1) // 512 + 1)]

            def write_out(m):
                t0 = m * 128
                for ai, (src, dst) in enumerate(((Qk, self.QT), (Kf, self.KT[0]), (Kb, self.KT[1]))):
                    for hg in range(2):
                        pb = ptb[(ai * 2 + hg) % 2]
                        pbt = "ptb%d" % ((ai * 2 + hg) % 2)
                        for h in range(4):
                            S.op("pe", lambda e, pb=pb, h=h, hg=hg, src=src: e.transpose(
                                out=pb[:, h, :], in_=src[:, hg * 4 + h, :], identity=identb[:]),
                                reads=[src.name, "identb"], writes=[pbt])
                        ttok = ("TT", ai, hg)
                        eng = "act" if hg == 0 else "dve"
                        if eng == "act":
                            S.op("act", lambda e, pb=pb, ai=ai: e.copy(out=TT[:, ai, :, :], in_=pb[:]),
                                 reads=[pbt], writes=[("TT", ai)])
                        else:
                            S.op("dve", lambda e, pb=pb, ai=ai: e.tensor_copy(out=TT[:, ai, :, :], in_=pb[:]),
                                 reads=[pbt], writes=[("TT", ai)])
                        S.dma("sp", "wo_tt%d" % ai, dst[hg * 4:hg * 4 + 4, :, t0:t0 + 128].rearrange("h d t -> d h t"),
                              TT[:, ai, :, :], reads=[("TT", ai)], writes=[("dQT", ai, hg, m)])
                S.dma("sp", "wo_kf", self.Kt[0][t0:t0 + 128], Kf[:], reads=[Kf.name], writes=[("dKf", m)])
                S.dma("sp", "wo_kb", self.Kt[1][t0:t0 + 128], Kb[:], reads=[Kb.name], writes=[("dKb", m)])
                S.dma("sp", "wo_v", self.Vt[t0:t0 + 128], Vk[:], reads=[Vk.name], writes=[("dV", m)])
                S.dma("sp", "wo_gf", self.Gt[0][t0:t0 + 128], Gf[:], reads=[Gf.name], writes=[("dGf", m)])
                S.dma("sp", "wo_gb", self.Gt[1][t0:t0 + 128], Gb[:], reads=[Gb.name], writes=[("dGb", m)])
                S.dma("sp", "wo_ga", self.GATE[t0:t0 + 128], GA[:], reads=["GA"], writes=[("dGA", m)])

            def stage2_even(m):
                y = Y[m % 2]
                yn = "Y%d" % (m % 2)
                A = A3[m % 3]
                S.op("dve", lambda e: e.tensor_tensor(out=acc[:], in0=y[:, 0:1024], in1=cw[:, 2, :], op=ALU.mult),
                     reads=ytoks(m, 0, 1024) + ["cw"], writes=["cacc"])
                for oi, o in enumerate((-2, -1, 1, 2)):
                    j = o + 2
                    for half in range(2):
                        p = pp[(oi * 2 + half) % 3]
                        ptok = "pp%d" % ((oi * 2 + half) % 3)
                        mms = [(sh[:, oi, :], A, "A3_%d" % (m % 3), "sh")]
                        if o < 0 and m > 0:
                            if m % BND == 0:
                                mms.append((shkb[m // BND - 1][:, oi, :], A3[(m - 1) % 3], "A3_%d" % ((m - 1) % 3),
                                            shkb[m // BND - 1].name))
                            else:
                                mms.append((sh[:, 4 + oi, :], A3[(m - 1) % 3], "A3_%d" % ((m - 1) % 3), "sh"))
                        if o > 0 and m < NT - 1:
                            if (m + 1) % BND == 0:
                                mms.append((shkb[(m + 1) // BND - 1][:, oi, :], A3[(m + 1) % 3],
                                            "A3_%d" % ((m + 1) % 3), shkb[(m + 1) // BND - 1].name))
                            else:
                                mms.append((sh[:, 4 + oi, :], A3[(m + 1) % 3], "A3_%d" % ((m + 1) % 3), "sh"))
                        for qi, (l, r, rt, lt) in enumerate(mms):
                            S.op("pe", lambda e, p=p, l=l, r=r, half=half, qi=qi, nq=len(mms): e.matmul(
                                p[:], lhsT=l, rhs=r[:, half * 512:(half + 1) * 512], start=(qi == 0), stop=(qi == nq - 1)),
                                reads=[rt, lt], writes=[ptok])
                        S.op("dve", lambda e, p=p, half=half, j=j: e.tensor_tensor(
                            out=tmp1[:, half * 512:(half + 1) * 512], in0=p[:], in1=cw[:, j, half * 512:(half + 1) * 512],
                            op=ALU.mult), reads=[ptok, "cw"], writes=["tmp1"])
                        S.op("pool", lambda e, half=half: e.tensor_tensor(
                            out=acc[:, half * 512:(half + 1) * 512], in0=acc[:, half * 512:(half + 1) * 512],
                            in1=tmp1[:, half * 512:(half + 1) * 512], op=ALU.add), reads=["tmp1", "cacc"], writes=["cacc"])
                S.op("act", lambda e: e.activation(out=tmp2[:], in_=acc[:], func=AF.Silu), reads=["cacc"], writes=["tmp2"])
                S.op("dve", lambda e: e.tensor_copy(out=Qk[:, 0:4, :], in_=tmp2[:, 0:512].rearrange("p (h d) -> p h d", h=4)),
                     reads=["tmp2"], writes=[Qk.name])
                S.op("dve", lambda e: e.tensor_tensor(out=gt[:], in0=y[:, 2048:2064], in1=gbb[:], op=ALU.add),
                     reads=ytoks(m, 2048, 2064) + ["gbb"], writes=["gt"])
                S.op("act", lambda e: e.activation(out=E1[:], in_=gt[:], func=AF.Exp), reads=["gt"], writes=["E1"])
                S.op("act", lambda e: e.activation(out=E2[:], in_=gt[:], func=AF.Exp, scale=-1.0), reads=["gt"], writes=["E2"])
                S.op("act", lambda e: e.activation(out=E2[:], in_=E2[:], func=AF.Ln, bias=1.0), reads=["E2"], writes=["E2"])
                for d, (Kd, Gd) in enumerate(((Kf, Gf), (Kb, Gb))):
                    S.op("dve", lambda e, Kd=Kd, d=d: e.scalar_tensor_tensor(
                        out=Kd[:, 0:4, :], in0=tmp2[:, 512:1024].rearrange("p (h d) -> p h d", h=4), scalar=128.0 ** -0.5,
                        in1=E1[:, 8 * d:8 * d + 4].unsqueeze(2).to_broadcast([128, 4, 128]), op0=ALU.mult, op1=ALU.mult),
                        reads=["tmp2", "E1"], writes=[Kd.name])
                    S.op("pool", lambda e, Gd=Gd, d=d: e.tensor_scalar(
                        out=Gd[:, 0:4, :], in0=E2[:, 8 * d + 4:8 * d + 8].unsqueeze(2).to_broadcast([128, 4, 128]),
                        scalar1=-1.0, scalar2=None, op0=ALU.mult), reads=["E2"], writes=[Gd.name])
                S.op("act", lambda e: e.copy(out=Vk[:, 0:4, :], in_=y[:, 1024:1536].rearrange("p (h d) -> p h d", h=4)),
                     reads=ytoks(m, 1024, 1536), writes=[Vk.name])
                S.op("act", lambda e: e.activation(out=GA[:, 0:512], in_=y[:, 1536:2048], func=AF.Sigmoid),
                     reads=ytoks(m, 1536, 2048), writes=["GA"])
                S.op("dve", lambda e: e.tensor_scalar(
                    out=Qk[:, 4:8, 0:64], in0=y[:, 2064:2320].rearrange("p (h d) -> p h d", h=4), scalar1=64.0 ** -0.5,
                    scalar2=None, op0=ALU.mult), reads=ytoks(m, 2064, 2320), writes=[Qk.name])
                S.op("dve", lambda e: e.tensor_copy(out=Kf[:, 4:8, 0:64], in_=y[:, 2320:2576].rearrange("p (h d) -> p h d", h=4)),
                     reads=ytoks(m, 2320, 2576), writes=[Kf.name])
                S.op("pool", lambda e: e.tensor_copy(out=Kb[:, 4:8, 0:64], in_=y[:, 2320:2576].rearrange("p (h d) -> p h d", h=4)),
                     reads=ytoks(m, 2320, 2576), writes=[Kb.name])
                S.op("act", lambda e: e.copy(out=Vk[:, 4:8, :], in_=y[:, 2576:3088].rearrange("p (h d) -> p h d", h=4)),
                     reads=ytoks(m, 2576, 3088), writes=[Vk.name])
                S.op("act", lambda e: e.activation(out=GA[:, 512:1024], in_=y[:, 3088:3600], func=AF.Silu),
                     reads=ytoks(m, 3088, 3600), writes=["GA"])
                S.op("dve", lambda e: e.tensor_tensor(out=GA[:], in0=GA[:], in1=nw[:], op=ALU.mult),
                     reads=["GA", "nw"], writes=["GA"])
                for j, Gd in enumerate((Gf, Gb)):
                    S.op("pe", lambda e, j=j: e.transpose(out=psm[0:16, 0:128], in_=y[:, 3600 + 16 * j:3616 + 16 * j],
                                                         identity=ident[:]), reads=ytoks(m, 3600, 3632) + ["ident"], writes=["psm"])
                    S.op("dve", lambda e, j=j: e.tensor_copy(out=baT[j][:], in_=psm[0:16, 0:128]), reads=["psm"],
                         writes=["baT%d" % j])
                    S.op("pe", lambda e, j=j: e.matmul(psm[:, 0:256], lhsT=baT[j][:], rhs=a2w[:, j, :], start=True, stop=False),
                         reads=["baT%d" % j, "a2w"], writes=["psm"])
                    S.op("pe", lambda e, j=j: e.matmul(psm[:, 0:256], lhsT=ones1[:], rhs=a2b[:, j, :], start=False, stop=True),
                         reads=["ones1", "a2b"], writes=["psm"])
                    S.op("act", lambda e: e.activation(out=tmp1[:, 0:256], in_=psm[:, 0:256], func=AF.Exp, scale=-1.0),
                         reads=["psm"], writes=["tmp1"])
                    S.op("act", lambda e: e.activation(out=tmp1[:, 0:256], in_=tmp1[:, 0:256], func=AF.Ln, bias=1.0),
                         reads=["tmp1"], writes=["tmp1"])
                    S.op("dve", lambda e, Gd=Gd: e.tensor_scalar(
                        out=Gd[:, 4:8, 0:64], in0=tmp1[:, 0:256].rearrange("p (h d) -> p h d", h=4), scalar1=-1.0 / 16.0,
                        scalar2=None, op0=ALU.mult), reads=["tmp1"], writes=[Gd.name])
                write_out(m)

            def stage2_odd(m):
                y = Y[m % 2]
                t0 = m * 128
                S.dma("sp", "rotld", rot[:], self.c_rot[t0:t0 + 128].rearrange("t (a c) -> t a c", a=2), writes=["rot"])
                S.op("act", lambda e: e.activation(out=tmp1[:, 0:512], in_=y[:, 0:512], func=AF.Silu),
                     reads=ytoks(m, 0, 512), writes=["tmp1"])
                S.op("dve", lambda e: e.tensor_scalar(out=Qk[:, 0:4, :], in0=tmp1[:, 0:512].rearrange("p (h d) -> p h d", h=4),
                                                      scalar1=128.0 ** -0.5, scalar2=None, op0=ALU.mult),
                     reads=["tmp1"], writes=[Qk.name])
                S.op("act", lambda e: e.activation(out=tmp2[:], in_=y[:, 512:1536], func=AF.Sigmoid),
                     reads=ytoks(m, 512, 1536), writes=["tmp2"])
                S.op("dve", lambda e: e.tensor_tensor(out=tmp2[:], in0=tmp2[:], in1=oml[:], op=ALU.mult),
                     reads=["tmp2", "oml"], writes=["tmp2"])
                S.op("dve", lambda e: e.tensor_tensor(out=tmp2[:], in0=tmp2[:], in1=lb[:], op=ALU.add),
                     reads=["tmp2", "lb"], writes=["tmp2"])
                for d, (Kd, Gd) in enumerate(((Kf, Gf), (Kb, Gb))):
                    fv = tmp2[:, d * 512:(d + 1) * 512].rearrange("p (h d) -> p h d", h=4)
                    S.op("pool", lambda e, Kd=Kd, fv=fv: e.tensor_scalar(out=Kd[:, 0:4, :], in0=fv, scalar1=-1.0, scalar2=1.0,
                                                                        op0=ALU.mult, op1=ALU.add), reads=["tmp2"], writes=[Kd.name])
                    S.op("dve", lambda e, fv=fv, d=d: e.tensor_scalar(out=tmp1[:, d * 512:(d + 1) * 512].rearrange("p (h d) -> p h d", h=4),
                                                                    in0=fv, scalar1=TINY, scalar2=None, op0=ALU.max),
                         reads=["tmp2"], writes=["tmp1"])
                    S.op("act", lambda e, Gd=Gd, d=d: e.activation(out=Gd[:, 0:4, :],
                                                                  in_=tmp1[:, d * 512:(d + 1) * 512].rearrange("p (h d) -> p h d", h=4),
                                                                  func=AF.Ln), reads=["tmp1"], writes=[Gd.name])
                S.op("act", lambda e: e.copy(out=Vk[:, 0:4, :], in_=y[:, 1536:2048].rearrange("p (h d) -> p h d", h=4)),
                     reads=ytoks(m, 1536, 2048), writes=[Vk.name])
                S.op("act", lambda e: e.activation(out=GA[:, 0:512], in_=y[:, 2048:2560], func=AF.Silu),
                     reads=ytoks(m, 2048, 2560), writes=["GA"])
                S.op("dve", lambda e: e.tensor_tensor(out=GA[:, 0:512], in0=GA[:, 0:512], in1=nw[:, 0:512], op=ALU.mult),
                     reads=["GA", "nw"], writes=["GA"])
                cosb = rot[:, 0, 0:64].unsqueeze(1).to_broadcast([128, 4, 64])
                sinb = rot[:, 0, 64:128].unsqueeze(1).to_broadcast([128, 4, 64])
                cosk = rot[:, 1, 0:64].unsqueeze(1).to_broadcast([128, 4, 64])
                sink = rot[:, 1, 64:128].unsqueeze(1).to_broadcast([128, 4, 64])
                for (c0, cb, sb_, dsts) in ((2560, cosb, sinb, (Qk,)), (3072, cosk, sink, (Kf, Kb))):
                    xv = y[:, c0:c0 + 512].rearrange("p (h d) -> p h d", h=4)
                    x1, x2 = xv[:, :, 0:64], xv[:, :, 64:128]
                    t1 = tmp1[:, 0:256].rearrange("p (h d) -> p h d", h=4)
                    t2 = tmp1[:, 256:512].rearrange("p (h d) -> p h d", h=4)
                    t3 = tmp1[:, 512:768].rearrange("p (h d) -> p h d", h=4)
                    t4 = tmp1[:, 768:1024].rearrange("p (h d) -> p h d", h=4)
                    yt = ytoks(m, c0, c0 + 512)
                    S.op("dve", lambda e, t1=t1, x1=x1, cb=cb: e.tensor_tensor(out=t1, in0=x1, in1=cb, op=ALU.mult),
                         reads=yt + ["rot"], writes=["tmp1"])
                    S.op("pool", lambda e, t2=t2, x2=x2, sb_=sb_: e.tensor_tensor(out=t2, in0=x2, in1=sb_, op=ALU.mult),
                         reads=yt + ["rot"], writes=["tmp1"])
                    S.op("dve", lambda e, t3=t3, x1=x1, sb_=sb_: e.tensor_tensor(out=t3, in0=x1, in1=sb_, op=ALU.mult),
                         reads=yt + ["rot"], writes=["tmp1"])
                    S.op("pool", lambda e, t4=t4, x2=x2, cb=cb: e.tensor_tensor(out=t4, in0=x2, in1=cb, op=ALU.mult),
                         reads=yt + ["rot"], writes=["tmp1"])
                    for dst in dsts:
                        S.op("dve", lambda e, dst=dst, t1=t1, t2=t2: e.tensor_tensor(out=dst[:, 4:8, 0:64], in0=t1, in1=t2,
                                                                                    op=ALU.subtract), reads=["tmp1"], writes=[dst.name])
                        S.op("pool", lambda e, dst=dst, t3=t3, t4=t4: e.tensor_tensor(out=dst[:, 4:8, 64:128], in0=t3, in1=t4,
                                                                                     op=ALU.add), reads=["tmp1"], writes=[dst.name])
                S.op("act", lambda e: e.copy(out=Vk[:, 4:8, :], in_=y[:, 3584:4096].rearrange("p (h d) -> p h d", h=4)),
                     reads=ytoks(m, 3584, 4096), writes=[Vk.name])
                S.op("act", lambda e: e.activation(out=GA[:, 512:1024], in_=y[:, 4096:4608], func=AF.Silu),
                     reads=ytoks(m, 4096, 4608), writes=["GA"])
                write_out(m)

            load_x(0)
            if even:
                for n in range(NT + 1):
                    if n < NT:
                        stage1(n)
                    if n >= 1:
                        stage2_even(n - 1)
            else:
                for n in range(NT):
                    stage1(n)
                    stage2_odd(n)
            S.flush()

    def scan_phase(self, even):
        S = self.S
        T = self.T
        L = 64
        BT = 256
        CB = BT // L
        NB = T // BT
        NCH = T // L
        BNDC = self.BND // L
        nden = 4 if even else 0
        with ExitStack() as es:
            tri = self.sb(es, "tri", [64, 8, 64], F32)
            onesb = self.sb(es, "onesb", [64, 1], BF16)
            S.dma("sp", "cst1", tri[:], self.c_tri, writes=["tri"])
            S.op("pool", lambda e: e.memset(onesb[:], 1.0), writes=["onesb"])
            QTs = [self.sb(es, "sQT%d" % i, [128, 8, BT], BF16) for i in range(2)]
            KTs = [self.sb(es, "sKT%d" % i, [128, 8, BT], BF16) for i in range(2)]
            Ks = [self.sb(es, "sK%d" % i, [64, CB, 8, 128], BF16) for i in range(2)]
            Vs = [self.sb(es, "sV%d" % i, [64, CB, 8, 128], BF16) for i in range(2)]
            Gs = [self.sb(es, "sG%d" % i, [64, CB, 8, 128], F32) for i in range(2)]
            Gh = [self.sb(es, "sGh%d" % i, [64, CB, 8, 128], BF16) for i in range(2)]
            Gl = [self.sb(es, "sGl%d" % i, [64, CB, 8, 128], BF16) for i in range(2)]
            trib = self.sb(es, "trib", [64, 8, 64], BF16)
            S.op("dve", lambda e: e.tensor_copy(out=trib[:], in_=tri[:]), reads=["tri"], writes=["trib"])
            Os = [self.sb(es, "sO%d" % i, [64, CB, 8, 128], F32) for i in range(2)]
            EE = [self.sb(es, "EE%d" % i, [128, 8, 2, 64], F32) for i in range(2)]
            Ek = [self.sb(es, "Ek%d" % i, [128, 8, 64], F32) for i in range(2)]
            Es = [self.sb(es, "Es%d" % i, [64, 8, 128], F32) for i in range(2)]
            Q1 = [self.sb(es, "Q1_%d" % i, [128, 8, 64], BF16) for i in range(2)]
            K1 = [self.sb(es, "K1_%d" % i, [128, 8, 64], BF16) for i in range(2)]
            Q2 = [self.sb(es, "Q2_%d" % i, [128, 8, 64], BF16) for i in range(2)]
            K2 = [self.sb(es, "K2_%d" % i, [64, 8, 128], BF16) for i in range(2)]
            PT = [self.sb(es, "PT%d" % i, [64, 8, 64], BF16) for i in range(2)]
            S32 = self.sb(es, "S32", [128, 8, 128], F32)
            Sb = self.sb(es, "Sb", [128, 8, 128], BF16)
            Sn = self.sb(es, "Sn", [128, 4], F32)
            Snb = self.sb(es, "Snb", [128, 4], BF16)
            dn = self.sb(es, "dn", [64, 8], F32)
            p_suf = self.ps(es, "p_suf", [64, 512])
            p_bm = self.ps(es, "p_bm", [128, 4, 2, 64])
            p_st = self.ps(es, "p_st", [64, 8, 64])
            p_o = self.ps(es, "p_o", [64, 8, 128])
            p_u = self.ps(es, "p_u", [128, 8, 128])
            p_sm = self.ps(es, "p_sm", [128, 16])
            S.flush()

            for d in range(2):
                rev = d == 1
                i0 = 4 if rev else 0
                tim = trib[:, i0:i0 + 2, :]
                tsuf = trib[:, i0 + 2, :]
                msk = tri[:, i0 + 3, :]
                mskb = msk.unsqueeze(1).to_broadcast([64, 8, 64])
                deccol = 0 if rev else 63
                for tl, nm in ((S32, "S32"), (Sb, "Sb"), (Sn, "Sn"), (Snb, "Snb")):
                    S.op("pool", lambda e, tl=tl: e.memset(tl[:], 0.0), reads=[nm], writes=[nm])
                blocks = list(range(NB))
                if rev:
                    blocks = blocks[::-1]
                seq = []
                for ib, bi in enumerate(blocks):
                    cs = list(range(CB))
                    if rev:
                        cs = cs[::-1]
                    for c in cs:
                        seq.append((ib, bi, c))

                def load_block(bi, slot, d=d):
                    t0 = bi * BT
                    S.dma("sp", "sld_q%d" % slot, QTs[slot][:], self.QT[:, :, t0:t0 + BT].rearrange("h d t -> d h t"),
                          writes=["sQT%d" % slot])
                    S.dma("sp", "sld_k%d" % slot, KTs[slot][:], self.KT[d][:, :, t0:t0 + BT].rearrange("h d t -> d h t"),
                          writes=["sKT%d" % slot])
                    S.dma("sp", "sld_kt%d" % slot, Ks[slot][:],
                          self.Kt[d][t0:t0 + BT].rearrange("(c p) h e -> p c h e", p=L), writes=["sK%d" % slot])
                    S.dma("sp", "sld_v%d" % slot, Vs[slot][:],
                          self.Vt[t0:t0 + BT].rearrange("(c p) h e -> p c h e", p=L), writes=["sV%d" % slot])
                    S.dma("sp", "sld_g%d" % slot, Gs[slot][:],
                          self.Gt[d][t0:t0 + BT].rearrange("(c p) h e -> p c h e", p=L), writes=["sG%d" % slot])
                    S.op("act", lambda e, slot=slot: e.copy(out=Gh[slot][:], in_=Gs[slot][:]), reads=["sG%d" % slot],
                         writes=["sGh%d" % slot])
                    S.op("pool", lambda e, slot=slot: e.tensor_tensor(out=Gl[slot][:], in0=Gs[slot][:], in1=Gh[slot][:],
                                                                     op=ALU.subtract), reads=["sG%d" % slot, "sGh%d" % slot],
                         writes=["sGl%d" % slot])

                def prep_round(k, r):
                    ib, bi, c = seq[k]
                    slot = ib % 2
                    p = k % 2
                    gh_n, gl_n = "sGh%d" % slot, "sGl%d" % slot
                    gh = Gh[slot][:, c, :, :]
                    gl = Gl[slot][:, c, :, :]
                    S.op("pe", lambda e, gh=gh, r=r: e.matmul(p_suf[:], lhsT=tsuf, rhs=gh[:, 4 * r:4 * r + 4, :],
                                                              start=True, stop=False), reads=[gh_n, "trib"], writes=["p_suf"])
                    S.op("pe", lambda e, gl=gl, r=r: e.matmul(p_suf[:], lhsT=tsuf, rhs=gl[:, 4 * r:4 * r + 4, :],
                                                              start=False, stop=True), reads=[gl_n, "trib"], writes=["p_suf"])
                    S.op("act", lambda e, p=p, r=r: e.activation(out=Es[p][:, 4 * r:4 * r + 4, :],
                                                                in_=p_suf[:].rearrange("q (h e) -> q h e", h=4), func=AF.Exp),
                         reads=["p_suf"], writes=[("Es", p, r)])
                    for h4 in range(4):
                        S.op("pe", lambda e, gh=gh, r=r, h4=h4: e.matmul(p_bm[:, h4, :, :], lhsT=gh[:, 4 * r + h4, :], rhs=tim,
                                                                        start=True, stop=False), reads=[gh_n, "trib"], writes=["p_bm"])
                        S.op("pe", lambda e, gl=gl, r=r, h4=h4: e.matmul(p_bm[:, h4, :, :], lhsT=gl[:, 4 * r + h4, :], rhs=tim,
                                                                        start=False, stop=True), reads=[gl_n, "trib"], writes=["p_bm"])
                    S.op("act", lambda e, p=p, r=r: e.activation(out=EE[p][:, 4 * r:4 * r + 4, :, :], in_=p_bm[:], func=AF.Exp),
                         reads=["p_bm"], writes=[("EE", p, r)])
                    S.op("act", lambda e, p=p, r=r: e.activation(out=Ek[p][:, 4 * r:4 * r + 4, :], in_=p_bm[:, :, 1, :],
                                                                func=AF.Exp, scale=-1.0), reads=["p_bm"], writes=[("Ek", p, r)])

                def prep_tail(k):
                    ib, bi, c = seq[k]
                    slot = ib % 2
                    p = k % 2
                    tsl = slice(c * L, (c + 1) * L)
                    ee = [("EE", p, 0), ("EE", p, 1)]
                    S.op("dve", lambda e, p=p, slot=slot, tsl=tsl: e.tensor_tensor(
                        out=Q1[p][:], in0=QTs[slot][:, :, tsl], in1=EE[p][:, :, 1, :], op=ALU.mult),
                        reads=["sQT%d" % slot] + ee, writes=[("Q1", p)])
                    S.op("pool", lambda e, p=p, slot=slot, tsl=tsl: e.tensor_tensor(
                        out=K1[p][:], in0=KTs[slot][:, :, tsl], in1=Ek[p][:], op=ALU.mult),
                        reads=["sKT%d" % slot, ("Ek", p, 0), ("Ek", p, 1)], writes=[("K1", p)])
                    S.op("pool", lambda e, p=p, slot=slot, tsl=tsl: e.tensor_tensor(
                        out=Q2[p][:], in0=QTs[slot][:, :, tsl], in1=EE[p][:, :, 0, :], op=ALU.mult),
                        reads=["sQT%d" % slot] + ee, writes=[("Q2", p)])
                    S.op("dve", lambda e, p=p, slot=slot, c=c: e.tensor_tensor(
                        out=K2[p][:], in0=Ks[slot][:, c, :, :], in1=Es[p][:], op=ALU.mult),
                        reads=["sK%d" % slot, ("Es", p, 0), ("Es", p, 1)], writes=[("K2", p)])

                def main_a(k):
                    ib, bi, c = seq[k]
                    slot = ib % 2
                    p = k % 2
                    gc = bi * CB + c
                    bidx = None
                    if not rev and gc > 0 and gc % BNDC == 0:
                        bidx = gc // BNDC
                    if rev and (gc + 1) % BNDC == 0 and gc + 1 < NCH:
                        bidx = (gc + 1) // BNDC
                    if bidx is not None:
                        S.op("dve", lambda e, bidx=bidx: e.tensor_scalar(
                            out=S32[:], in0=S32[:], scalar1=self.kf[:, bidx:bidx + 1], scalar2=None, op0=ALU.mult),
                            reads=["S32", "kf"], writes=["S32"])
                        S.op("pool", lambda e: e.tensor_copy(out=Sb[:], in_=S32[:]), reads=["S32"], writes=["Sb"])
                        if nden:
                            S.op("dve", lambda e, bidx=bidx: e.tensor_scalar(
                                out=Sn[:], in0=Sn[:], scalar1=self.kf[:, bidx:bidx + 1], scalar2=None, op0=ALU.mult),
                                reads=["Sn", "kf"], writes=["Sn"])
                            S.op("pool", lambda e: e.tensor_copy(out=Snb[:], in_=Sn[:]), reads=["Sn"], writes=["Snb"])
                    for h in range(8):
                        S.op("pe", lambda e, h=h, p=p: e.matmul(p_st[:, h, :], lhsT=K1[p][:, h, :], rhs=Q1[p][:, h, :],
                                                                start=True, stop=True), reads=[("K1", p), ("Q1", p)], writes=["p_st"])
                    S.op("dve", lambda e, p=p: e.tensor_tensor(out=PT[p][:], in0=p_st[:], in1=mskb, op=ALU.mult),
                         reads=["p_st", "tri"], writes=[("PT", p)])

                def main_b(k):
                    ib, bi, c = seq[k]
                    slot = ib % 2
                    p = k % 2
                    vn = "sV%d" % slot
                    on = "sO%d" % slot
                    for h in range(8):
                        S.op("pe", lambda e, h=h, p=p: e.matmul(p_o[:, h, :], lhsT=Q2[p][:, h, :], rhs=Sb[:, h, :], start=True, stop=False),
                             reads=[("Q2", p), "Sb"], writes=["p_o"])
                        S.op("pe", lambda e, h=h, p=p, slot=slot, c=c: e.matmul(p_o[:, h, :], lhsT=PT[p][:, h, :], rhs=Vs[slot][:, c, h, :],
                                                                               start=False, stop=True), reads=[("PT", p), vn], writes=["p_o"])
                    for h in range(nden):
                        S.op("pe", lambda e, h=h, p=p: e.matmul(p_sm[0:64, h:h + 1], lhsT=Q2[p][:, h, :], rhs=Snb[:, h:h + 1],
                                                                start=True, stop=False), reads=[("Q2", p), "Snb"], writes=["p_smd"])
                        S.op("pe", lambda e, h=h, p=p: e.matmul(p_sm[0:64, h:h + 1], lhsT=PT[p][:, h, :], rhs=onesb[:],
                                                                start=False, stop=True), reads=[("PT", p), "onesb"], writes=["p_smd"])
                    for h in range(8):
                        S.op("pe", lambda e, h=h, p=p, slot=slot, c=c: e.matmul(p_u[:, h, :], lhsT=K2[p][:, h, :], rhs=Vs[slot][:, c, h, :],
                                                                               start=True, stop=True), reads=[("K2", p), vn], writes=["p_u"])
                    for h in range(nden):
                        S.op("pe", lambda e, h=h, p=p: e.matmul(p_sm[:, 8 + h:9 + h], lhsT=K2[p][:, h, :], rhs=onesb[:],
                                                                start=True, stop=True), reads=[("K2", p), "onesb"], writes=["p_smu"])
                    if nden:
                        S.op("dve", lambda e: e.tensor_copy(out=dn[:, 0:4], in_=p_sm[0:64, 0:4]), reads=["p_smd"], writes=["dn"])
                        S.op("dve", lambda e: e.scalar_tensor_tensor(out=dn[:, 4:8], in0=dn[:, 0:4], scalar=-1.0, in1=dn[:, 0:4],
                                                                     op0=ALU.mult, op1=ALU.max), reads=["dn"], writes=["dn"])
                        S.op("dve", lambda e: e.tensor_scalar(out=dn[:, 4:8], in0=dn[:, 4:8], scalar1=1.0, scalar2=None, op0=ALU.max),
                             reads=["dn"], writes=["dn"])
                        S.op("dve", lambda e: e.reciprocal(out=dn[:, 0:4], in_=dn[:, 4:8]), reads=["dn"], writes=["dn"])
                        S.op("dve", lambda e, slot=slot, c=c: e.tensor_tensor(
                            out=Os[slot][:, c, 0:4, :], in0=p_o[:, 0:4, :], in1=dn[:, 0:4].unsqueeze(2).to_broadcast([64, 4, 128]),
                            op=ALU.mult), reads=["p_o", "dn"], writes=[(on, c, 0)])
                    else:
                        S.op("dve", lambda e, slot=slot, c=c: e.tensor_copy(out=Os[slot][:, c, 0:4, :], in_=p_o[:, 0:4, :]),
                             reads=["p_o"], writes=[(on, c, 0)])
                    S.op("act", lambda e, slot=slot, c=c: e.copy(out=Os[slot][:, c, 4:8, :], in_=p_o[:, 4:8, :]),
                         reads=["p_o"], writes=[(on, c, 1)])
                    ee = [("EE", p, 0), ("EE", p, 1)]
                    S.op("dve", lambda e, p=p: e.tensor_tensor(
                        out=S32[:], in0=S32[:], in1=EE[p][:, :, 0, deccol:deccol + 1].to_broadcast([128, 8, 128]), op=ALU.mult),
                        reads=["S32"] + ee, writes=["S32"])
                    S.op("dve", lambda e: e.tensor_tensor(out=S32[:], in0=S32[:], in1=p_u[:], op=ALU.add),
                         reads=["S32", "p_u"], writes=["S32"])
                    S.op("act", lambda e: e.copy(out=Sb[:], in_=S32[:]), reads=["S32"], writes=["Sb"])
                    if nden:
                        S.op("dve", lambda e, p=p: e.tensor_tensor(out=Sn[:], in0=Sn[:], in1=EE[p][:, 0:4, 0, deccol], op=ALU.mult),
                             reads=["Sn"] + ee, writes=["Sn"])
                        S.op("dve", lambda e: e.tensor_tensor(out=Sn[:], in0=Sn[:], in1=p_sm[:, 8:12], op=ALU.add),
                             reads=["Sn", "p_smu"], writes=["Sn"])
                        S.op("act", lambda e: e.copy(out=Snb[:], in_=Sn[:]), reads=["Sn"], writes=["Snb"])
                    last_c = 0 if rev else CB - 1
                    if c == last_c:
                        t0 = bi * BT
                        S.dma("sp", "sst%d" % slot, self.OD[d][t0:t0 + BT].rearrange("(c p) (h e) -> p c h e", p=L, h=8),
                              Os[slot][:], reads=[(on, c_, hf) for c_ in range(CB) for hf in range(2)], writes=[("dO", d, bi)])

                load_block(blocks[0], 0)
                if NB > 1:
                    load_block(blocks[1], 1)
                prep_round(0, 0)
                prep_round(0, 1)
                prep_tail(0)
                for k in range(len(seq)):
                    nxt = k + 1 < len(seq)
                    if nxt:
                        prep_round(k + 1, 0)
                    main_a(k)
                    if nxt:
                        prep_round(k + 1, 1)
                    main_b(k)
                    ib_k = seq[k][0]
                    if (k + 1) % CB == 0 and ib_k + 2 < NB:
                        load_block(blocks[ib_k + 2], ib_k % 2)
                    if nxt:
                        prep_tail(k + 1)
                S.flush()

    def epi_phase(self, x_in, x_out, even, w_out, g_vec, b_vec):
        S = self.S
        T = self.T
        NT = T // 128
        with ExitStack() as es:
            ident = self.make_identity(es)
            wo = self.sb(es, "ewo", [128, 8, 1024], BF16)
            xs = [self.sb(es, "ex%d" % i, [128, 1024], F32) for i in range(2)]
            of = [self.sb(es, "eof%d" % i, [128, 1024], F32) for i in range(2)]
            ob = [self.sb(es, "eob%d" % i, [128, 1024], F32) for i in range(2)]
            ga = [self.sb(es, "ega%d" % i, [128, 1024], F32) for i in range(2)]
            sq = self.sb(es, "esq", [128, 1024], F32)
            onT = self.sb(es, "eonT", [128, 8, 128], BF16)
            s1 = self.sb(es, "es1", [128, 8], F32)
            s2 = self.sb(es, "es2", [128, 8], F32)
            cfl = self.sb(es, "ecfl", [128, 8], F32)
            gb = self.sb(es, "egb", [128, 1024], F32)
            bb = self.sb(es, "ebb", [128, 1024], F32)
            stats = self.sb(es, "estats", [128, 2, 6], F32)
            mv = self.sb(es, "emv", [128, 2], F32)
            rstd = self.sb(es, "erstd", [128, 1], F32)
            nmr = self.sb(es, "enmr", [128, 1], F32)
            pt = [self.ps(es, "ept%d" % i, [128, 512]) for i in range(2)]
            py = [self.ps(es, "epy%d" % i, [128, 512]) for i in range(2)]
            S.dma("sp", "cst1", gb[:], g_vec.partition_broadcast(128), writes=["gb"])
            S.dma("sp", "cst2", bb[:], b_vec.partition_broadcast(128), writes=["bb"])
            lnh = (0, 4) if even else (4, 8)
            S.op("pool", lambda e: e.memset(cfl[:], 0.0), writes=["ecfl"])
            S.op("pool", lambda e: e.memset(cfl[:, lnh[0]:lnh[1]], 1.0 / 128.0), reads=["ecfl"], writes=["ecfl"])
            self.load_weight_bf16(w_out, wo, 1024, 1024, [_View(of[0][:]), _View(of[1][:])], "ewo")
            S.flush()
            xin_v = x_in.rearrange("(n p) d -> n p d", p=128)
            xout_v = x_out.rearrange("(n p) d -> n p d", p=128)

            def load(i):
                sl = i % 2
                S.dma("sp", "eld_x%d" % sl, xs[sl][:], xin_v[i], writes=["ex%d" % sl])
                S.dma("sp", "eld_f%d" % sl, of[sl][:], self.OD[0][i * 128:(i + 1) * 128], reads=[("dO", 0, i)], writes=["eof%d" % sl])
                S.dma("sp", "eld_b%d" % sl, ob[sl][:], self.OD[1][i * 128:(i + 1) * 128], reads=[("dO", 1, i)], writes=["eob%d" % sl])
                S.dma("sp", "eld_g%d" % sl, ga[sl][:], self.GATE[i * 128:(i + 1) * 128], reads=[("dGA", i)], writes=["ega%d" % sl])

            load(0)
            for i in range(NT):
                sl = i % 2
                if i + 1 < NT:
                    load(i + 1)
                o = of[sl]
                on_ = "eof%d" % sl
                o3 = o[:].rearrange("p (h e) -> p h e", h=8)
                S.op("dve", lambda e, o=o, sl=sl: e.tensor_tensor(out=o[:], in0=o[:], in1=ob[sl][:], op=ALU.add),
                     reads=[on_, "eob%d" % sl], writes=[on_])
                S.op("act", lambda e, o=o: e.activation(out=sq[:], in_=o[:], func=AF.Square), reads=[on_], writes=["esq"])
                S.op("dve", lambda e, o3=o3: e.tensor_reduce(out=s1[:], in_=o3, axis=AX.X, op=ALU.add), reads=[on_], writes=["es1"])
                S.op("dve", lambda e: e.tensor_reduce(out=s2[:], in_=sq[:].rearrange("p (h e) -> p h e", h=8), axis=AX.X, op=ALU.add),
                     reads=["esq"], writes=["es2"])
                S.op("dve", lambda e: e.tensor_tensor(out=s1[:], in0=s1[:], in1=cfl[:], op=ALU.mult), reads=["es1", "ecfl"], writes=["es1"])
                S.op("dve", lambda e: e.tensor_tensor(out=sq[:, 0:8], in0=s1[:], in1=s1[:], op=ALU.mult), reads=["es1"], writes=["esq"])
                S.op("dve", lambda e: e.scalar_tensor_tensor(out=s2[:], in0=s2[:], scalar=1.0 / 128.0, in1=sq[:, 0:8],
                                                             op0=ALU.mult, op1=ALU.subtract), reads=["es2", "esq"], writes=["es2"])
                S.op("act", lambda e: e.activation(out=s2[:], in_=s2[:], func=AF.Sqrt, bias=float(LN_EPS), scale=1.0),
                     reads=["es2"], writes=["es2"])
                S.op("dve", lambda e: e.reciprocal(out=s2[:], in_=s2[:]), reads=["es2"], writes=["es2"])
                S.op("dve", lambda e, o3=o3: e.tensor_tensor(out=o3, in0=o3, in1=s1[:].unsqueeze(2).to_broadcast([128, 8, 128]),
                                                            op=ALU.subtract), reads=[on_, "es1"], writes=[on_])
                S.op("pool", lambda e, o3=o3: e.tensor_tensor(out=o3, in0=o3, in1=s2[:].unsqueeze(2).to_broadcast([128, 8, 128]),
                                                             op=ALU.mult), reads=[on_, "es2"], writes=[on_])
                S.op("dve", lambda e, o=o, sl=sl: e.tensor_tensor(out=o[:], in0=o[:], in1=ga[sl][:], op=ALU.mult),
                     reads=[on_, "ega%d" % sl], writes=[on_])
                for half in range(2):
                    p = pt[half]
                    for k4 in range(4):
                        k = half * 4 + k4
                        S.op("pe", lambda e, p=p, k=k, k4=k4, o=o: e.transpose(out=p[:, k4 * 128:(k4 + 1) * 128],
                                                                             in_=o[:, k * 128:(k + 1) * 128], identity=ident[:]),
                             reads=[on_, "ident"], writes=["ept%d" % half])
                    if half == 0:
                        S.op("dve", lambda e, p=p: e.tensor_copy(out=onT[:, 0:4, :], in_=p[:].rearrange("p (k t) -> p k t", k=4)),
                             reads=["ept0"], writes=[("eonT", 0)])
                    else:
                        S.op("act", lambda e, p=p: e.copy(out=onT[:, 4:8, :], in_=p[:].rearrange("p (k t) -> p k t", k=4)),
                             reads=["ept1"], writes=[("eonT", 1)])
                xt = xs[sl]
                xtok = "ex%d" % sl
                for c in range(2):
                    for k in range(8):
                        S.op("pe", lambda e, c=c, k=k: e.matmul(py[c][:], lhsT=onT[:, k, :], rhs=wo[:, k, c * 512:(c + 1) * 512],
                                                                start=(k == 0), stop=(k == 7)),
                             reads=[("eonT", 0), ("eonT", 1), "ewo"], writes=["epy%d" % c])
                    S.op("dve", lambda e, c=c, xt=xt: e.scalar_tensor_tensor(
                        out=xt[:, c * 512:(c + 1) * 512], in0=xt[:, c * 512:(c + 1) * 512], scalar=ALPHA, in1=py[c][:],
                        op0=ALU.mult, op1=ALU.add), reads=["epy%d" % c, xtok], writes=[xtok])
                self.layer_norm_rows(xt[:], xtok, gb, bb, stats, mv, rstd, nmr, LN_EPS, pfx="e")
                S.dma("sp", "est%d" % sl, xout_v[i], xt[:], reads=[xtok], writes=[("xout", i)])
            S.flush()


class Builder(Builder_, MixerMixin):
    def setup_globals(self):
        nc = self.nc
        self.c_ident = self.dram_in("c_ident", [128, 128])
        self.c_shift = nc.dram_tensor("c_shift", [128, 8, 128], BF16, kind="ExternalInput").ap()
        self.c_tri = self.dram_in("c_tri", [64, 8, 64])
        self.c_rot = self.dram_in("c_rot", [self.T, 256])
        self.c_flags = self.dram_in("c_flags", [128, 4])
        self.kf = self.sb(self.es, "kf", [128, 4], F32)
        self.S.dma("sp", "kf", self.kf[:], self.c_flags, writes=["kf"])
        self.S.flush()
        self.alloc_scratch()


def host_consts():
    import ml_dtypes
    c = {}
    c["c_ident"] = np.eye(128, dtype=np.float32)
    sh = np.zeros((128, 8, 128), np.float32)
    u = np.arange(128)[:, None]
    t = np.arange(128)[None, :]
    for oi, o in enumerate((-2, -1, 1, 2)):
        sh[:, oi, :] = (u == t + o)
        if o < 0:
            sh[:, 4 + oi, :] = (u == 128 + t + o)
        else:
            sh[:, 4 + oi, :] = (u == t + o - 128)
    c["c_shift"] = sh.astype(ml_dtypes.bfloat16)
    tri = np.zeros((64, 8, 64), np.float32)
    u = np.arange(64)[:, None]
    t = np.arange(64)[None, :]
    tri[:, 0, :] = (u <= t)
    tri[:, 1, :] = ((u >= 32) & (u <= t)) * 1.0 - ((u > t) & (u <= 31)) * 1.0
    tri[:, 2, :] = (u > t)
    tri[:, 3, :] = (u <= t)
    tri[:, 4, :] = (u >= t)
    tri[:, 5, :] = ((u >= t) & (u <= 31)) * 1.0 - ((u >= 32) & (u < t)) * 1.0
    tri[:, 6, :] = (u < t)
    tri[:, 7, :] = (u >= t)
    c["c_tri"] = tri
    return c


def rot_table(T, seq_len):
    pos = (np.arange(T) % seq_len).astype(np.float32)
    inv = (1.0 / (10000.0 ** (np.arange(0, 128, 2, dtype=np.float32) / np.float32(128)))).astype(np.float32)
    ang = (pos[:, None] * inv[None, :]).astype(np.float32)
    cos, sin = np.cos(ang).astype(np.float32), np.sin(ang).astype(np.float32)
    sc = np.float32(128.0 ** -0.5)
    return np.concatenate([cos, sin, cos * sc, sin * sc], axis=1).astype(np.float32)


WNAMES = ["ffn1_w_in", "ffn1_w_out", "ffn2_w_in", "ffn2_w_out", "ln_g", "ln_b", "ev_w_in", "ev_gate_b", "ev_conv_w",
          "ev_gla_a2_w", "ev_gla_a2_b", "ev_norm_w", "ev_w_out", "od_w_in", "od_lb_logits", "od_norm_w", "od_w_out"]


def build_program(T, bnd, shapes, depth=DEPTH):
    B = Builder(T, bnd, depth)
    B.BND = bnd
    B.setup_globals()
    W = {n: B.dram_in(n, shapes[n]) for n in WNAMES}
    x = B.dram_in("x", [T, 1024])
    y = B.dram_out("y", [T, 1024])
    xa = B.dram_tmp("s_xa", [T, 1024])
    xb = B.dram_tmp("s_xb", [T, 1024])
    cur = x
    for l in range(depth):
        j = l // 2
        B.ffn_phase(cur, xa, W["ffn1_w_in"][l], W["ffn1_w_out"][l], W["ln_g"][l, 0], W["ln_b"][l, 0])
        if l % 2 == 0:
            P = dict(w_in=W["ev_w_in"][j], gate_b=W["ev_gate_b"][j], conv_w=W["ev_conv_w"][j], a2_w=W["ev_gla_a2_w"][j],
                     a2_b=W["ev_gla_a2_b"][j], norm_w=W["ev_norm_w"][j])
            wo = W["ev_w_out"][j]
        else:
            P = dict(w_in=W["od_w_in"][j], lb_logits=W["od_lb_logits"], norm_w=W["od_norm_w"][j], layer_idx=j)
            wo = W["od_w_out"][j]
        B.proj_phase(xa, l % 2 == 0, P)
        B.scan_phase(l % 2 == 0)
        B.epi_phase(xa, xb, l % 2 == 0, wo, W["ln_g"][l, 1], W["ln_b"][l, 1])
        dst = y if l == depth - 1 else xa
        B.ffn_phase(xb, dst, W["ffn2_w_in"][l], W["ffn2_w_out"][l], W["ln_g"][l, 2], W["ln_b"][l, 2])
        cur = xa
    B.es.close()
    return B


def kernel(**inputs):
    xp = np.asarray(inputs["x_prompt"], np.float32)
    xs = np.asarray(inputs["x_sample"], np.float32)
    T = 16384
    BND = 4096
    weights = {n: np.ascontiguousarray(np.asarray(inputs[n], np.float32)) for n in WNAMES}
    shapes = {n: weights[n].shape for n in WNAMES}
    B = build_program(T, BND, shapes)
    consts = host_consts()
    rot_p = rot_table(T, 16384)
    rot_s = rot_table(T, 4096)
    fl_p = np.ones((128, 4), np.float32)
    fl_s = np.zeros((128, 4), np.float32)
    xcore = [xp[0], xp[1], xs[0:4].reshape(T, 1024), xs[4:8].reshape(T, 1024)]
    maps = []
    for c in range(N_CORES):
        src = c if c < 4 else 2 + (c % 2)
        m = dict(weights)
        m.update(consts)
        m["x"] = np.ascontiguousarray(xcore[src])
        m["c_rot"] = rot_p if src < 2 else rot_s
        m["c_flags"] = fl_p if src < 2 else fl_s
        maps.append(m)
    res = run_bass_kernel_spmd(B.nc, maps, core_ids=list(range(N_CORES)))
    r = res.results
    y_prompt = np.stack([np.asarray(r[0]["y"]), np.asarray(r[1]["y"])], 0).astype(np.float32)
    y_sample = np.concatenate([np.asarray(r[2]["y"]).reshape(4, 4096, 1024),
                               np.asarray(r[3]["y"]).reshape(4, 4096, 1024)], 0).astype(np.float32)
    return (y_prompt, y_sample)
```

```python
import numpy as np
from contextlib import ExitStack
import concourse.bass as bass
import concourse.mybir as mybir
from concourse.bass_utils import run_bass_kernel_spmd

F32 = mybir.dt.float32
BF16 = mybir.dt.bfloat16
AF = mybir.ActivationFunctionType
ALU = mybir.AluOpType
AX = mybir.AxisListType

D_MODEL = 1024
DEPTH = 4
D_FF = 2816
LN_EPS = 1e-5
ALPHA = (2 * DEPTH) ** 0.25
N_CORES = 8


class _Op:
    __slots__ = ("eng", "fn", "deps", "is_dma", "key", "cnt", "signal", "sigval", "dma_all")

    def __init__(self, eng, fn, is_dma=False, key=None):
        self.eng = eng
        self.fn = fn
        self.deps = []
        self.is_dma = is_dma
        self.key = key
        self.cnt = 0
        self.signal = False
        self.sigval = 0
        self.dma_all = ()


class Sched:
    CENG = ("pe", "act", "dve", "pool")
    ALLENG = ("pe", "act", "dve", "pool", "sp")

    def __init__(self, nc, es):
        self.nc = nc
        self.es = es
        self.sem = {e: es.enter_context(nc.semaphore("sem_" + e)) for e in self.CENG}
        self.sig_cnt = {e: 0 for e in self.CENG}
        self.dma_sem = {}
        self.dma_cnt = {}
        self.waited = {e: {} for e in self.ALLENG}
        self.tok = {}
        self.ops = []
        self.last = {e: None for e in self.ALLENG}
        self.n_inst = 0

    STRICT = True

    def _need(self, x, w):
        return not (w.eng == "pe" and x.eng == "pe")

    def _add(self, op, reads, writes):
        deps = op.deps
        for t in reads:
            st = self.tok.get(t)
            if st is None:
                st = self.tok[t] = [None, []]
            w = st[0]
            if w is not None and self._need(op, w):
                deps.append(w)
            rl = st[1]
            if not op.is_dma:
                for i_, r_ in enumerate(rl):
                    if r_.eng == op.eng and not r_.is_dma:
                        rl[i_] = op
                        break
                else:
                    rl.append(op)
            else:
                rl.append(op)
        for t in writes:
            st = self.tok.get(t)
            if st is None:
                st = self.tok[t] = [None, []]
            w = st[0]
            if w is not None and self._need(op, w):
                deps.append(w)
            for r in st[1]:
                if r is not op and self._need(op, r):
                    deps.append(r)
            st[0] = op
            st[1] = []
        for d in deps:
            d.signal = True
        self.ops.append(op)
        self.last[op.eng] = op

    def op(self, eng, fn, reads=(), writes=()):
        o = _Op(eng, fn)
        self._add(o, reads, writes)
        return o

    def dma(self, eng, key, out, in_, reads=(), writes=()):
        if key not in self.dma_sem:
            self.dma_sem[key] = self.es.enter_context(
                self.nc.semaphore("dsem%d" % len(self.dma_sem)))
            self.dma_cnt[key] = 0
        o = _Op(eng, lambda e: e.dma_start(out=out, in_=in_), is_dma=True, key=key)
        self._add(o, reads, writes)
        self.dma_cnt[key] += 16
        o.cnt = self.dma_cnt[key]
        o.signal = True
        return o

    def barrier(self):
        lasts = [self.last[e] for e in self.ALLENG if self.last[e] is not None]
        dma_all = [(k, c) for k, c in self.dma_cnt.items()]
        for e in self.ALLENG:
            o = _Op(e, None)
            for l in lasts:
                if l.eng != e or l.is_dma:
                    if not l.is_dma:
                        o.deps.append(l)
                        l.signal = True
            o.dma_all = dma_all
            self.ops.append(o)
        self.tok = {}

    def flush(self):
        self.barrier()
        ops = self.ops
        self.ops = []
        for o in ops:
            if o.signal and not o.is_dma:
                self.sig_cnt[o.eng] += 1
                o.sigval = self.sig_cnt[o.eng]
        per = {e: [o for o in ops if o.eng == e] for e in self.ALLENG}
        nc = self.nc

        def emit(eng_name, engine):
            waited = self.waited[eng_name]
            for o in per[eng_name]:
                need = {}
                for d in o.deps:
                    if d.is_dma:
                        k = ("d", d.key)
                        v = d.cnt
                    else:
                        k = ("c", d.eng)
                        v = d.sigval
                    if v > need.get(k, 0):
                        need[k] = v
                for (k, c) in o.dma_all:
                    kk = ("d", k)
                    if c > need.get(kk, 0):
                        need[kk] = c
                for k, v in need.items():
                    if v > waited.get(k, 0):
                        waited[k] = v
                        s = self.dma_sem[k[1]] if k[0] == "d" else self.sem[k[1]]
                        engine.wait_ge(s, v)
                        self.n_inst += 1
                if o.fn is None:
                    continue
                ins = o.fn(engine)
                self.n_inst += 1
                if o.is_dma:
                    ins.then_inc(self.dma_sem[o.key], 16)
                elif o.signal:
                    ins.then_inc(self.sem[o.eng], 1)

        with nc.Block() as blk:
            @blk.tensor
            def _(e):
                emit("pe", e)

            @blk.scalar
            def _(e):
                emit("act", e)

            @blk.vector
            def _(e):
                emit("dve", e)

            @blk.gpsimd
            def _(e):
                emit("pool", e)

            @blk.sync
            def _(e):
                emit("sp", e)
        self.last = {e: None for e in self.ALLENG}


class Builder_:
    def __init__(self, T, seq_len, depth=DEPTH):
        self.T = T
        self.seq_len = seq_len
        self.depth = depth
        self.nc = bass.Bass("TRN2", target_bir_lowering=False)
        self.es = ExitStack()
        self.S = Sched(self.nc, self.es)
        self.uid = 0

    def dram_in(self, name, shape):
        return self.nc.dram_tensor(name, list(shape), F32, kind="ExternalInput").ap()

    def dram_out(self, name, shape):
        return self.nc.dram_tensor(name, list(shape), F32, kind="ExternalOutput").ap()

    DEBUG = False

    def dram_tmp(self, name, shape, dtype=F32):
        if self.DEBUG:
            return self.nc.dram_tensor(name, list(shape), dtype, kind="ExternalOutput").ap()
        return self.nc.dram_tensor(name, list(shape), dtype).ap()

    def sb(self, es, name, shape, dtype):
        self.uid += 1
        return es.enter_context(self.nc.sbuf_tensor("%s_%d" % (name, self.uid), list(shape), dtype))

    def ps(self, es, name, shape, dtype=F32):
        self.uid += 1
        return es.enter_context(self.nc.psum_tensor("%s_%d" % (name, self.uid), list(shape), dtype))

    def make_identity(self, es):
        S = self.S
        ident = self.sb(es, "ident", [128, 128], F32)
        S.dma("sp", "ident", ident[:], self.c_ident, writes=["ident"])
        return ident

    def load_weight_bf16(self, w_dram, wt, K, N, stage_tiles, tokname, scale=None):
        S = self.S
        SW = stage_tiles[0].shape[-1]
        kc = K // 128
        wv = w_dram.rearrange("(k p) n -> p k n", p=128)
        i = 0
        for k in range(kc):
            for n0 in range(0, N, SW):
                n1 = min(N, n0 + SW)
                si = i % len(stage_tiles)
                st = stage_tiles[si]
                stok = "wstage%d" % si
                S.dma("sp", stok, st[:, 0:n1 - n0], wv[:, k, n0:n1], writes=[stok])
                eng = ("dve", "pool", "act")[i % 3]
                dst = wt[:, k, n0:n1]
                src = st[:, 0:n1 - n0]
                if eng == "act":
                    S.op("act", lambda e, dst=dst, src=src: e.copy(out=dst, in_=src),
                         reads=[stok], writes=[(tokname, i)])
                else:
                    S.op(eng, lambda e, dst=dst, src=src: e.tensor_copy(out=dst, in_=src),
                         reads=[stok], writes=[(tokname, i)])
                i += 1

    def ffn_phase(self, x_in, x_out, w_in, w_out, g_vec, b_vec):
        S = self.S
        nc = self.nc
        T = self.T
        TT = 256
        NS = TT // 128
        ntiles = T // TT
        FC = D_FF // 128
        with ExitStack() as es:
            ident = self.make_identity(es)
            win = self.sb(es, "win", [128, 8, 2 * D_FF], BF16)
            wout = self.sb(es, "wout", [128, FC, D_MODEL], BF16)
            xs = [self.sb(es, "x%d" % i, [128, NS, D_MODEL], F32) for i in range(2)]
            xTs = [self.sb(es, "xT%d" % i, [128, 8, TT], BF16) for i in range(2)]
            aT = self.sb(es, "aT", [128, FC, TT], BF16)
            sg = [self.sb(es, "sg%d" % i, [128, TT], F32) for i in range(2)]
            gb = self.sb(es, "gb", [128, D_MODEL], F32)
            bb = self.sb(es, "bb", [128, D_MODEL], F32)
            stats = self.sb(es, "stats", [128, 2, 6], F32)
            mv = self.sb(es, "mv", [128, 2], F32)
            rstd = self.sb(es, "rstd", [128, 1], F32)
            nmr = self.sb(es, "nmr", [128, 1], F32)
            pt = [self.ps(es, "pt%d" % i, [128, 512]) for i in range(2)]
            pg = [self.ps(es, "pg%d" % i, [128, 512]) for i in range(2)]
            pu = [self.ps(es, "pu%d" % i, [128, 512]) for i in range(2)]
            py = [self.ps(es, "py%d" % i, [128, 512]) for i in range(2)]

            S.dma("sp", "gb", gb[:], g_vec.partition_broadcast(128), writes=["gb"])
            S.dma("sp", "bb", bb[:], b_vec.partition_broadcast(128), writes=["bb"])
            stage = [xs[0][:, i, :] for i in range(NS)] + [xs[1][:, i, :] for i in range(NS)]
            stage_w = [_View(v) for v in stage]
            self.load_weight_bf16(w_in, win, D_MODEL, 2 * D_FF, stage_w, "win")
            self.load_weight_bf16(w_out, wout, D_FF, D_MODEL, stage_w, "wout")
            S.flush()
            xin_v = x_in.rearrange("(n s p) d -> n p s d", p=128, s=NS)
            xout_v = x_out.rearrange("(n s p) d -> n p s d", p=128, s=NS)

            def load_x(i):
                sl = i % 2
                S.dma("sp", "xld%d" % sl, xs[sl][:], xin_v[i], writes=["x%d" % sl])

            load_x(0)
            tcount = [0]

            def do_transposes(i):
                sl = i % 2
                xt = xs[sl]
                xtok = "x%d" % sl
                xTt = xTs[sl]
                for k in range(8):
                    p = pt[tcount[0] % 2]
                    ptok = "pt%d" % (tcount[0] % 2)
                    tcount[0] += 1
                    for s in range(NS):
                        S.op("pe", lambda e, p=p, s=s, k=k, xt=xt: e.transpose(
                            out=p[:, s * 128:(s + 1) * 128], in_=xt[:, s, k * 128:(k + 1) * 128],
                            identity=ident[:]), reads=[xtok, "ident"], writes=[ptok])
                    if k % 2 == 0:
                        S.op("dve", lambda e, p=p, k=k, xTt=xTt: e.tensor_copy(out=xTt[:, k, :], in_=p[:, 0:TT]),
                             reads=[ptok], writes=[("xT", sl, k)])
                    else:
                        S.op("act", lambda e, p=p, k=k, xTt=xTt: e.copy(out=xTt[:, k, :], in_=p[:, 0:TT]),
                             reads=[ptok], writes=[("xT", sl, k)])

            do_transposes(0)
            for i in range(ntiles):
                sl = i % 2
                xt = xs[sl]
                xtok = "x%d" % sl
                xT = xTs[sl]
                if i + 1 < ntiles:
                    load_x(i + 1)
                for j in range(FC):
                    b = j % 2
                    g_ps, u_ps = pg[b], pu[b]
                    for k in range(8):
                        S.op("pe", lambda e, g_ps=g_ps, k=k, j=j, xT=xT: e.matmul(
                            g_ps[:, 0:TT], lhsT=win[:, k, j * 128:(j + 1) * 128], rhs=xT[:, k, :],
                            start=(k == 0), stop=(k == 7)), reads=["win", ("xT", sl, k)], writes=["pg%d" % b])
                    for k in range(8):
                        S.op("pe", lambda e, u_ps=u_ps, k=k, j=j, xT=xT: e.matmul(
                            u_ps[:, 0:TT], lhsT=win[:, k, D_FF + j * 128:D_FF + (j + 1) * 128],
                            rhs=xT[:, k, :], start=(k == 0), stop=(k == 7)),
                            reads=["win", ("xT", sl, k)], writes=["pu%d" % b])
                    S.op("act", lambda e, g_ps=g_ps, b=b: e.activation(
                        out=sg[b][:], in_=g_ps[:, 0:TT], func=AF.Silu),
                        reads=["pg%d" % b], writes=["sg%d" % b])
                    S.op("dve", lambda e, u_ps=u_ps, b=b, j=j: e.tensor_tensor(
                        out=aT[:, j, :], in0=sg[b][:], in1=u_ps[:, 0:TT], op=ALU.mult),
                        reads=["sg%d" % b, "pu%d" % b], writes=[("aT", j)])
                if i + 1 < ntiles:
                    do_transposes(i + 1)
                for s in range(NS):
                    for c in range(2):
                        y_ps = py[c]
                        for j in range(FC):
                            S.op("pe", lambda e, y_ps=y_ps, j=j, s=s, c=c: e.matmul(
                                y_ps[:], lhsT=aT[:, j, s * 128:(s + 1) * 128],
                                rhs=wout[:, j, c * 512:(c + 1) * 512],
                                start=(j == 0), stop=(j == FC - 1)),
                                reads=[("aT", j), "wout"], writes=["py%d" % c])
                        S.op("dve", lambda e, y_ps=y_ps, s=s, c=c, xt=xt: e.scalar_tensor_tensor(
                            out=xt[:, s, c * 512:(c + 1) * 512], in0=xt[:, s, c * 512:(c + 1) * 512],
                            scalar=2.0 * ALPHA, in1=y_ps[:], op0=ALU.mult, op1=ALU.add),
                            reads=["py%d" % c, xtok], writes=[xtok])
                    if getattr(self, "dbg", None) and i == 0 and s == 0:
                        S.dma("sp", "dbg", self.dbg["z"], xt[:, 0, :], reads=[xtok], writes=["dbgz"])
                        pass
                    self.layer_norm_rows(xt[:, s, :], xtok, gb, bb, stats, mv, rstd, nmr, 4.0 * LN_EPS)
                S.dma("sp", "xst%d" % sl, xout_v[i], xs[sl][:], reads=[xtok],
                      writes=[("xout", i)])
            S.flush()

    def layer_norm_rows(self, z, ztok, gb, bb, stats, mv, rstd, nmr, eps, pfx=""):
        S = self.S
        for c in range(2):
            S.op("dve", lambda e, c=c: e.bn_stats(out=stats[:, c, :], in_=z[:, c * 512:(c + 1) * 512]),
                 reads=[ztok], writes=["stats"])
        S.op("dve", lambda e: e.bn_aggr(out=mv[:], in_=stats[:]), reads=["stats"], writes=["mv"])
        S.op("act", lambda e: e.activation(out=rstd[:], in_=mv[:, 1:2], func=AF.Sqrt, bias=float(eps), scale=1.0),
             reads=["mv"], writes=["rstd"])
        S.op("dve", lambda e: e.reciprocal(out=rstd[:], in_=rstd[:]), reads=["rstd"], writes=["rstd"])
        S.op("dve", lambda e: e.scalar_tensor_tensor(out=nmr[:], in0=mv[:, 0:1], scalar=-1.0, in1=rstd[:],
                                                     op0=ALU.mult, op1=ALU.mult),
             reads=["mv", "rstd"], writes=["nmr"])
        S.op("act", lambda e: e.activation(out=z, in_=z, func=AF.Identity, bias=nmr[:], scale=rstd[:]),
             reads=[ztok, "rstd", "nmr"], writes=[ztok])
        S.op("pool", lambda e: e.tensor_tensor(out=z, in0=z, in1=gb[:], op=ALU.mult),
             reads=[ztok, "gb"], writes=[ztok])
        S.op("pool", lambda e: e.tensor_tensor(out=z, in0=z, in1=bb[:], op=ALU.add),
             reads=[ztok, "bb"], writes=[ztok])


class _View:
    def __init__(self, ap):
        self.ap = ap
        self.shape = ap.shape

    def __getitem__(self, k):
        return self.ap[k]


EV_COLS = 3632
OD_COLS = 4608
LOG_GAMMA = [float(np.log1p(-(2.0 ** (-5.0 - h)))) for h in range(4)]
TINY = 1e-30


def _op(S, eng, fn, r, w):
    S.op(eng, fn, reads=r, writes=w)


class MixerMixin:
    def alloc_scratch(self):
        T = self.T
        t = self.dram_tmp
        self.QT = t("s_QT", [8, 128, T], BF16)
        self.KT = [t("s_KTf", [8, 128, T], BF16), t("s_KTb", [8, 128, T], BF16)]
        self.Kt = [t("s_Kf", [T, 8, 128], BF16), t("s_Kb", [T, 8, 128], BF16)]
        self.Vt = t("s_V", [T, 8, 128], BF16)
        self.Gt = [t("s_Gf", [T, 8, 128], F32), t("s_Gb", [T, 8, 128], F32)]
        self.GATE = t("s_GATE", [T, 1024], F32)
        self.OD = [t("s_OF", [T, 1024], F32), t("s_OB", [T, 1024], F32)]

    def proj_phase(self, x_in, even, P):
        S = self.S
        T = self.T
        NT = T // 128
        ncol = EV_COLS if even else OD_COLS
        BND = self.BND // 128
        with ExitStack() as es:
            ident = self.make_identity(es)
            identb = self.sb(es, "identb", [128, 128], BF16)
            win = self.sb(es, "pwin", [128, 8, ncol], BF16)
            xs = [self.sb(es, "px%d" % i, [128, 1024], F32) for i in range(2)]
            xT = self.sb(es, "pxT", [128, 8, 128], BF16)
            Y = [self.sb(es, "Y%d" % i, [128, ncol], F32) for i in range(2)]
            osets = []
            for si in range(1):
                osets.append((self.sb(es, "Qk", [128, 8, 128], BF16), self.sb(es, "Kf", [128, 8, 128], BF16),
                              self.sb(es, "Kb", [128, 8, 128], BF16), self.sb(es, "Vk", [128, 8, 128], BF16),
                              self.sb(es, "Gf", [128, 8, 128], F32), self.sb(es, "Gb", [128, 8, 128], F32),
                              self.sb(es, "GA", [128, 1024], F32), self.sb(es, "TTs", [128, 3, 4, 128], BF16)))
            nw = self.sb(es, "nw", [128, 1024], F32)
            tmp1 = self.sb(es, "tmp1", [128, 1024], F32)
            tmp2 = self.sb(es, "tmp2", [128, 1024], F32)
            pp = [self.ps(es, "pp%d" % i, [128, 512]) for i in range(3)]
            ptb = [self.ps(es, "ptb%d" % i, [128, 4, 128], BF16) for i in range(2)]
            psm = self.ps(es, "psm", [128, 512])

            S.op("dve", lambda e: e.tensor_copy(out=identb[:], in_=ident[:]), reads=["ident"], writes=["identb"])
            for oset in osets:
                for tl in (oset[0], oset[1], oset[2], oset[4], oset[5]):
                    S.op("pool", lambda e, tl=tl: e.memset(tl[:], 0.0), writes=[tl.name])
            if even:
                A5 = [self.sb(es, "A5_%d" % i, [128, 5, 1024], BF16) for i in range(3)]
                pcv = [self.ps(es, "pcv%d" % i, [128, 512]) for i in range(2)]
                cw = self.sb(es, "cw", [128, 5, 1024], F32)
                sh = self.sb(es, "sh", [128, 8, 128], BF16)
                shk = self.sb(es, "shk", [128, 4, 128], BF16)
                gbb = self.sb(es, "gbb", [128, 16], F32)
                gt = self.sb(es, "gt", [128, 16], F32)
                E1 = self.sb(es, "E1", [128, 16], F32)
                E2 = self.sb(es, "E2", [128, 16], F32)
                a2w = self.sb(es, "a2w", [16, 2, 256], F32)
                a2b = self.sb(es, "a2b", [1, 2, 256], F32)
                ones1 = self.sb(es, "ones1", [1, 128], F32)
                baT = [self.sb(es, "baT%d" % j, [16, 128], F32) for j in range(2)]
                S.dma("sp", "cst1", cw[:], P["conv_w"].partition_broadcast(128), writes=["cw"])
                S.dma("sp", "cst2", sh[:], self.c_shift, writes=["sh"])
                S.dma("sp", "cst3", gbb[:], P["gate_b"].partition_broadcast(128), writes=["gbb"])
                S.dma("sp", "cst4", a2w[:], P["a2_w"].rearrange("j r c -> r j c"), writes=["a2w"])
                S.dma("sp", "cst5", a2b[:], P["a2_b"].rearrange("(o j) c -> o j c", o=1), writes=["a2b"])
                S.dma("sp", "cst6", nw[:], P["norm_w"].partition_broadcast(128), writes=["nw"])
                S.op("pool", lambda e: e.memset(ones1[:], 1.0), writes=["ones1"])
            else:
                lbl = self.sb(es, "lbl", [128, 2, 1024], F32)
                lb = self.sb(es, "lb", [128, 1024], F32)
                oml = self.sb(es, "oml", [128, 1024], F32)
                rot = self.sb(es, "rot", [128, 2, 128], F32)
                S.dma("sp", "cst1", lbl[:], P["lb_logits"].rearrange("l j c -> l (j c)").partition_broadcast(128),
                      writes=["lbl"])
                S.dma("sp", "cst6", nw[:, 0:512], P["norm_w"].partition_broadcast(128), writes=["nw"])
                if P["layer_idx"] == 0:
                    S.op("pool", lambda e: e.memset(lb[:], 0.0), writes=["lb"])
                else:
                    S.op("act", lambda e: e.activation(out=lbl[:], in_=lbl[:], func=AF.Exp), reads=["lbl"], writes=["lbl"])
                    S.op("dve", lambda e: e.tensor_tensor(out=lb[:], in0=lbl[:, 0, :], in1=lbl[:, 1, :], op=ALU.add),
                         reads=["lbl"], writes=["lb"])
                    S.op("dve", lambda e: e.reciprocal(out=lb[:], in_=lb[:]), reads=["lb"], writes=["lb"])
                    S.op("dve", lambda e: e.tensor_tensor(out=lb[:], in0=lb[:], in1=lbl[:, 1, :], op=ALU.mult),
                         reads=["lb", "lbl"], writes=["lb"])
                S.op("dve", lambda e: e.tensor_scalar(out=oml[:], in0=lb[:], scalar1=-1.0, scalar2=1.0,
                                                      op0=ALU.mult, op1=ALU.add), reads=["lb"], writes=["oml"])
                for oset in osets:
                    Gf, Gb = oset[4], oset[5]
                    for h in range(4):
                        S.op("pool", lambda e, h=h, Gf=Gf: e.memset(Gf[:, 4 + h, :], LOG_GAMMA[h]), reads=[Gf.name], writes=[Gf.name])
                        S.op("pool", lambda e, h=h, Gb=Gb: e.memset(Gb[:, 4 + h, :], LOG_GAMMA[h]), reads=[Gb.name], writes=[Gb.name])
            stage = [(_View(Y[0][:, 0:2048]), ), (_View(Y[1][:, 0:2048]),)]
            self.load_weight_bf16(P["w_in"], win, D_MODEL, ncol, [s[0] for s in stage], "pwin")
            S.flush()
            if even:
                nb = (NT - 1) // BND
                shkb = []
                for b in range(1, nb + 1):
                    tl = self.sb(es, "shkb%d" % b, [128, 4, 128], BF16)
                    S.op("dve", lambda e, tl=tl, b=b: e.tensor_scalar(
                        out=tl[:], in0=sh[:, 4:8, :], scalar1=self.kf[:, b:b + 1], scalar2=None, op0=ALU.mult),
                        reads=["sh", "kf"], writes=[tl.name])
                    shkb.append(tl)

            xin_v = x_in.rearrange("(n p) d -> n p d", p=128)

            def load_x(i):
                S.dma("sp", "pxld%d" % (i % 2), xs[i % 2][:], xin_v[i], writes=["px%d" % (i % 2)])

            def stage1(n):
                xt = xs[n % 2]
                xtok = "px%d" % (n % 2)
                y = Y[n % 2]
                ytok = "Y%d" % (n % 2)
                if n + 1 < NT:
                    load_x(n + 1)
                for k in range(8):
                    p = pp[k % 3]
                    ptok = "pp%d" % (k % 3)
                    S.op("pe", lambda e, p=p, k=k: e.transpose(out=p[:, 0:128], in_=xt[:, k * 128:(k + 1) * 128],
                                                              identity=ident[:]), reads=[xtok, "ident"], writes=[ptok])
                    eng = "dve" if k % 2 == 0 else "act"
                    if eng == "dve":
                        S.op("dve", lambda e, p=p, k=k: e.tensor_copy(out=xT[:, k, :], in_=p[:, 0:128]),
                             reads=[ptok], writes=["pxT"])
                    else:
                        S.op("act", lambda e, p=p, k=k: e.copy(out=xT[:, k, :], in_=p[:, 0:128]),
                             reads=[ptok], writes=["pxT"])
                ci = 0
                for c0 in range(0, ncol, 512):
                    c1 = min(ncol, c0 + 512)
                    p = pp[ci % 3]
                    ptok = "pp%d" % (ci % 3)
                    for k in range(8):
                        S.op("pe", lambda e, p=p, k=k, c0=c0, c1=c1: e.matmul(
                            p[:, 0:c1 - c0], lhsT=xT[:, k, :], rhs=win[:, k, c0:c1], start=(k == 0), stop=(k == 7)),
                            reads=["pxT", "pwin"], writes=[ptok])
                    if ci % 2 == 0:
                        S.op("dve", lambda e, p=p, c0=c0, c1=c1: e.tensor_copy(out=y[:, c0:c1], in_=p[:, 0:c1 - c0]),
                             reads=[ptok], writes=[(ytok, ci)])
                    else:
                        S.op("act", lambda e, p=p, c0=c0, c1=c1: e.copy(out=y[:, c0:c1], in_=p[:, 0:c1 - c0]),
                             reads=[ptok], writes=[(ytok, ci)])
                    ci += 1
                if even:
                    for j in range(5):
                        eng = ("dve", "pool", "dve", "pool", "dve")[j]
                        S.op(eng, lambda e, j=j: e.tensor_tensor(out=A5[n % 3][:, j, :], in0=y[:, 0:1024], in1=cw[:, j, :], op=ALU.mult),
                             reads=[(ytok, 0), (ytok, 1), "cw"], writes=[("A5", n % 3, j)])
                return ci

            def ytoks(m, c0, c1):
                return [("Y%d" % (m % 2), ci) for ci in range(c0 // 512, (c1 - 1) // 512 + 1)]

            def write_out(m):
                Qk, Kf, Kb, Vk, Gf, Gb, GA, TT = osets[m % len(osets)]
                t0 = m * 128
                for ai, (src, dst) in enumerate(((Qk, self.QT), (Kf, self.KT[0]), (Kb, self.KT[1]))):
                    for hg in range(2):
                        pb = ptb[(ai * 2 + hg) % 2]
                        pbt = "ptb%d" % ((ai * 2 + hg) % 2)
                        for h in range(4):
                            S.op("pe", lambda e, pb=pb, h=h, hg=hg, src=src: e.transpose(
                                out=pb[:, h, :], in_=src[:, hg * 4 + h, :], identity=identb[:]),
                                reads=[src.name, "identb"], writes=[pbt])
                        ttok = ("TT", ai, hg)
                        eng = "act" if hg == 0 else "dve"
                        if eng == "act":
                            S.op("act", lambda e, pb=pb, ai=ai: e.copy(out=TT[:, ai, :, :], in_=pb[:]),
                                 reads=[pbt], writes=[("TT", m % 2, ai)])
                        else:
                            S.op("dve", lambda e, pb=pb, ai=ai: e.tensor_copy(out=TT[:, ai, :, :], in_=pb[:]),
                                 reads=[pbt], writes=[("TT", m % 2, ai)])
                        S.dma("sp", "wo_tt%d_%d" % (ai, m % 2), dst[hg * 4:hg * 4 + 4, :, t0:t0 + 128].rearrange("h d t -> d h t"),
                              TT[:, ai, :, :], reads=[("TT", m % 2, ai)], writes=[("dQT", ai, hg, m)])
                S.dma("sp", "wo_kf%d" % (m % 2), self.Kt[0][t0:t0 + 128], Kf[:], reads=[Kf.name], writes=[("dKf", m)])
                S.dma("sp", "wo_kb%d" % (m % 2), self.Kt[1][t0:t0 + 128], Kb[:], reads=[Kb.name], writes=[("dKb", m)])
                S.dma("sp", "wo_v%d" % (m % 2), self.Vt[t0:t0 + 128], Vk[:], reads=[Vk.name], writes=[("dV", m)])
                S.dma("sp", "wo_gf%d" % (m % 2), self.Gt[0][t0:t0 + 128], Gf[:], reads=[Gf.name], writes=[("dGf", m)])
                S.dma("sp", "wo_gb%d" % (m % 2), self.Gt[1][t0:t0 + 128], Gb[:], reads=[Gb.name], writes=[("dGb", m)])
                S.dma("sp", "wo_ga%d" % (m % 2), self.GATE[t0:t0 + 128], GA[:], reads=[GA.name], writes=[("dGA", m)])

            def stage2_even(m):
                Qk, Kf, Kb, Vk, Gf, Gb, GA, TT = osets[m % len(osets)]
                y = Y[m % 2]
                yn = "Y%d" % (m % 2)
                for half in range(2):
                    pc = pcv[half]
                    mms = []
                    for j in range(5):
                        o = j - 2
                        oi = {-2: 0, -1: 1, 1: 2, 2: 3}.get(o)
                        if o == 0:
                            mms.append((identb[:], A5[m % 3], j, "identb", ("A5", m % 3, j)))
                        else:
                            mms.append((sh[:, oi, :], A5[m % 3], j, "sh", ("A5", m % 3, j)))
                        if o < 0 and m > 0:
                            if m % BND == 0:
                                tl = shkb[m // BND - 1]
                                mms.append((tl[:, oi, :], A5[(m - 1) % 3], j, tl.name, ("A5", (m - 1) % 3, j)))
                            else:
                                mms.append((sh[:, 4 + oi, :], A5[(m - 1) % 3], j, "sh", ("A5", (m - 1) % 3, j)))
                        if o > 0 and m < NT - 1:
                            if (m + 1) % BND == 0:
                                tl = shkb[(m + 1) // BND - 1]
                                mms.append((tl[:, oi, :], A5[(m + 1) % 3], j, tl.name, ("A5", (m + 1) % 3, j)))
                            else:
                                mms.append((sh[:, 4 + oi, :], A5[(m + 1) % 3], j, "sh", ("A5", (m + 1) % 3, j)))
                    for qi, (l, r, j, lt, rt) in enumerate(mms):
                        S.op("pe", lambda e, pc=pc, l=l, r=r, j=j, half=half, qi=qi, nq=len(mms): e.matmul(
                            pc[:], lhsT=l, rhs=r[:, j, half * 512:(half + 1) * 512], start=(qi == 0), stop=(qi == nq - 1)),
                            reads=[rt, lt], writes=["pcv%d" % half])
                    S.op("act", lambda e, pc=pc, half=half: e.activation(out=tmp2[:, half * 512:(half + 1) * 512], in_=pc[:], func=AF.Silu),
                         reads=["pcv%d" % half], writes=["tmp2"])
                S.op("dve", lambda e: e.tensor_copy(out=Qk[:, 0:4, :], in_=tmp2[:, 0:512].rearrange("p (h d) -> p h d", h=4)),
                     reads=["tmp2"], writes=[Qk.name])
                S.op("dve", lambda e: e.tensor_tensor(out=gt[:], in0=y[:, 2048:2064], in1=gbb[:], op=ALU.add),
                     reads=ytoks(m, 2048, 2064) + ["gbb"], writes=["gt"])
                S.op("act", lambda e: e.activation(out=E1[:], in_=gt[:], func=AF.Exp), reads=["gt"], writes=["E1"])
                S.op("act", lambda e: e.activation(out=E2[:], in_=gt[:], func=AF.Exp, scale=-1.0), reads=["gt"], writes=["E2"])
                S.op("act", lambda e: e.activation(out=E2[:], in_=E2[:], func=AF.Ln, bias=1.0), reads=["E2"], writes=["E2"])
                for d, (Kd, Gd) in enumerate(((Kf, Gf), (Kb, Gb))):
                    S.op("dve", lambda e, Kd=Kd, d=d: e.scalar_tensor_tensor(
                        out=Kd[:, 0:4, :], in0=tmp2[:, 512:1024].rearrange("p (h d) -> p h d", h=4), scalar=128.0 ** -0.5,
                        in1=E1[:, 8 * d:8 * d + 4].unsqueeze(2).to_broadcast([128, 4, 128]), op0=ALU.mult, op1=ALU.mult),
                        reads=["tmp2", "E1"], writes=[Kd.name])
                    S.op("pool", lambda e, Gd=Gd, d=d: e.tensor_scalar(
                        out=Gd[:, 0:4, :], in0=E2[:, 8 * d + 4:8 * d + 8].unsqueeze(2).to_broadcast([128, 4, 128]),
                        scalar1=-1.0, scalar2=None, op0=ALU.mult), reads=["E2"], writes=[Gd.name])
                S.op("act", lambda e: e.copy(out=Vk[:, 0:4, :], in_=y[:, 1024:1536].rearrange("p (h d) -> p h d", h=4)),
                     reads=ytoks(m, 1024, 1536), writes=[Vk.name])
                S.op("act", lambda e: e.activation(out=GA[:, 0:512], in_=y[:, 1536:2048], func=AF.Sigmoid),
                     reads=ytoks(m, 1536, 2048), writes=[GA.name])
                S.op("dve", lambda e: e.tensor_scalar(
                    out=Qk[:, 4:8, 0:64], in0=y[:, 2064:2320].rearrange("p (h d) -> p h d", h=4), scalar1=64.0 ** -0.5,
                    scalar2=None, op0=ALU.mult), reads=ytoks(m, 2064, 2320), writes=[Qk.name])
                S.op("dve", lambda e: e.tensor_copy(out=Kf[:, 4:8, 0:64], in_=y[:, 2320:2576].rearrange("p (h d) -> p h d", h=4)),
                     reads=ytoks(m, 2320, 2576), writes=[Kf.name])
                S.op("pool", lambda e: e.tensor_copy(out=Kb[:, 4:8, 0:64], in_=y[:, 2320:2576].rearrange("p (h d) -> p h d", h=4)),
                     reads=ytoks(m, 2320, 2576), writes=[Kb.name])
                S.op("act", lambda e: e.copy(out=Vk[:, 4:8, :], in_=y[:, 2576:3088].rearrange("p (h d) -> p h d", h=4)),
                     reads=ytoks(m, 2576, 3088), writes=[Vk.name])
                S.op("act", lambda e: e.activation(out=GA[:, 512:1024], in_=y[:, 3088:3600], func=AF.Silu),
                     reads=ytoks(m, 3088, 3600), writes=[GA.name])
                S.op("dve", lambda e: e.tensor_tensor(out=GA[:], in0=GA[:], in1=nw[:], op=ALU.mult),
                     reads=[GA.name, "nw"], writes=[GA.name])
                for j, Gd in enumerate((Gf, Gb)):
                    S.op("pe", lambda e, j=j: e.transpose(out=psm[0:16, 0:128], in_=y[:, 3600 + 16 * j:3616 + 16 * j],
                                                         identity=ident[:]), reads=ytoks(m, 3600, 3632) + ["ident"], writes=["psm"])
                    S.op("dve", lambda e, j=j: e.tensor_copy(out=baT[j][:], in_=psm[0:16, 0:128]), reads=["psm"],
                         writes=["baT%d" % j])
                    S.op("pe", lambda e, j=j: e.matmul(psm[:, 0:256], lhsT=baT[j][:], rhs=a2w[:, j, :], start=True, stop=False),
                         reads=["baT%d" % j, "a2w"], writes=["psm"])
                    S.op("pe", lambda e, j=j: e.matmul(psm[:, 0:256], lhsT=ones1[:], rhs=a2b[:, j, :], start=False, stop=True),
                         reads=["ones1", "a2b"], writes=["psm"])
                    S.op("act", lambda e: e.activation(out=tmp1[:, 0:256], in_=psm[:, 0:256], func=AF.Exp, scale=-1.0),
                         reads=["psm"], writes=["tmp1"])
                    S.op("act", lambda e: e.activation(out=tmp1[:, 0:256], in_=tmp1[:, 0:256], func=AF.Ln, bias=1.0),
                         reads=["tmp1"], writes=["tmp1"])
                    S.op("dve", lambda e, Gd=Gd: e.tensor_scalar(
                        out=Gd[:, 4:8, 0:64], in0=tmp1[:, 0:256].rearrange("p (h d) -> p h d", h=4), scalar1=-1.0 / 16.0,
                        scalar2=None, op0=ALU.mult), reads=["tmp1"], writes=[Gd.name])
                write_out(m)

            def stage2_odd(m):
                Qk, Kf, Kb, Vk, Gf, Gb, GA, TT = osets[m % len(osets)]
                y = Y[m % 2]
                t0 = m * 128
                S.dma("sp", "rotld", rot[:], self.c_rot[t0:t0 + 128].rearrange("t (a c) -> t a c", a=2), writes=["rot"])
                S.op("act", lambda e: e.activation(out=tmp1[:, 0:512], in_=y[:, 0:512], func=AF.Silu),
                     reads=ytoks(m, 0, 512), writes=["tmp1"])
                S.op("dve", lambda e: e.tensor_scalar(out=Qk[:, 0:4, :], in0=tmp1[:, 0:512].rearrange("p (h d) -> p h d", h=4),
                                                      scalar1=128.0 ** -0.5, scalar2=None, op0=ALU.mult),
                     reads=["tmp1"], writes=[Qk.name])
                S.op("act", lambda e: e.activation(out=tmp2[:], in_=y[:, 512:1536], func=AF.Sigmoid),
                     reads=ytoks(m, 512, 1536), writes=["tmp2"])
                S.op("dve", lambda e: e.tensor_tensor(out=tmp2[:], in0=tmp2[:], in1=oml[:], op=ALU.mult),
                     reads=["tmp2", "oml"], writes=["tmp2"])
                S.op("dve", lambda e: e.tensor_tensor(out=tmp2[:], in0=tmp2[:], in1=lb[:], op=ALU.add),
                     reads=["tmp2", "lb"], writes=["tmp2"])
                for d, (Kd, Gd) in enumerate(((Kf, Gf), (Kb, Gb))):
                    fv = tmp2[:, d * 512:(d + 1) * 512].rearrange("p (h d) -> p h d", h=4)
                    S.op("pool", lambda e, Kd=Kd, fv=fv: e.tensor_scalar(out=Kd[:, 0:4, :], in0=fv, scalar1=-1.0, scalar2=1.0,
                                                                        op0=ALU.mult, op1=ALU.add), reads=["tmp2"], writes=[Kd.name])
                    S.op("dve", lambda e, fv=fv, d=d: e.tensor_scalar(out=tmp1[:, d * 512:(d + 1) * 512].rearrange("p (h d) -> p h d", h=4),
                                                                    in0=fv, scalar1=TINY, scalar2=None, op0=ALU.max),
                         reads=["tmp2"], writes=["tmp1"])
                    S.op("act", lambda e, Gd=Gd, d=d: e.activation(out=Gd[:, 0:4, :],
                                                                  in_=tmp1[:, d * 512:(d + 1) * 512].rearrange("p (h d) -> p h d", h=4),
                                                                  func=AF.Ln), reads=["tmp1"], writes=[Gd.name])
                S.op("act", lambda e: e.copy(out=Vk[:, 0:4, :], in_=y[:, 1536:2048].rearrange("p (h d) -> p h d", h=4)),
                     reads=ytoks(m, 1536, 2048), writes=[Vk.name])
                S.op("act", lambda e: e.activation(out=GA[:, 0:512], in_=y[:, 2048:2560], func=AF.Silu),
                     reads=ytoks(m, 2048, 2560), writes=[GA.name])
                S.op("dve", lambda e: e.tensor_tensor(out=GA[:, 0:512], in0=GA[:, 0:512], in1=nw[:, 0:512], op=ALU.mult),
                     reads=[GA.name, "nw"], writes=[GA.name])
                cosb = rot[:, 0, 0:64].unsqueeze(1).to_broadcast([128, 4, 64])
                sinb = rot[:, 0, 64:128].unsqueeze(1).to_broadcast([128, 4, 64])
                cosk = rot[:, 1, 0:64].unsqueeze(1).to_broadcast([128, 4, 64])
                sink = rot[:, 1, 64:128].unsqueeze(1).to_broadcast([128, 4, 64])
                for (c0, cb, sb_, dsts) in ((2560, cosb, sinb, (Qk,)), (3072, cosk, sink, (Kf, Kb))):
                    xv = y[:, c0:c0 + 512].rearrange("p (h d) -> p h d", h=4)
                    x1, x2 = xv[:, :, 0:64], xv[:, :, 64:128]
                    t1 = tmp1[:, 0:256].rearrange("p (h d) -> p h d", h=4)
                    t2 = tmp1[:, 256:512].rearrange("p (h d) -> p h d", h=4)
                    t3 = tmp1[:, 512:768].rearrange("p (h d) -> p h d", h=4)
                    t4 = tmp1[:, 768:1024].rearrange("p (h d) -> p h d", h=4)
                    yt = ytoks(m, c0, c0 + 512)
                    S.op("dve", lambda e, t1=t1, x1=x1, cb=cb: e.tensor_tensor(out=t1, in0=x1, in1=cb, op=ALU.mult),
                         reads=yt + ["rot"], writes=["tmp1"])
                    S.op("pool", lambda e, t2=t2, x2=x2, sb_=sb_: e.tensor_tensor(out=t2, in0=x2, in1=sb_, op=ALU.mult),
                         reads=yt + ["rot"], writes=["tmp1"])
                    S.op("dve", lambda e, t3=t3, x1=x1, sb_=sb_: e.tensor_tensor(out=t3, in0=x1, in1=sb_, op=ALU.mult),
                         reads=yt + ["rot"], writes=["tmp1"])
                    S.op("pool", lambda e, t4=t4, x2=x2, cb=cb: e.tensor_tensor(out=t4, in0=x2, in1=cb, op=ALU.mult),
                         reads=yt + ["rot"], writes=["tmp1"])
                    for dst in dsts:
                        S.op("dve", lambda e, dst=dst, t1=t1, t2=t2: e.tensor_tensor(out=dst[:, 4:8, 0:64], in0=t1, in1=t2,
                                                                                    op=ALU.subtract), reads=["tmp1"], writes=[dst.name])
                        S.op("pool", lambda e, dst=dst, t3=t3, t4=t4: e.tensor_tensor(out=dst[:, 4:8, 64:128], in0=t3, in1=t4,
                                                                                     op=ALU.add), reads=["tmp1"], writes=[dst.name])
                S.op("act", lambda e: e.copy(out=Vk[:, 4:8, :], in_=y[:, 3584:4096].rearrange("p (h d) -> p h d", h=4)),
                     reads=ytoks(m, 3584, 4096), writes=[Vk.name])
                S.op("act", lambda e: e.activation(out=GA[:, 512:1024], in_=y[:, 4096:4608], func=AF.Silu),
                     reads=ytoks(m, 4096, 4608), writes=[GA.name])
                write_out(m)

            load_x(0)
            if even:
                for n in range(NT + 1):
                    if n < NT:
                        stage1(n)
                    if n >= 1:
                        stage2_even(n - 1)
            else:
                for n in range(NT):
                    stage1(n)
                    stage2_odd(n)
            S.flush()

    def scan_phase(self, even):
        S = self.S
        T = self.T
        L = 64
        BT = 256
        CB = BT // L
        NB = T // BT
        NCH = T // L
        BNDC = self.BND // L
        nden = 4 if even else 0
        with ExitStack() as es:
            tri = self.sb(es, "tri", [64, 8, 64], F32)
            onesb = self.sb(es, "onesb", [64, 1], BF16)
            S.dma("sp", "cst1", tri[:], self.c_tri, writes=["tri"])
            S.op("pool", lambda e: e.memset(onesb[:], 1.0), writes=["onesb"])
            QTs = [self.sb(es, "sQT%d" % i, [128, 8, BT], BF16) for i in range(2)]
            KTs = [self.sb(es, "sKT%d" % i, [128, 8, BT], BF16) for i in range(2)]
            Ks = [self.sb(es, "sK%d" % i, [64, CB, 8, 128], BF16) for i in range(2)]
            Vs = [self.sb(es, "sV%d" % i, [64, CB, 8, 128], BF16) for i in range(2)]
            Gs = [self.sb(es, "sG%d" % i, [64, CB, 8, 128], F32) for i in range(2)]
            Gh = [self.sb(es, "sGh%d" % i, [64, CB, 8, 128], BF16) for i in range(2)]
            Gl = [self.sb(es, "sGl%d" % i, [64, CB, 8, 128], BF16) for i in range(2)]
            trib = self.sb(es, "trib", [64, 8, 64], BF16)
            S.op("dve", lambda e: e.tensor_copy(out=trib[:], in_=tri[:]), reads=["tri"], writes=["trib"])
            Os = [self.sb(es, "sO%d" % i, [64, CB, 8, 128], F32) for i in range(2)]
            EE = [self.sb(es, "EE%d" % i, [128, 8, 2, 64], F32) for i in range(2)]
            Ek = [self.sb(es, "Ek%d" % i, [128, 8, 64], F32) for i in range(2)]
            Es = [self.sb(es, "Es%d" % i, [64, 8, 128], F32) for i in range(2)]
            Q1 = [self.sb(es, "Q1_%d" % i, [128, 8, 64], BF16) for i in range(2)]
            K1 = [self.sb(es, "K1_%d" % i, [128, 8, 64], BF16) for i in range(2)]
            Q2 = [self.sb(es, "Q2_%d" % i, [128, 8, 64], BF16) for i in range(2)]
            K2 = [self.sb(es, "K2_%d" % i, [64, 8, 128], BF16) for i in range(2)]
            PT = [self.sb(es, "PT%d" % i, [64, 8, 64], BF16) for i in range(2)]
            S32 = self.sb(es, "S32", [128, 8, 128], F32)
            Sb = self.sb(es, "Sb", [128, 8, 128], BF16)
            Sn = self.sb(es, "Sn", [128, 4], F32)
            Snb = self.sb(es, "Snb", [128, 4], BF16)
            dn = self.sb(es, "dn", [64, 8], F32)
            p_suf = self.ps(es, "p_suf", [64, 512])
            p_bm = self.ps(es, "p_bm", [128, 4, 2, 64])
            p_st = self.ps(es, "p_st", [64, 8, 64])
            p_o = self.ps(es, "p_o", [64, 8, 128])
            p_u = self.ps(es, "p_u", [128, 8, 128])
            p_sm = self.ps(es, "p_sm", [128, 16])
            S.flush()

            for d in range(2):
                rev = d == 1
                i0 = 4 if rev else 0
                tim = trib[:, i0:i0 + 2, :]
                tsuf = trib[:, i0 + 2, :]
                msk = tri[:, i0 + 3, :]
                mskb = msk.unsqueeze(1).to_broadcast([64, 8, 64])
                deccol = 0 if rev else 63
                for tl, nm in ((S32, "S32"), (Sb, "Sb"), (Sn, "Sn"), (Snb, "Snb")):
                    S.op("pool", lambda e, tl=tl: e.memset(tl[:], 0.0), reads=[nm], writes=[nm])
                blocks = list(range(NB))
                if rev:
                    blocks = blocks[::-1]
                seq = []
                for ib, bi in enumerate(blocks):
                    cs = list(range(CB))
                    if rev:
                        cs = cs[::-1]
                    for c in cs:
                        seq.append((ib, bi, c))

                def load_block(bi, slot, d=d):
                    t0 = bi * BT
                    S.dma("sp", "sld_q%d" % slot, QTs[slot][:], self.QT[:, :, t0:t0 + BT].rearrange("h d t -> d h t"),
                          writes=["sQT%d" % slot])
                    S.dma("sp", "sld_k%d" % slot, KTs[slot][:], self.KT[d][:, :, t0:t0 + BT].rearrange("h d t -> d h t"),
                          writes=["sKT%d" % slot])
                    S.dma("sp", "sld_kt%d" % slot, Ks[slot][:],
                          self.Kt[d][t0:t0 + BT].rearrange("(c p) h e -> p c h e", p=L), writes=["sK%d" % slot])
                    S.dma("sp", "sld_v%d" % slot, Vs[slot][:],
                          self.Vt[t0:t0 + BT].rearrange("(c p) h e -> p c h e", p=L), writes=["sV%d" % slot])
                    S.dma("sp", "sld_g%d" % slot, Gs[slot][:],
                          self.Gt[d][t0:t0 + BT].rearrange("(c p) h e -> p c h e", p=L), writes=["sG%d" % slot])
                    S.op("act", lambda e, slot=slot: e.copy(out=Gh[slot][:], in_=Gs[slot][:]), reads=["sG%d" % slot],
                         writes=["sGh%d" % slot])
                    S.op("pool", lambda e, slot=slot: e.tensor_tensor(out=Gl[slot][:], in0=Gs[slot][:], in1=Gh[slot][:],
                                                                     op=ALU.subtract), reads=["sG%d" % slot, "sGh%d" % slot],
                         writes=["sGl%d" % slot])

                def prep_round(k, r):
                    ib, bi, c = seq[k]
                    slot = ib % 2
                    p = k % 2
                    gh_n, gl_n = "sGh%d" % slot, "sGl%d" % slot
                    gh = Gh[slot][:, c, :, :]
                    gl = Gl[slot][:, c, :, :]
                    S.op("pe", lambda e, gh=gh, r=r: e.matmul(p_suf[:], lhsT=tsuf, rhs=gh[:, 4 * r:4 * r + 4, :],
                                                              start=True, stop=False), reads=[gh_n, "trib"], writes=["p_suf"])
                    S.op("pe", lambda e, gl=gl, r=r: e.matmul(p_suf[:], lhsT=tsuf, rhs=gl[:, 4 * r:4 * r + 4, :],
                                                              start=False, stop=True), reads=[gl_n, "trib"], writes=["p_suf"])
                    S.op("act", lambda e, p=p, r=r: e.activation(out=Es[p][:, 4 * r:4 * r + 4, :],
                                                                in_=p_suf[:].rearrange("q (h e) -> q h e", h=4), func=AF.Exp),
                         reads=["p_suf"], writes=[("Es", p, r)])
                    for h4 in range(4):
                        S.op("pe", lambda e, gh=gh, r=r, h4=h4: e.matmul(p_bm[:, h4, :, :], lhsT=gh[:, 4 * r + h4, :], rhs=tim,
                                                                        start=True, stop=False), reads=[gh_n, "trib"], writes=["p_bm"])
                        S.op("pe", lambda e, gl=gl, r=r, h4=h4: e.matmul(p_bm[:, h4, :, :], lhsT=gl[:, 4 * r + h4, :], rhs=tim,
                                                                        start=False, stop=True), reads=[gl_n, "trib"], writes=["p_bm"])
                    S.op("act", lambda e, p=p, r=r: e.activation(out=EE[p][:, 4 * r:4 * r + 4, :, :], in_=p_bm[:], func=AF.Exp),
                         reads=["p_bm"], writes=[("EE", p, r)])
                    S.op("act", lambda e, p=p, r=r: e.activation(out=Ek[p][:, 4 * r:4 * r + 4, :], in_=p_bm[:, :, 1, :],
                                                                func=AF.Exp, scale=-1.0), reads=["p_bm"], writes=[("Ek", p, r)])

                def prep_tail(k):
                    ib, bi, c = seq[k]
                    slot = ib % 2
                    p = k % 2
                    tsl = slice(c * L, (c + 1) * L)
                    ee = [("EE", p, 0), ("EE", p, 1)]
                    S.op("dve", lambda e, p=p, slot=slot, tsl=tsl: e.tensor_tensor(
                        out=Q1[p][:], in0=QTs[slot][:, :, tsl], in1=EE[p][:, :, 1, :], op=ALU.mult),
                        reads=["sQT%d" % slot] + ee, writes=[("Q1", p)])
                    S.op("pool", lambda e, p=p, slot=slot, tsl=tsl: e.tensor_tensor(
                        out=K1[p][:], in0=KTs[slot][:, :, tsl], in1=Ek[p][:], op=ALU.mult),
                        reads=["sKT%d" % slot, ("Ek", p, 0), ("Ek", p, 1)], writes=[("K1", p)])
                    S.op("pool", lambda e, p=p, slot=slot, tsl=tsl: e.tensor_tensor(
                        out=Q2[p][:], in0=QTs[slot][:, :, tsl], in1=EE[p][:, :, 0, :], op=ALU.mult),
                        reads=["sQT%d" % slot] + ee, writes=[("Q2", p)])
                    S.op("dve", lambda e, p=p, slot=slot, c=c: e.tensor_tensor(
                        out=K2[p][:], in0=Ks[slot][:, c, :, :], in1=Es[p][:], op=ALU.mult),
                        reads=["sK%d" % slot, ("Es", p, 0), ("Es", p, 1)], writes=[("K2", p)])

                def main_a(k):
                    ib, bi, c = seq[k]
                    slot = ib % 2
                    p = k % 2
                    gc = bi * CB + c
                    bidx = None
                    if not rev and gc > 0 and gc % BNDC == 0:
                        bidx = gc // BNDC
                    if rev and (gc + 1) % BNDC == 0 and gc + 1 < NCH:
                        bidx = (gc + 1) // BNDC
                    if bidx is not None:
                        S.op("dve", lambda e, bidx=bidx: e.tensor_scalar(
                            out=S32[:], in0=S32[:], scalar1=self.kf[:, bidx:bidx + 1], scalar2=None, op0=ALU.mult),
                            reads=["S32", "kf"], writes=["S32"])
                        S.op("pool", lambda e: e.tensor_copy(out=Sb[:], in_=S32[:]), reads=["S32"], writes=["Sb"])
                        if nden:
                            S.op("dve", lambda e, bidx=bidx: e.tensor_scalar(
                                out=Sn[:], in0=Sn[:], scalar1=self.kf[:, bidx:bidx + 1], scalar2=None, op0=ALU.mult),
                                reads=["Sn", "kf"], writes=["Sn"])
                            S.op("pool", lambda e: e.tensor_copy(out=Snb[:], in_=Sn[:]), reads=["Sn"], writes=["Snb"])
                    for h in range(8):
                        S.op("pe", lambda e, h=h, p=p: e.matmul(p_st[:, h, :], lhsT=K1[p][:, h, :], rhs=Q1[p][:, h, :],
                                                                start=True, stop=True), reads=[("K1", p), ("Q1", p)], writes=["p_st"])
                    S.op("dve", lambda e, p=p: e.tensor_tensor(out=PT[p][:], in0=p_st[:], in1=mskb, op=ALU.mult),
                         reads=["p_st", "tri"], writes=[("PT", p)])

                def main_b(k):
                    ib, bi, c = seq[k]
                    slot = ib % 2
                    p = k % 2
                    vn = "sV%d" % slot
                    on = "sO%d" % slot
                    for h in range(8):
                        S.op("pe", lambda e, h=h, p=p: e.matmul(p_o[:, h, :], lhsT=Q2[p][:, h, :], rhs=Sb[:, h, :], start=True, stop=False),
                             reads=[("Q2", p), "Sb"], writes=["p_o"])
                        S.op("pe", lambda e, h=h, p=p, slot=slot, c=c: e.matmul(p_o[:, h, :], lhsT=PT[p][:, h, :], rhs=Vs[slot][:, c, h, :],
                                                                               start=False, stop=True), reads=[("PT", p), vn], writes=["p_o"])
                    for h in range(nden):
                        S.op("pe", lambda e, h=h, p=p: e.matmul(p_sm[0:64, h:h + 1], lhsT=Q2[p][:, h, :], rhs=Snb[:, h:h + 1],
                                                                start=True, stop=False), reads=[("Q2", p), "Snb"], writes=["p_smd"])
                        S.op("pe", lambda e, h=h, p=p: e.matmul(p_sm[0:64, h:h + 1], lhsT=PT[p][:, h, :], rhs=onesb[:],
                                                                start=False, stop=True), reads=[("PT", p), "onesb"], writes=["p_smd"])
                    for h in range(8):
                        S.op("pe", lambda e, h=h, p=p, slot=slot, c=c: e.matmul(p_u[:, h, :], lhsT=K2[p][:, h, :], rhs=Vs[slot][:, c, h, :],
                                                                               start=True, stop=True), reads=[("K2", p), vn], writes=["p_u"])
                    for h in range(nden):
                        S.op("pe", lambda e, h=h, p=p: e.matmul(p_sm[:, 8 + h:9 + h], lhsT=K2[p][:, h, :], rhs=onesb[:],
                                                                start=True, stop=True), reads=[("K2", p), "onesb"], writes=["p_smu"])
                    if nden:
                        S.op("dve", lambda e: e.tensor_copy(out=dn[:, 0:4], in_=p_sm[0:64, 0:4]), reads=["p_smd"], writes=["dn"])
                        S.op("dve", lambda e: e.scalar_tensor_tensor(out=dn[:, 4:8], in0=dn[:, 0:4], scalar=-1.0, in1=dn[:, 0:4],
                                                                     op0=ALU.mult, op1=ALU.max), reads=["dn"], writes=["dn"])
                        S.op("dve", lambda e: e.tensor_scalar(out=dn[:, 4:8], in0=dn[:, 4:8], scalar1=1.0, scalar2=None, op0=ALU.max),
                             reads=["dn"], writes=["dn"])
                        S.op("dve", lambda e: e.reciprocal(out=dn[:, 0:4], in_=dn[:, 4:8]), reads=["dn"], writes=["dn"])
                        S.op("dve", lambda e, slot=slot, c=c: e.tensor_tensor(
                            out=Os[slot][:, c, 0:4, :], in0=p_o[:, 0:4, :], in1=dn[:, 0:4].unsqueeze(2).to_broadcast([64, 4, 128]),
                            op=ALU.mult), reads=["p_o", "dn"], writes=[(on, c, 0)])
                    else:
                        S.op("dve", lambda e, slot=slot, c=c: e.tensor_copy(out=Os[slot][:, c, 0:4, :], in_=p_o[:, 0:4, :]),
                             reads=["p_o"], writes=[(on, c, 0)])
                    S.op("act", lambda e, slot=slot, c=c: e.copy(out=Os[slot][:, c, 4:8, :], in_=p_o[:, 4:8, :]),
                         reads=["p_o"], writes=[(on, c, 1)])
                    ee = [("EE", p, 0), ("EE", p, 1)]
                    S.op("dve", lambda e, p=p: e.tensor_tensor(
                        out=S32[:], in0=S32[:], in1=EE[p][:, :, 0, deccol:deccol + 1].to_broadcast([128, 8, 128]), op=ALU.mult),
                        reads=["S32"] + ee, writes=["S32"])
                    S.op("dve", lambda e: e.tensor_tensor(out=S32[:], in0=S32[:], in1=p_u[:], op=ALU.add),
                         reads=["S32", "p_u"], writes=["S32"])
                    S.op("act", lambda e: e.copy(out=Sb[:], in_=S32[:]), reads=["S32"], writes=["Sb"])
                    if nden:
                        S.op("dve", lambda e, p=p: e.tensor_tensor(out=Sn[:], in0=Sn[:], in1=EE[p][:, 0:4, 0, deccol], op=ALU.mult),
                             reads=["Sn"] + ee, writes=["Sn"])
                        S.op("dve", lambda e: e.tensor_tensor(out=Sn[:], in0=Sn[:], in1=p_sm[:, 8:12], op=ALU.add),
                             reads=["Sn", "p_smu"], writes=["Sn"])
                        S.op("act", lambda e: e.copy(out=Snb[:], in_=Sn[:]), reads=["Sn"], writes=["Snb"])
                    last_c = 0 if rev else CB - 1
                    if c == last_c:
                        t0 = bi * BT
                        S.dma("sp", "sst%d" % slot, self.OD[d][t0:t0 + BT].rearrange("(c p) (h e) -> p c h e", p=L, h=8),
                              Os[slot][:], reads=[(on, c_, hf) for c_ in range(CB) for hf in range(2)], writes=[("dO", d, bi)])

                load_block(blocks[0], 0)
                if NB > 1:
                    load_block(blocks[1], 1)
                prep_round(0, 0)
                prep_round(0, 1)
                prep_tail(0)
                for k in range(len(seq)):
                    nxt = k + 1 < len(seq)
                    if nxt:
                        prep_round(k + 1, 0)
                    main_a(k)
                    if nxt:
                        prep_round(k + 1, 1)
                    main_b(k)
                    ib_k = seq[k][0]
                    if (k + 1) % CB == 0 and ib_k + 2 < NB:
                        load_block(blocks[ib_k + 2], ib_k % 2)
                    if nxt:
                        prep_tail(k + 1)
                S.flush()

    def epi_phase(self, x_in, x_out, even, w_out, g_vec, b_vec):
        S = self.S
        T = self.T
        NT = T // 128
        with ExitStack() as es:
            ident = self.make_identity(es)
            wo = self.sb(es, "ewo", [128, 8, 1024], BF16)
            xs = [self.sb(es, "ex%d" % i, [128, 1024], F32) for i in range(2)]
            of = [self.sb(es, "eof%d" % i, [128, 1024], F32) for i in range(2)]
            ob = [self.sb(es, "eob%d" % i, [128, 1024], F32) for i in range(2)]
            ga = [self.sb(es, "ega%d" % i, [128, 1024], F32) for i in range(2)]
            sq = self.sb(es, "esq", [128, 1024], F32)
            onT = self.sb(es, "eonT", [128, 8, 128], BF16)
            s1 = self.sb(es, "es1", [128, 8], F32)
            s2 = self.sb(es, "es2", [128, 8], F32)
            cfl = self.sb(es, "ecfl", [128, 8], F32)
            gb = self.sb(es, "egb", [128, 1024], F32)
            bb = self.sb(es, "ebb", [128, 1024], F32)
            stats = self.sb(es, "estats", [128, 2, 6], F32)
            mv = self.sb(es, "emv", [128, 2], F32)
            rstd = self.sb(es, "erstd", [128, 1], F32)
            nmr = self.sb(es, "enmr", [128, 1], F32)
            pt = [self.ps(es, "ept%d" % i, [128, 512]) for i in range(2)]
            py = [self.ps(es, "epy%d" % i, [128, 512]) for i in range(2)]
            S.dma("sp", "cst1", gb[:], g_vec.partition_broadcast(128), writes=["gb"])
            S.dma("sp", "cst2", bb[:], b_vec.partition_broadcast(128), writes=["bb"])
            lnh = (0, 4) if even else (4, 8)
            S.op("pool", lambda e: e.memset(cfl[:], 0.0), writes=["ecfl"])
            S.op("pool", lambda e: e.memset(cfl[:, lnh[0]:lnh[1]], 1.0 / 128.0), reads=["ecfl"], writes=["ecfl"])
            self.load_weight_bf16(w_out, wo, 1024, 1024, [_View(of[0][:]), _View(of[1][:])], "ewo")
            S.flush()
            xin_v = x_in.rearrange("(n p) d -> n p d", p=128)
            xout_v = x_out.rearrange("(n p) d -> n p d", p=128)

            def load(i):
                sl = i % 2
                S.dma("sp", "eld_x%d" % sl, xs[sl][:], xin_v[i], writes=["ex%d" % sl])
                S.dma("sp", "eld_f%d" % sl, of[sl][:], self.OD[0][i * 128:(i + 1) * 128], reads=[("dO", 0, i)], writes=["eof%d" % sl])
                S.dma("sp", "eld_b%d" % sl, ob[sl][:], self.OD[1][i * 128:(i + 1) * 128], reads=[("dO", 1, i)], writes=["eob%d" % sl])
                S.dma("sp", "eld_g%d" % sl, ga[sl][:], self.GATE[i * 128:(i + 1) * 128], reads=[("dGA", i)], writes=["ega%d" % sl])

            load(0)
            for i in range(NT):
                sl = i % 2
                if i + 1 < NT:
                    load(i + 1)
                o = of[sl]
                on_ = "eof%d" % sl
                o3 = o[:].rearrange("p (h e) -> p h e", h=8)
                S.op("dve", lambda e, o=o, sl=sl: e.tensor_tensor(out=o[:], in0=o[:], in1=ob[sl][:], op=ALU.add),
                     reads=[on_, "eob%d" % sl], writes=[on_])
                S.op("act", lambda e, o=o: e.activation(out=sq[:], in_=o[:], func=AF.Square), reads=[on_], writes=["esq"])
                S.op("dve", lambda e, o3=o3: e.tensor_reduce(out=s1[:], in_=o3, axis=AX.X, op=ALU.add), reads=[on_], writes=["es1"])
                S.op("dve", lambda e: e.tensor_reduce(out=s2[:], in_=sq[:].rearrange("p (h e) -> p h e", h=8), axis=AX.X, op=ALU.add),
                     reads=["esq"], writes=["es2"])
                S.op("dve", lambda e: e.tensor_tensor(out=s1[:], in0=s1[:], in1=cfl[:], op=ALU.mult), reads=["es1", "ecfl"], writes=["es1"])
                S.op("dve", lambda e: e.tensor_tensor(out=sq[:, 0:8], in0=s1[:], in1=s1[:], op=ALU.mult), reads=["es1"], writes=["esq"])
                S.op("dve", lambda e: e.scalar_tensor_tensor(out=s2[:], in0=s2[:], scalar=1.0 / 128.0, in1=sq[:, 0:8],
                                                             op0=ALU.mult, op1=ALU.subtract), reads=["es2", "esq"], writes=["es2"])
                S.op("act", lambda e: e.activation(out=s2[:], in_=s2[:], func=AF.Sqrt, bias=float(LN_EPS), scale=1.0),
                     reads=["es2"], writes=["es2"])
                S.op("dve", lambda e: e.reciprocal(out=s2[:], in_=s2[:]), reads=["es2"], writes=["es2"])
                S.op("dve", lambda e, o3=o3: e.tensor_tensor(out=o3, in0=o3, in1=s1[:].unsqueeze(2).to_broadcast([128, 8, 128]),
                                                            op=ALU.subtract), reads=[on_, "es1"], writes=[on_])
                S.op("pool", lambda e, o3=o3: e.tensor_tensor(out=o3, in0=o3, in1=s2[:].unsqueeze(2).to_broadcast([128, 8, 128]),
                                                             op=ALU.mult), reads=[on_, "es2"], writes=[on_])
                S.op("dve", lambda e, o=o, sl=sl: e.tensor_tensor(out=o[:], in0=o[:], in1=ga[sl][:], op=ALU.mult),
                     reads=[on_, "ega%d" % sl], writes=[on_])
                for half in range(2):
                    p = pt[half]
                    for k4 in range(4):
                        k = half * 4 + k4
                        S.op("pe", lambda e, p=p, k=k, k4=k4, o=o: e.transpose(out=p[:, k4 * 128:(k4 + 1) * 128],
                                                                             in_=o[:, k * 128:(k + 1) * 128], identity=ident[:]),
                             reads=[on_, "ident"], writes=["ept%d" % half])
                    if half == 0:
                        S.op("dve", lambda e, p=p: e.tensor_copy(out=onT[:, 0:4, :], in_=p[:].rearrange("p (k t) -> p k t", k=4)),
                             reads=["ept0"], writes=[("eonT", 0)])
                    else:
                        S.op("act", lambda e, p=p: e.copy(out=onT[:, 4:8, :], in_=p[:].rearrange("p (k t) -> p k t", k=4)),
                             reads=["ept1"], writes=[("eonT", 1)])
                xt = xs[sl]
                xtok = "ex%d" % sl
                for c in range(2):
                    for k in range(8):
                        S.op("pe", lambda e, c=c, k=k: e.matmul(py[c][:], lhsT=onT[:, k, :], rhs=wo[:, k, c * 512:(c + 1) * 512],
                                                                start=(k == 0), stop=(k == 7)),
                             reads=[("eonT", 0), ("eonT", 1), "ewo"], writes=["epy%d" % c])
                    S.op("dve", lambda e, c=c, xt=xt: e.scalar_tensor_tensor(
                        out=xt[:, c * 512:(c + 1) * 512], in0=xt[:, c * 512:(c + 1) * 512], scalar=ALPHA, in1=py[c][:],
                        op0=ALU.mult, op1=ALU.add), reads=["epy%d" % c, xtok], writes=[xtok])
                self.layer_norm_rows(xt[:], xtok, gb, bb, stats, mv, rstd, nmr, LN_EPS, pfx="e")
                S.dma("sp", "est%d" % sl, xout_v[i], xt[:], reads=[xtok], writes=[("xout", i)])
            S.flush()


class Builder(Builder_, MixerMixin):
    def setup_globals(self):
        nc = self.nc
        self.c_ident = self.dram_in("c_ident", [128, 128])
        self.c_shift = nc.dram_tensor("c_shift", [128, 8, 128], BF16, kind="ExternalInput").ap()
        self.c_tri = self.dram_in("c_tri", [64, 8, 64])
        self.c_rot = self.dram_in("c_rot", [self.T, 256])
        self.c_flags = self.dram_in("c_flags", [128, 4])
        self.kf = self.sb(self.es, "kf", [128, 4], F32)
        self.S.dma("sp", "kf", self.kf[:], self.c_flags, writes=["kf"])
        self.S.flush()
        self.alloc_scratch()


def host_consts():
    import ml_dtypes
    c = {}
    c["c_ident"] = np.eye(128, dtype=np.float32)
    sh = np.zeros((128, 8, 128), np.float32)
    u = np.arange(128)[:, None]
    t = np.arange(128)[None, :]
    for oi, o in enumerate((-2, -1, 1, 2)):
        sh[:, oi, :] = (u == t + o)
        if o < 0:
            sh[:, 4 + oi, :] = (u == 128 + t + o)
        else:
            sh[:, 4 + oi, :] = (u == t + o - 128)
    c["c_shift"] = sh.astype(ml_dtypes.bfloat16)
    tri = np.zeros((64, 8, 64), np.float32)
    u = np.arange(64)[:, None]
    t = np.arange(64)[None, :]
    tri[:, 0, :] = (u <= t)
    tri[:, 1, :] = ((u >= 32) & (u <= t)) * 1.0 - ((u > t) & (u <= 31)) * 1.0
    tri[:, 2, :] = (u > t)
    tri[:, 3, :] = (u <= t)
    tri[:, 4, :] = (u >= t)
    tri[:, 5, :] = ((u >= t) & (u <= 31)) * 1.0 - ((u >= 32) & (u < t)) * 1.0
    tri[:, 6, :] = (u < t)
    tri[:, 7, :] = (u >= t)
    c["c_tri"] = tri
    return c


def rot_table(T, seq_len):
    pos = (np.arange(T) % seq_len).astype(np.float32)
    inv = (1.0 / (10000.0 ** (np.arange(0, 128, 2, dtype=np.float32) / np.float32(128)))).astype(np.float32)
    ang = (pos[:, None] * inv[None, :]).astype(np.float32)
    cos, sin = np.cos(ang).astype(np.float32), np.sin(ang).astype(np.float32)
    sc = np.float32(128.0 ** -0.5)
    return np.concatenate([cos, sin, cos * sc, sin * sc], axis=1).astype(np.float32)


WNAMES = ["ffn1_w_in", "ffn1_w_out", "ffn2_w_in", "ffn2_w_out", "ln_g", "ln_b", "ev_w_in", "ev_gate_b", "ev_conv_w",
          "ev_gla_a2_w", "ev_gla_a2_b", "ev_norm_w", "ev_w_out", "od_w_in", "od_lb_logits", "od_norm_w", "od_w_out"]


def build_program(T, bnd, shapes, depth=DEPTH):
    B = Builder(T, bnd, depth)
    B.BND = bnd
    B.setup_globals()
    W = {n: B.dram_in(n, shapes[n]) for n in WNAMES}
    x = B.dram_in("x", [T, 1024])
    y = B.dram_out("y", [T, 1024])
    xa = B.dram_tmp("s_xa", [T, 1024])
    xb = B.dram_tmp("s_xb", [T, 1024])
    cur = x
    for l in range(depth):
        j = l // 2
        B.ffn_phase(cur, xa, W["ffn1_w_in"][l], W["ffn1_w_out"][l], W["ln_g"][l, 0], W["ln_b"][l, 0])
        if l % 2 == 0:
            P = dict(w_in=W["ev_w_in"][j], gate_b=W["ev_gate_b"][j], conv_w=W["ev_conv_w"][j], a2_w=W["ev_gla_a2_w"][j],
                     a2_b=W["ev_gla_a2_b"][j], norm_w=W["ev_norm_w"][j])
            wo = W["ev_w_out"][j]
        else:
            P = dict(w_in=W["od_w_in"][j], lb_logits=W["od_lb_logits"], norm_w=W["od_norm_w"][j], layer_idx=j)
            wo = W["od_w_out"][j]
        B.proj_phase(xa, l % 2 == 0, P)
        B.scan_phase(l % 2 == 0)
        B.epi_phase(xa, xb, l % 2 == 0, wo, W["ln_g"][l, 1], W["ln_b"][l, 1])
        dst = y if l == depth - 1 else xa
        B.ffn_phase(xb, dst, W["ffn2_w_in"][l], W["ffn2_w_out"][l], W["ln_g"][l, 2], W["ln_b"][l, 2])
        cur = xa
    B.es.close()
    return B


def kernel(**inputs):
    xp = np.asarray(inputs["x_prompt"], np.float32)
    xs = np.asarray(inputs["x_sample"], np.float32)
    T = 16384
    BND = 4096
    weights = {n: np.ascontiguousarray(np.asarray(inputs[n], np.float32)) for n in WNAMES}
    shapes = {n: weights[n].shape for n in WNAMES}
    B = build_program(T, BND, shapes)
    consts = host_consts()
    rot_p = rot_table(T, 16384)
    rot_s = rot_table(T, 4096)
    fl_p = np.ones((128, 4), np.float32)
    fl_s = np.zeros((128, 4), np.float32)
    xcore = [xp[0], xp[1], xs[0:4].reshape(T, 1024), xs[4:8].reshape(T, 1024)]
    maps = []
    for c in range(N_CORES):
        src = c if c < 4 else 2 + (c % 2)
        m = dict(weights)
        m.update(consts)
        m["x"] = np.ascontiguousarray(xcore[src])
        m["c_rot"] = rot_p if src < 2 else rot_s
        m["c_flags"] = fl_p if src < 2 else fl_s
        maps.append(m)
    res = run_bass_kernel_spmd(B.nc, maps, core_ids=list(range(N_CORES)))
    r = res.results
    y_prompt = np.stack([np.asarray(r[0]["y"]), np.asarray(r[1]["y"])], 0).astype(np.float32)
    y_sample = np.concatenate([np.asarray(r[2]["y"]).reshape(4, 4096, 1024),
                               np.asarray(r[3]["y"]).reshape(4, 4096, 1024)], 0).astype(np.float32)
    return (y_prompt, y_sample)
```

```python
import numpy as np
from contextlib import ExitStack
import concourse.bass as bass
import concourse.mybir as mybir
from concourse.bass_utils import run_bass_kernel_spmd

F32 = mybir.dt.float32
BF16 = mybir.dt.bfloat16
AF = mybir.ActivationFunctionType
ALU = mybir.AluOpType
AX = mybir.AxisListType

D_MODEL = 1024
DEPTH = 4
D_FF = 2816
LN_EPS = 1e-5
ALPHA = (2 * DEPTH) ** 0.25
N_CORES = 8


class _Op:
    __slots__ = ("eng", "fn", "deps", "is_dma", "key", "cnt", "signal", "sigval", "dma_all")

    def __init__(self, eng, fn, is_dma=False, key=None):
        self.eng = eng
        self.fn = fn
        self.deps = []
        self.is_dma = is_dma
        self.key = key
        self.cnt = 0
        self.signal = False
        self.sigval = 0
        self.dma_all = ()


class Sched:
    CENG = ("pe", "act", "dve", "pool")
    ALLENG = ("pe", "act", "dve", "pool", "sp")

    def __init__(self, nc, es):
        self.nc = nc
        self.es = es
        self.sem = {e: es.enter_context(nc.semaphore("sem_" + e)) for e in self.CENG}
        self.sig_cnt = {e: 0 for e in self.CENG}
        self.dma_sem = {}
        self.dma_cnt = {}
        self.waited = {e: {} for e in self.ALLENG}
        self.tok = {}
        self.ops = []
        self.last = {e: None for e in self.ALLENG}
        self.n_inst = 0

    STRICT = True

    def _need(self, x, w):
        return not (w.eng == "pe" and x.eng == "pe")

    def _add(self, op, reads, writes):
        deps = op.deps
        for t in reads:
            st = self.tok.get(t)
            if st is None:
                st = self.tok[t] = [None, []]
            w = st[0]
            if w is not None and self._need(op, w):
                deps.append(w)
            rl = st[1]
            if not op.is_dma:
                for i_, r_ in enumerate(rl):
                    if r_.eng == op.eng and not r_.is_dma:
                        rl[i_] = op
                        break
                else:
                    rl.append(op)
            else:
                rl.append(op)
        for t in writes:
            st = self.tok.get(t)
            if st is None:
                st = self.tok[t] = [None, []]
            w = st[0]
            if w is not None and self._need(op, w):
                deps.append(w)
            for r in st[1]:
                if r is not op and self._need(op, r):
                    deps.append(r)
            st[0] = op
            st[1] = []
        for d in deps:
            d.signal = True
        self.ops.append(op)
        self.last[op.eng] = op

    def op(self, eng, fn, reads=(), writes=()):
        o = _Op(eng, fn)
        self._add(o, reads, writes)
        return o

    def dma(self, eng, key, out, in_, reads=(), writes=()):
        if key not in self.dma_sem:
            self.dma_sem[key] = self.es.enter_context(
                self.nc.semaphore("dsem%d" % len(self.dma_sem)))
            self.dma_cnt[key] = 0
        o = _Op(eng, lambda e: e.dma_start(out=out, in_=in_), is_dma=True, key=key)
        self._add(o, reads, writes)
        self.dma_cnt[key] += 16
        o.cnt = self.dma_cnt[key]
        o.signal = True
        return o

    def barrier(self):
        lasts = [self.last[e] for e in self.ALLENG if self.last[e] is not None]
        dma_all = [(k, c) for k, c in self.dma_cnt.items()]
        for e in self.ALLENG:
            o = _Op(e, None)
            for l in lasts:
                if l.eng != e or l.is_dma:
                    if not l.is_dma:
                        o.deps.append(l)
                        l.signal = True
            o.dma_all = dma_all
            self.ops.append(o)
        self.tok = {}

    def flush(self):
        self.barrier()
        ops = self.ops
        self.ops = []
        for o in ops:
            if o.signal and not o.is_dma:
                self.sig_cnt[o.eng] += 1
                o.sigval = self.sig_cnt[o.eng]
        per = {e: [o for o in ops if o.eng == e] for e in self.ALLENG}
        nc = self.nc

        def emit(eng_name, engine):
            waited = self.waited[eng_name]
            for o in per[eng_name]:
                need = {}
                for d in o.deps:
                    if d.is_dma:
                        k = ("d", d.key)
                        v = d.cnt
                    else:
                        k = ("c", d.eng)
                        v = d.sigval
                    if v > need.get(k, 0):
                        need[k] = v
                for (k, c) in o.dma_all:
                    kk = ("d", k)
                    if c > need.get(kk, 0):
                        need[kk] = c
                for k, v in need.items():
                    if v > waited.get(k, 0):
                        waited[k] = v
                        s = self.dma_sem[k[1]] if k[0] == "d" else self.sem[k[1]]
                        engine.wait_ge(s, v)
                        self.n_inst += 1
                if o.fn is None:
                    continue
                ins = o.fn(engine)
                self.n_inst += 1
                if o.is_dma:
                    ins.then_inc(self.dma_sem[o.key], 16)
                elif o.signal:
                    ins.then_inc(self.sem[o.eng], 1)

        with nc.Block() as blk:
            @blk.tensor
            def _(e):
                emit("pe", e)

            @blk.scalar
            def _(e):
                emit("act", e)

            @blk.vector
            def _(e):
                emit("dve", e)

            @blk.gpsimd
            def _(e):
                emit("pool", e)

            @blk.sync
            def _(e):
                emit("sp", e)
        self.last = {e: None for e in self.ALLENG}


class Builder_:
    def __init__(self, T, seq_len, depth=DEPTH):
        self.T = T
        self.seq_len = seq_len
        self.depth = depth
        self.nc = bass.Bass("TRN2", target_bir_lowering=False)
        self.es = ExitStack()
        self.S = Sched(self.nc, self.es)
        self.uid = 0

    def dram_in(self, name, shape):
        return self.nc.dram_tensor(name, list(shape), F32, kind="ExternalInput").ap()

    def dram_out(self, name, shape):
        return self.nc.dram_tensor(name, list(shape), F32, kind="ExternalOutput").ap()

    DEBUG = False

    def dram_tmp(self, name, shape, dtype=F32):
        if self.DEBUG:
            return self.nc.dram_tensor(name, list(shape), dtype, kind="ExternalOutput").ap()
        return self.nc.dram_tensor(name, list(shape), dtype).ap()

    def sb(self, es, name, shape, dtype):
        self.uid += 1
        return es.enter_context(self.nc.sbuf_tensor("%s_%d" % (name, self.uid), list(shape), dtype))

    def ps(self, es, name, shape, dtype=F32):
        self.uid += 1
        return es.enter_context(self.nc.psum_tensor("%s_%d" % (name, self.uid), list(shape), dtype))

    def make_identity(self, es):
        S = self.S
        ident = self.sb(es, "ident", [128, 128], F32)
        S.dma("sp", "ident", ident[:], self.c_ident, writes=["ident"])
        return ident

    def load_weight_bf16(self, w_dram, wt, K, N, stage_tiles, tokname, scale=None):
        S = self.S
        SW = stage_tiles[0].shape[-1]
        kc = K // 128
        wv = w_dram.rearrange("(k p) n -> p k n", p=128)
        i = 0
        for k in range(kc):
            for n0 in range(0, N, SW):
                n1 = min(N, n0 + SW)
                si = i % len(stage_tiles)
                st = stage_tiles[si]
                stok = "wstage%d" % si
                S.dma("sp", stok, st[:, 0:n1 - n0], wv[:, k, n0:n1], writes=[stok])
                eng = ("dve", "pool", "act")[i % 3]
                dst = wt[:, k, n0:n1]
                src = st[:, 0:n1 - n0]
                if eng == "act":
                    S.op("act", lambda e, dst=dst, src=src: e.copy(out=dst, in_=src),
                         reads=[stok], writes=[(tokname, i)])
                else:
                    S.op(eng, lambda e, dst=dst, src=src: e.tensor_copy(out=dst, in_=src),
                         reads=[stok], writes=[(tokname, i)])
                i += 1

    def ffn_phase(self, x_in, x_out, w_in, w_out, g_vec, b_vec):
        S = self.S
        nc = self.nc
        T = self.T
        TT = 256
        NS = TT // 128
        ntiles = T // TT
        FC = D_FF // 128
        with ExitStack() as es:
            ident = self.make_identity(es)
            win = self.sb(es, "win", [128, 8, 2 * D_FF], BF16)
            wout = self.sb(es, "wout", [128, FC, D_MODEL], BF16)
            xs = [self.sb(es, "x%d" % i, [128, NS, D_MODEL], F32) for i in range(2)]
            xTs = [self.sb(es, "xT%d" % i, [128, 8, TT], BF16) for i in range(2)]
            aT = self.sb(es, "aT", [128, FC, TT], BF16)
            sg = [self.sb(es, "sg%d" % i, [128, TT], F32) for i in range(2)]
            gb = self.sb(es, "gb", [128, D_MODEL], F32)
            bb = self.sb(es, "bb", [128, D_MODEL], F32)
            stats = self.sb(es, "stats", [128, 2, 6], F32)
            mv = self.sb(es, "mv", [128, 2], F32)
            rstd = self.sb(es, "rstd", [128, 1], F32)
            nmr = self.sb(es, "nmr", [128, 1], F32)
            pt = [self.ps(es, "pt%d" % i, [128, 512]) for i in range(2)]
            pg = [self.ps(es, "pg%d" % i, [128, 512]) for i in range(2)]
            pu = [self.ps(es, "pu%d" % i, [128, 512]) for i in range(2)]
            py = [self.ps(es, "py%d" % i, [128, 512]) for i in range(2)]

            S.dma("sp", "gb", gb[:], g_vec.partition_broadcast(128), writes=["gb"])
            S.dma("sp", "bb", bb[:], b_vec.partition_broadcast(128), writes=["bb"])
            stage = [xs[0][:, i, :] for i in range(NS)] + [xs[1][:, i, :] for i in range(NS)]
            stage_w = [_View(v) for v in stage]
            self.load_weight_bf16(w_in, win, D_MODEL, 2 * D_FF, stage_w, "win")
            self.load_weight_bf16(w_out, wout, D_FF, D_MODEL, stage_w, "wout")
            S.flush()
            xin_v = x_in.rearrange("(n s p) d -> n p s d", p=128, s=NS)
            xout_v = x_out.rearrange("(n s p) d -> n p s d", p=128, s=NS)

            def load_x(i):
                sl = i % 2
                S.dma("sp", "xld%d" % sl, xs[sl][:], xin_v[i], writes=["x%d" % sl])

            load_x(0)
            tcount = [0]

            def do_transposes(i):
                sl = i % 2
                xt = xs[sl]
                xtok = "x%d" % sl
                xTt = xTs[sl]
                for k in range(8):
                    p = pt[tcount[0] % 2]
                    ptok = "pt%d" % (tcount[0] % 2)
                    tcount[0] += 1
                    for s in range(NS):
                        S.op("pe", lambda e, p=p, s=s, k=k, xt=xt: e.transpose(
                            out=p[:, s * 128:(s + 1) * 128], in_=xt[:, s, k * 128:(k + 1) * 128],
                            identity=ident[:]), reads=[xtok, "ident"], writes=[ptok])
                    if k % 2 == 0:
                        S.op("dve", lambda e, p=p, k=k, xTt=xTt: e.tensor_copy(out=xTt[:, k, :], in_=p[:, 0:TT]),
                             reads=[ptok], writes=[("xT", sl, k)])
                    else:
                        S.op("act", lambda e, p=p, k=k, xTt=xTt: e.copy(out=xTt[:, k, :], in_=p[:, 0:TT]),
                             reads=[ptok], writes=[("xT", sl, k)])

            do_transposes(0)
            for i in range(ntiles):
                sl = i % 2
                xt = xs[sl]
                xtok = "x%d" % sl
                xT = xTs[sl]
                if i + 1 < ntiles:
                    load_x(i + 1)
                for j in range(FC):
                    b = j % 2
                    g_ps, u_ps = pg[b], pu[b]
                    for k in range(8):
                        S.op("pe", lambda e, g_ps=g_ps, k=k, j=j, xT=xT: e.matmul(
                            g_ps[:, 0:TT], lhsT=win[:, k, j * 128:(j + 1) * 128], rhs=xT[:, k, :],
                            start=(k == 0), stop=(k == 7)), reads=["win", ("xT", sl, k)], writes=["pg%d" % b])
                    for k in range(8):
                        S.op("pe", lambda e, u_ps=u_ps, k=k, j=j, xT=xT: e.matmul(
                            u_ps[:, 0:TT], lhsT=win[:, k, D_FF + j * 128:D_FF + (j + 1) * 128],
                            rhs=xT[:, k, :], start=(k == 0), stop=(k == 7)),
                            reads=["win", ("xT", sl, k)], writes=["pu%d" % b])
                    S.op("act", lambda e, g_ps=g_ps, b=b: e.activation(
                        out=sg[b][:], in_=g_ps[:, 0:TT], func=AF.Silu),
                        reads=["pg%d" % b], writes=["sg%d" % b])
                    S.op("dve", lambda e, u_ps=u_ps, b=b, j=j: e.tensor_tensor(
                        out=aT[:, j, :], in0=sg[b][:], in1=u_ps[:, 0:TT], op=ALU.mult),
                        reads=["sg%d" % b, "pu%d" % b], writes=[("aT", j)])
                if i + 1 < ntiles:
                    do_transposes(i + 1)
                for s in range(NS):
                    for c in range(2):
                        y_ps = py[c]
                        for j in range(FC):
                            S.op("pe", lambda e, y_ps=y_ps, j=j, s=s, c=c: e.matmul(
                                y_ps[:], lhsT=aT[:, j, s * 128:(s + 1) * 128],
                                rhs=wout[:, j, c * 512:(c + 1) * 512],
                                start=(j == 0), stop=(j == FC - 1)),
                                reads=[("aT", j), "wout"], writes=["py%d" % c])
                        S.op("dve", lambda e, y_ps=y_ps, s=s, c=c, xt=xt: e.scalar_tensor_tensor(
                            out=xt[:, s, c * 512:(c + 1) * 512], in0=xt[:, s, c * 512:(c + 1) * 512],
                            scalar=2.0 * ALPHA, in1=y_ps[:], op0=ALU.mult, op1=ALU.add),
                            reads=["py%d" % c, xtok], writes=[xtok])
                    if getattr(self, "dbg", None) and i == 0 and s == 0:
                        S.dma("sp", "dbg", self.dbg["z"], xt[:, 0, :], reads=[xtok], writes=["dbgz"])
                        pass
                    self.layer_norm_rows(xt[:, s, :], xtok, gb, bb, stats, mv, rstd, nmr, 4.0 * LN_EPS)
                S.dma("sp", "xst%d" % sl, xout_v[i], xs[sl][:], reads=[xtok],
                      writes=[("xout", i)])
            S.flush()

    def layer_norm_rows(self, z, ztok, gb, bb, stats, mv, rstd, nmr, eps, pfx=""):
        S = self.S
        for c in range(2):
            S.op("dve", lambda e, c=c: e.bn_stats(out=stats[:, c, :], in_=z[:, c * 512:(c + 1) * 512]),
                 reads=[ztok], writes=["stats"])
        S.op("dve", lambda e: e.bn_aggr(out=mv[:], in_=stats[:]), reads=["stats"], writes=["mv"])
        S.op("act", lambda e: e.activation(out=rstd[:], in_=mv[:, 1:2], func=AF.Sqrt, bias=float(eps), scale=1.0),
             reads=["mv"], writes=["rstd"])
        S.op("dve", lambda e: e.reciprocal(out=rstd[:], in_=rstd[:]), reads=["rstd"], writes=["rstd"])
        S.op("dve", lambda e: e.scalar_tensor_tensor(out=nmr[:], in0=mv[:, 0:1], scalar=-1.0, in1=rstd[:],
                                                     op0=ALU.mult, op1=ALU.mult),
             reads=["mv", "rstd"], writes=["nmr"])
        S.op("act", lambda e: e.activation(out=z, in_=z, func=AF.Identity, bias=nmr[:], scale=rstd[:]),
             reads=[ztok, "rstd", "nmr"], writes=[ztok])
        S.op("pool", lambda e: e.tensor_tensor(out=z, in0=z, in1=gb[:], op=ALU.mult),
             reads=[ztok, "gb"], writes=[ztok])
        S.op("pool", lambda e: e.tensor_tensor(out=z, in0=z, in1=bb[:], op=ALU.add),
             reads=[ztok, "bb"], writes=[ztok])


class _View:
    def __init__(self, ap):
        self.ap = ap
        self.shape = ap.shape

    def __getitem__(self, k):
        return self.ap[k]


EV_COLS = 3632
OD_COLS = 4608
LOG_GAMMA = [float(np.log1p(-(2.0 ** (-5.0 - h)))) for h in range(4)]
TINY = 1e-30


def _op(S, eng, fn, r, w):
    S.op(eng, fn, reads=r, writes=w)


class MixerMixin:
    def alloc_scratch(self):
        T = self.T
        t = self.dram_tmp
        self.QT = t("s_QT", [8, 128, T], BF16)
        self.KT = [t("s_KTf", [8, 128, T], BF16), t("s_KTb", [8, 128, T], BF16)]
        self.Kt = [t("s_Kf", [T, 8, 128], BF16), t("s_Kb", [T, 8, 128], BF16)]
        self.Vt = t("s_V", [T, 8, 128], BF16)
        self.Gt = [t("s_Gf", [T, 8, 128], F32), t("s_Gb", [T, 8, 128], F32)]
        self.GATE = t("s_GATE", [T, 1024], F32)
        self.OD = [t("s_OF", [T, 1024], F32), t("s_OB", [T, 1024], F32)]

    def proj_phase(self, x_in, even, P):
        S = self.S
        T = self.T
        NT = T // 128
        ncol = EV_COLS if even else OD_COLS
        BND = self.BND // 128
        with ExitStack() as es:
            ident = self.make_identity(es)
            identb = self.sb(es, "identb", [128, 128], BF16)
            win = self.sb(es, "pwin", [128, 8, ncol], BF16)
            xs = [self.sb(es, "px%d" % i, [128, 1024], F32) for i in range(2)]
            xT = self.sb(es, "pxT", [128, 8, 128], BF16)
            Y = [self.sb(es, "Y%d" % i, [128, ncol], F32) for i in range(2)]
            osets = []
            for si in range(1):
                osets.append((self.sb(es, "Qk", [128, 8, 128], BF16), self.sb(es, "Kf", [128, 8, 128], BF16),
                              self.sb(es, "Kb", [128, 8, 128], BF16), self.sb(es, "Vk", [128, 8, 128], BF16),
                              self.sb(es, "Gf", [128, 8, 128], F32), self.sb(es, "Gb", [128, 8, 128], F32),
                              self.sb(es, "GA", [128, 1024], F32), self.sb(es, "TTs", [128, 3, 4, 128], BF16)))
            nw = self.sb(es, "nw", [128, 1024], F32)
            tmp1 = self.sb(es, "tmp1", [128, 1024], F32)
            tmp2 = self.sb(es, "tmp2", [128, 1024], F32)
            pp = [self.ps(es, "pp%d" % i, [128, 512]) for i in range(3)]
            ptb = [self.ps(es, "ptb%d" % i, [128, 4, 128], BF16) for i in range(2)]
            psm = self.ps(es, "psm", [128, 512])

            S.op("dve", lambda e: e.tensor_copy(out=identb[:], in_=ident[:]), reads=["ident"], writes=["identb"])
            for oset in osets:
                for tl in (oset[0], oset[1], oset[2], oset[4], oset[5]):
                    S.op("pool", lambda e, tl=tl: e.memset(tl[:], 0.0), writes=[tl.name])
            if even:
                A5 = [self.sb(es, "A5_%d" % i, [128, 5, 1024], BF16) for i in range(3)]
                pcv = [self.ps(es, "pcv%d" % i, [128, 512]) for i in range(2)]
                cw = self.sb(es, "cw", [128, 5, 1024], F32)
                sh = self.sb(es, "sh", [128, 8, 128], BF16)
                shk = self.sb(es, "shk", [128, 4, 128], BF16)
                gbb = self.sb(es, "gbb", [128, 16], F32)
                gt = self.sb(es, "gt", [128, 16], F32)
                E1 = self.sb(es, "E1", [128, 16], F32)
                E2 = self.sb(es, "E2", [128, 16], F32)
                a2w = self.sb(es, "a2w", [16, 2, 256], F32)
                a2b = self.sb(es, "a2b", [1, 2, 256], F32)
                ones1 = self.sb(es, "ones1", [1, 128], F32)
                baT = [self.sb(es, "baT%d" % j, [16, 128], F32) for j in range(2)]
                S.dma("sp", "cst1", cw[:], P["conv_w"].partition_broadcast(128), writes=["cw"])
                S.dma("sp", "cst2", sh[:], self.c_shift, writes=["sh"])
                S.dma("sp", "cst3", gbb[:], P["gate_b"].partition_broadcast(128), writes=["gbb"])
                S.dma("sp", "cst4", a2w[:], P["a2_w"].rearrange("j r c -> r j c"), writes=["a2w"])
                S.dma("sp", "cst5", a2b[:], P["a2_b"].rearrange("(o j) c -> o j c", o=1), writes=["a2b"])
                S.dma("sp", "cst6", nw[:], P["norm_w"].partition_broadcast(128), writes=["nw"])
                S.op("pool", lambda e: e.memset(ones1[:], 1.0), writes=["ones1"])
            else:
                lbl = self.sb(es, "lbl", [128, 2, 1024], F32)
                lb = self.sb(es, "lb", [128, 1024], F32)
                oml = self.sb(es, "oml", [128, 1024], F32)
                rot = self.sb(es, "rot", [128, 2, 128], F32)
                S.dma("sp", "cst1", lbl[:], P["lb_logits"].rearrange("l j c -> l (j c)").partition_broadcast(128),
                      writes=["lbl"])
                S.dma("sp", "cst6", nw[:, 0:512], P["norm_w"].partition_broadcast(128), writes=["nw"])
                if P["layer_idx"] == 0:
                    S.op("pool", lambda e: e.memset(lb[:], 0.0), writes=["lb"])
                else:
                    S.op("act", lambda e: e.activation(out=lbl[:], in_=lbl[:], func=AF.Exp), reads=["lbl"], writes=["lbl"])
                    S.op("dve", lambda e: e.tensor_tensor(out=lb[:], in0=lbl[:, 0, :], in1=lbl[:, 1, :], op=ALU.add),
                         reads=["lbl"], writes=["lb"])
                    S.op("dve", lambda e: e.reciprocal(out=lb[:], in_=lb[:]), reads=["lb"], writes=["lb"])
                    S.op("dve", lambda e: e.tensor_tensor(out=lb[:], in0=lb[:], in1=lbl[:, 1, :], op=ALU.mult),
                         reads=["lb", "lbl"], writes=["lb"])
                S.op("dve", lambda e: e.tensor_scalar(out=oml[:], in0=lb[:], scalar1=-1.0, scalar2=1.0,
                                                      op0=ALU.mult, op1=ALU.add), reads=["lb"], writes=["oml"])
                for oset in osets:
                    Gf, Gb = oset[4], oset[5]
                    for h in range(4):
                        S.op("pool", lambda e, h=h, Gf=Gf: e.memset(Gf[:, 4 + h, :], LOG_GAMMA[h]), reads=[Gf.name], writes=[Gf.name])
                        S.op("pool", lambda e, h=h, Gb=Gb: e.memset(Gb[:, 4 + h, :], LOG_GAMMA[h]), reads=[Gb.name], writes=[Gb.name])
            stage = [(_View(Y[0][:, 0:2048]), ), (_View(Y[1][:, 0:2048]),)]
            self.load_weight_bf16(P["w_in"], win, D_MODEL, ncol, [s[0] for s in stage], "pwin")
            S.flush()
            if even:
                nb = (NT - 1) // BND
                shkb = []
                for b in range(1, nb + 1):
                    tl = self.sb(es, "shkb%d" % b, [128, 4, 128], BF16)
                    S.op("dve", lambda e, tl=tl, b=b: e.tensor_scalar(
                        out=tl[:], in0=sh[:, 4:8, :], scalar1=self.kf[:, b:b + 1], scalar2=None, op0=ALU.mult),
                        reads=["sh", "kf"], writes=[tl.name])
                    shkb.append(tl)

            xin_v = x_in.rearrange("(n p) d -> n p d", p=128)

            def load_x(i):
                S.dma("sp", "pxld%d" % (i % 2), xs[i % 2][:], xin_v[i], writes=["px%d" % (i % 2)])

            def stage1(n):
                xt = xs[n % 2]
                xtok = "px%d" % (n % 2)
                y = Y[n % 2]
                ytok = "Y%d" % (n % 2)
                if n + 1 < NT:
                    load_x(n + 1)
                for k in range(8):
                    p = pp[k % 3]
                    ptok = "pp%d" % (k % 3)
                    S.op("pe", lambda e, p=p, k=k: e.transpose(out=p[:, 0:128], in_=xt[:, k * 128:(k + 1) * 128],
                                                              identity=ident[:]), reads=[xtok, "ident"], writes=[ptok])
                    eng = "dve" if k % 2 == 0 else "act"
                    if eng == "dve":
                        S.op("dve", lambda e, p=p, k=k: e.tensor_copy(out=xT[:, k, :], in_=p[:, 0:128]),
                             reads=[ptok], writes=["pxT"])
                    else:
                        S.op("act", lambda e, p=p, k=k: e.copy(out=xT[:, k, :], in_=p[:, 0:128]),
                             reads=[ptok], writes=["pxT"])
                ci = 0
                for c0 in range(0, ncol, 512):
                    c1 = min(ncol, c0 + 512)
                    p = pp[ci % 3]
                    ptok = "pp%d" % (ci % 3)
                    for k in range(8):
                        S.op("pe", lambda e, p=p, k=k, c0=c0, c1=c1: e.matmul(
                            p[:, 0:c1 - c0], lhsT=xT[:, k, :], rhs=win[:, k, c0:c1], start=(k == 0), stop=(k == 7)),
                            reads=["pxT", "pwin"], writes=[ptok])
                    if ci % 2 == 0:
                        S.op("dve", lambda e, p=p, c0=c0, c1=c1: e.tensor_copy(out=y[:, c0:c1], in_=p[:, 0:c1 - c0]),
                             reads=[ptok], writes=[(ytok, ci)])
                    else:
                        S.op("act", lambda e, p=p, c0=c0, c1=c1: e.copy(out=y[:, c0:c1], in_=p[:, 0:c1 - c0]),
                             reads=[ptok], writes=[(ytok, ci)])
                    ci += 1
                if even:
                    for j in range(5):
                        eng = ("dve", "pool", "dve", "pool", "dve")[j]
                        S.op(eng, lambda e, j=j: e.tensor_tensor(out=A5[n % 3][:, j, :], in0=y[:, 0:1024], in1=cw[:, j, :], op=ALU.mult),
                             reads=[(ytok, 0), (ytok, 1), "cw"], writes=[("A5", n % 3, j)])
                return ci

            def ytoks(m, c0, c1):
                return [("Y%d" % (m % 2), ci) for ci in range(c0 // 512, (c1 - 1) // 512 + 1)]

            def write_out(m):
                Qk, Kf, Kb, Vk, Gf, Gb, GA, TT = osets[m % len(osets)]
                t0 = m * 128
                for ai, (src, dst) in enumerate(((Qk, self.QT), (Kf, self.KT[0]), (Kb, self.KT[1]))):
                    for hg in range(2):
                        pb = ptb[(ai * 2 + hg) % 2]
                        pbt = "ptb%d" % ((ai * 2 + hg) % 2)
                        for h in range(4):
                            S.op("pe", lambda e, pb=pb, h=h, hg=hg, src=src: e.transpose(
                                out=pb[:, h, :], in_=src[:, hg * 4 + h, :], identity=identb[:]),
                                reads=[src.name, "identb"], writes=[pbt])
                        ttok = ("TT", ai, hg)
                        eng = "act" if hg == 0 else "dve"
                        if eng == "act":
                            S.op("act", lambda e, pb=pb, ai=ai: e.copy(out=TT[:, ai, :, :], in_=pb[:]),
                                 reads=[pbt], writes=[("TT", m % 2, ai)])
                        else:
                            S.op("dve", lambda e, pb=pb, ai=ai: e.tensor_copy(out=TT[:, ai, :, :], in_=pb[:]),
                                 reads=[pbt], writes=[("TT", m % 2, ai)])
                        S.dma("sp", "wo_tt%d_%d" % (ai, m % 2), dst[hg * 4:hg * 4 + 4, :, t0:t0 + 128].rearrange("h d t -> d h t"),
                              TT[:, ai, :, :], reads=[("TT", m % 2, ai)], writes=[("dQT", ai, hg, m)])
                S.dma("sp", "wo_kf%d" % (m % 2), self.Kt[0][t0:t0 + 128], Kf[:], reads=[Kf.name], writes=[("dKf", m)])
                S.dma("sp", "wo_kb%d" % (m % 2), self.Kt[1][t0:t0 + 128], Kb[:], reads=[Kb.name], writes=[("dKb", m)])
                S.dma("sp", "wo_v%d" % (m % 2), self.Vt[t0:t0 + 128], Vk[:], reads=[Vk.name], writes=[("dV", m)])
                S.dma("sp", "wo_gf%d" % (m % 2), self.Gt[0][t0:t0 + 128], Gf[:], reads=[Gf.name], writes=[("dGf", m)])
                S.dma("sp", "wo_gb%d" % (m % 2), self.Gt[1][t0:t0 + 128], Gb[:], reads=[Gb.name], writes=[("dGb", m)])
                S.dma("sp", "wo_ga%d" % (m % 2), self.GATE[t0:t0 + 128], GA[:], reads=[GA.name], writes=[("dGA", m)])

            def stage2_even(m):
                Qk, Kf, Kb, Vk, Gf, Gb, GA, TT = osets[m % len(osets)]
                y = Y[m % 2]
                yn = "Y%d" % (m % 2)
                for half in range(2):
                    pc = pcv[half]
                    mms = []
                    for j in range(5):
                        o = j - 2
                        oi = {-2: 0, -1: 1, 1: 2, 2: 3}.get(o)
                        if o == 0:
                            mms.append((identb[:], A5[m % 3], j, "identb", ("A5", m % 3, j)))
                        else:
                            mms.append((sh[:, oi, :], A5[m % 3], j, "sh", ("A5", m % 3, j)))
                        if o < 0 and m > 0:
                            if m % BND == 0:
                                tl = shkb[m // BND - 1]
                                mms.append((tl[:, oi, :], A5[(m - 1) % 3], j, tl.name, ("A5", (m - 1) % 3, j)))
                            else:
                                mms.append((sh[:, 4 + oi, :], A5[(m - 1) % 3], j, "sh", ("A5", (m - 1) % 3, j)))
                        if o > 0 and m < NT - 1:
                            if (m + 1) % BND == 0:
                                tl = shkb[(m + 1) // BND - 1]
                                mms.append((tl[:, oi, :], A5[(m + 1) % 3], j, tl.name, ("A5", (m + 1) % 3, j)))
                            else:
                                mms.append((sh[:, 4 + oi, :], A5[(m + 1) % 3], j, "sh", ("A5", (m + 1) % 3, j)))
                    for qi, (l, r, j, lt, rt) in enumerate(mms):
                        S.op("pe", lambda e, pc=pc, l=l, r=r, j=j, half=half, qi=qi, nq=len(mms): e.matmul(
                            pc[:], lhsT=l, rhs=r[:, j, half * 512:(half + 1) * 512], start=(qi == 0), stop=(qi == nq - 1)),
                            reads=[rt, lt], writes=["pcv%d" % half])
                    S.op("act", lambda e, pc=pc, half=half: e.activation(out=tmp2[:, half * 512:(half + 1) * 512], in_=pc[:], func=AF.Silu),
                         reads=["pcv%d" % half], writes=["tmp2"])
                S.op("dve", lambda e: e.tensor_copy(out=Qk[:, 0:4, :], in_=tmp2[:, 0:512].rearrange("p (h d) -> p h d", h=4)),
                     reads=["tmp2"], writes=[Qk.name])
                S.op("dve", lambda e: e.tensor_tensor(out=gt[:], in0=y[:, 2048:2064], in1=gbb[:], op=ALU.add),
                     reads=ytoks(m, 2048, 2064) + ["gbb"], writes=["gt"])
                S.op("act", lambda e: e.activation(out=E1[:], in_=gt[:], func=AF.Exp), reads=["gt"], writes=["E1"])
                S.op("act", lambda e: e.activation(out=E2[:], in_=gt[:], func=AF.Exp, scale=-1.0), reads=["gt"], writes=["E2"])
                S.op("act", lambda e: e.activation(out=E2[:], in_=E2[:], func=AF.Ln, bias=1.0), reads=["E2"], writes=["E2"])
                for d, (Kd, Gd) in enumerate(((Kf, Gf), (Kb, Gb))):
                    S.op("dve", lambda e, Kd=Kd, d=d: e.scalar_tensor_tensor(
                        out=Kd[:, 0:4, :], in0=tmp2[:, 512:1024].rearrange("p (h d) -> p h d", h=4), scalar=128.0 ** -0.5,
                        in1=E1[:, 8 * d:8 * d + 4].unsqueeze(2).to_broadcast([128, 4, 128]), op0=ALU.mult, op1=ALU.mult),
                        reads=["tmp2", "E1"], writes=[Kd.name])
                    S.op("pool", lambda e, Gd=Gd, d=d: e.tensor_scalar(
                        out=Gd[:, 0:4, :], in0=E2[:, 8 * d + 4:8 * d + 8].unsqueeze(2).to_broadcast([128, 4, 128]),
                        scalar1=-1.0, scalar2=None, op0=ALU.mult), reads=["E2"], writes=[Gd.name])
                S.op("act", lambda e: e.copy(out=Vk[:, 0:4, :], in_=y[:, 1024:1536].rearrange("p (h d) -> p h d", h=4)),
                     reads=ytoks(m, 1024, 1536), writes=[Vk.name])
                S.op("act", lambda e: e.activation(out=GA[:, 0:512], in_=y[:, 1536:2048], func=AF.Sigmoid),
                     reads=ytoks(m, 1536, 2048), writes=[GA.name])
                S.op("dve", lambda e: e.tensor_scalar(
                    out=Qk[:, 4:8, 0:64], in0=y[:, 2064:2320].rearrange("p (h d) -> p h d", h=4), scalar1=64.0 ** -0.5,
                    scalar2=None, op0=ALU.mult), reads=ytoks(m, 2064, 2320), writes=[Qk.name])
                S.op("dve", lambda e: e.tensor_copy(out=Kf[:, 4:8, 0:64], in_=y[:, 2320:2576].rearrange("p (h d) -> p h d", h=4)),
                     reads=ytoks(m, 2320, 2576), writes=[Kf.name])
                S.op("pool", lambda e: e.tensor_copy(out=Kb[:, 4:8, 0:64], in_=y[:, 2320:2576].rearrange("p (h d) -> p h d", h=4)),
                     reads=ytoks(m, 2320, 2576), writes=[Kb.name])
                S.op("act", lambda e: e.copy(out=Vk[:, 4:8, :], in_=y[:, 2576:3088].rearrange("p (h d) -> p h d", h=4)),
                     reads=ytoks(m, 2576, 3088), writes=[Vk.name])
                S.op("act", lambda e: e.activation(out=GA[:, 512:1024], in_=y[:, 3088:3600], func=AF.Silu),
                     reads=ytoks(m, 3088, 3600), writes=[GA.name])
                S.op("dve", lambda e: e.tensor_tensor(out=GA[:], in0=GA[:], in1=nw[:], op=ALU.mult),
                     reads=[GA.name, "nw"], writes=[GA.name])
                for j, Gd in enumerate((Gf, Gb)):
                    S.op("pe", lambda e, j=j: e.transpose(out=psm[0:16, 0:128], in_=y[:, 3600 + 16 * j:3616 + 16 * j],
                                                         identity=ident[:]), reads=ytoks(m, 3600, 3632) + ["ident"], writes=["psm"])
                    S.op("dve", lambda e, j=j: e.tensor_copy(out=baT[j][:], in_=psm[0:16, 0:128]), reads=["psm"],
                         writes=["baT%d" % j])
                    S.op("pe", lambda e, j=j: e.matmul(psm[:, 0:256], lhsT=baT[j][:], rhs=a2w[:, j, :], start=True, stop=False),
                         reads=["baT%d" % j, "a2w"], writes=["psm"])
                    S.op("pe", lambda e, j=j: e.matmul(psm[:, 0:256], lhsT=ones1[:], rhs=a2b[:, j, :], start=False, stop=True),
                         reads=["ones1", "a2b"], writes=["psm"])
                    S.op("act", lambda e: e.activation(out=tmp1[:, 0:256], in_=psm[:, 0:256], func=AF.Exp, scale=-1.0),
                         reads=["psm"], writes=["tmp1"])
                    S.op("act", lambda e: e.activation(out=tmp1[:, 0:256], in_=tmp1[:, 0:256], func=AF.Ln, bias=1.0),
                         reads=["tmp1"], writes=["tmp1"])
                    S.op("dve", lambda e, Gd=Gd: e.tensor_scalar(
                        out=Gd[:, 4:8, 0:64], in0=tmp1[:, 0:256].rearrange("p (h d) -> p h d", h=4), scalar1=-1.0 / 16.0,
                        scalar2=None, op0=ALU.mult), reads=["tmp1"], writes=[Gd.name])
                write_out(m)

            def stage2_odd(m):
                Qk, Kf, Kb, Vk, Gf, Gb, GA, TT = osets[m % len(osets)]
                y = Y[m % 2]
                t0 = m * 128
                S.dma("sp", "rotld", rot[:], self.c_rot[t0:t0 + 128].rearrange("t (a c) -> t a c", a=2), writes=["rot"])
                S.op("act", lambda e: e.activation(out=tmp1[:, 0:512], in_=y[:, 0:512], func=AF.Silu),
                     reads=ytoks(m, 0, 512), writes=["tmp1"])
                S.op("dve", lambda e: e.tensor_scalar(out=Qk[:, 0:4, :], in0=tmp1[:, 0:512].rearrange("p (h d) -> p h d", h=4),
                                                      scalar1=128.0 ** -0.5, scalar2=None, op0=ALU.mult),
                     reads=["tmp1"], writes=[Qk.name])
                S.op("act", lambda e: e.activation(out=tmp2[:], in_=y[:, 512:1536], func=AF.Sigmoid),
                     reads=ytoks(m, 512, 1536), writes=["tmp2"])
                S.op("dve", lambda e: e.tensor_tensor(out=tmp2[:], in0=tmp2[:], in1=oml[:], op=ALU.mult),
                     reads=["tmp2", "oml"], writes=["tmp2"])
                S.op("dve", lambda e: e.tensor_tensor(out=tmp2[:], in0=tmp2[:], in1=lb[:], op=ALU.add),
                     reads=["tmp2", "lb"], writes=["tmp2"])
                for d, (Kd, Gd) in enumerate(((Kf, Gf), (Kb, Gb))):
                    fv = tmp2[:, d * 512:(d + 1) * 512].rearrange("p (h d) -> p h d", h=4)
                    S.op("pool", lambda e, Kd=Kd, fv=fv: e.tensor_scalar(out=Kd[:, 0:4, :], in0=fv, scalar1=-1.0, scalar2=1.0,
                                                                        op0=ALU.mult, op1=ALU.add), reads=["tmp2"], writes=[Kd.name])
                    S.op("dve", lambda e, fv=fv, d=d: e.tensor_scalar(out=tmp1[:, d * 512:(d + 1) * 512].rearrange("p (h d) -> p h d", h=4),
                                                                    in0=fv, scalar1=TINY, scalar2=None, op0=ALU.max),
                         reads=["tmp2"], writes=["tmp1"])
                    S.op("act", lambda e, Gd=Gd, d=d: e.activation(out=Gd[:, 0:4, :],
                                                                  in_=tmp1[:, d * 512:(d + 1) * 512].rearrange("p (h d) -> p h d", h=4),
                                                                  func=AF.Ln), reads=["tmp1"], writes=[Gd.name])
                S.op("act", lambda e: e.copy(out=Vk[:, 0:4, :], in_=y[:, 1536:2048].rearrange("p (h d) -> p h d", h=4)),
                     reads=ytoks(m, 1536, 2048), writes=[Vk.name])
                S.op("act", lambda e: e.activation(out=GA[:, 0:512], in_=y[:, 2048:2560], func=AF.Silu),
                     reads=ytoks(m, 2048, 2560), writes=[GA.name])
                S.op("dve", lambda e: e.tensor_tensor(out=GA[:, 0:512], in0=GA[:, 0:512], in1=nw[:, 0:512], op=ALU.mult),
                     reads=[GA.name, "nw"], writes=[GA.name])
                cosb = rot[:, 0, 0:64].unsqueeze(1).to_broadcast([128, 4, 64])
                sinb = rot[:, 0, 64:128].unsqueeze(1).to_broadcast([128, 4, 64])
                cosk = rot[:, 1, 0:64].unsqueeze(1).to_broadcast([128, 4, 64])
                sink = rot[:, 1, 64:128].unsqueeze(1).to_broadcast([128, 4, 64])
                for (c0, cb, sb_, dsts) in ((2560, cosb, sinb, (Qk,)), (3072, cosk, sink, (Kf, Kb))):
                    xv = y[:, c0:c0 + 512].rearrange("p (h d) -> p h d", h=4)
                    x1, x2 = xv[:, :, 0:64], xv[:, :, 64:128]
                    t1 = tmp1[:, 0:256].rearrange("p (h d) -> p h d", h=4)
                    t2 = tmp1[:, 256:512].rearrange("p (h d) -> p h d", h=4)
                    t3 = tmp1[:, 512:768].rearrange("p (h d) -> p h d", h=4)
                    t4 = tmp1[:, 768:1024].rearrange("p (h d) -> p h d", h=4)
                    yt = ytoks(m, c0, c0 + 512)
                    S.op("dve", lambda e, t1=t1, x1=x1, cb=cb: e.tensor_tensor(out=t1, in0=x1, in1=cb, op=ALU.mult),
                         reads=yt + ["rot"], writes=["tmp1"])
                    S.op("pool", lambda e, t2=t2, x2=x2, sb_=sb_: e.tensor_tensor(out=t2, in0=x2, in1=sb_, op=ALU.mult),
                         reads=yt + ["rot"], writes=["tmp1"])
                    S.op("dve", lambda e, t3=t3, x1=x1, sb_=sb_: e.tensor_tensor(out=t3, in0=x1, in1=sb_, op=ALU.mult),
                         reads=yt + ["rot"], writes=["tmp1"])
                    S.op("pool", lambda e, t4=t4, x2=x2, cb=cb: e.tensor_tensor(out=t4, in0=x2, in1=cb, op=ALU.mult),
                         reads=yt + ["rot"], writes=["tmp1"])
                    for dst in dsts:
                        S.op("dve", lambda e, dst=dst, t1=t1, t2=t2: e.tensor_tensor(out=dst[:, 4:8, 0:64], in0=t1, in1=t2,
                                                                                    op=ALU.subtract), reads=["tmp1"], writes=[dst.name])
                        S.op("pool", lambda e, dst=dst, t3=t3, t4=t4: e.tensor_tensor(out=dst[:, 4:8, 64:128], in0=t3, in1=t4,
                                                                                     op=ALU.add), reads=["tmp1"], writes=[dst.name])
                S.op("act", lambda e: e.copy(out=Vk[:, 4:8, :], in_=y[:, 3584:4096].rearrange("p (h d) -> p h d", h=4)),
                     reads=ytoks(m, 3584, 4096), writes=[Vk.name])
                S.op("act", lambda e: e.activation(out=GA[:, 512:1024], in_=y[:, 4096:4608], func=AF.Silu),
                     reads=ytoks(m, 4096, 4608), writes=[GA.name])
                write_out(m)

            load_x(0)
            if even:
                for n in range(NT + 1):
                    if n < NT:
                        stage1(n)
                    if n >= 1:
                        stage2_even(n - 1)
            else:
                for n in range(NT):
                    stage1(n)
                    stage2_odd(n)
            S.flush()

    def scan_phase(self, even):
        S = self.S
        T = self.T
        L = 64
        BT = 256
        CB = BT // L
        NB = T // BT
        NCH = T // L
        BNDC = self.BND // L
        nden = 4 if even else 0
        with ExitStack() as es:
            tri = self.sb(es, "tri", [64, 8, 64], F32)
            onesb = self.sb(es, "onesb", [64, 1], BF16)
            S.dma("sp", "cst1", tri[:], self.c_tri, writes=["tri"])
            S.op("pool", lambda e: e.memset(onesb[:], 1.0), writes=["onesb"])
            QTs = [self.sb(es, "sQT%d" % i, [128, 8, BT], BF16) for i in range(2)]
            KTs = [self.sb(es, "sKT%d" % i, [128, 8, BT], BF16) for i in range(2)]
            Ks = [self.sb(es, "sK%d" % i, [64, CB, 8, 128], BF16) for i in range(2)]
            Vs = [self.sb(es, "sV%d" % i, [64, CB, 8, 128], BF16) for i in range(2)]
            Gs = [self.sb(es, "sG%d" % i, [64, CB, 8, 128], F32) for i in range(2)]
            Gh = [self.sb(es, "sGh%d" % i, [64, CB, 8, 128], BF16) for i in range(2)]
            Gl = [self.sb(es, "sGl%d" % i, [64, CB, 8, 128], BF16) for i in range(2)]
            trib = self.sb(es, "trib", [64, 8, 64], BF16)
            S.op("dve", lambda e: e.tensor_copy(out=trib[:], in_=tri[:]), reads=["tri"], writes=["trib"])
            Os = [self.sb(es, "sO%d" % i, [64, CB, 8, 128], F32) for i in range(2)]
            EE = [self.sb(es, "EE%d" % i, [128, 8, 2, 64], F32) for i in range(2)]
            Ek = [self.sb(es, "Ek%d" % i, [128, 8, 64], F32) for i in range(2)]
            Es = [self.sb(es, "Es%d" % i, [64, 8, 128], F32) for i in range(2)]
            Q1 = [self.sb(es, "Q1_%d" % i, [128, 8, 64], BF16) for i in range(2)]
            K1 = [self.sb(es, "K1_%d" % i, [128, 8, 64], BF16) for i in range(2)]
            Q2 = [self.sb(es, "Q2_%d" % i, [128, 8, 64], BF16) for i in range(2)]
            K2 = [self.sb(es, "K2_%d" % i, [64, 8, 128], BF16) for i in range(2)]
            PT = [self.sb(es, "PT%d" % i, [64, 8, 64], BF16) for i in range(2)]
            S32 = self.sb(es, "S32", [128, 8, 128], F32)
            Sb = self.sb(es, "Sb", [128, 8, 128], BF16)
            Sn = self.sb(es, "Sn", [128, 4], F32)
            Snb = self.sb(es, "Snb", [128, 4], BF16)
            dn = self.sb(es, "dn", [64, 8], F32)
            p_suf = self.ps(es, "p_suf", [64, 512])
            p_bm = self.ps(es, "p_bm", [128, 4, 2, 64])
            p_st = self.ps(es, "p_st", [64, 8, 64])
            p_o = self.ps(es, "p_o", [64, 8, 128])
            p_u = self.ps(es, "p_u", [128, 8, 128])
            p_sm = self.ps(es, "p_sm", [128, 16])
            S.flush()

            for d in range(2):
                rev = d == 1
                i0 = 4 if rev else 0
                tim = trib[:, i0:i0 + 2, :]
                tsuf = trib[:, i0 + 2, :]
                msk = tri[:, i0 + 3, :]
                mskb = msk.unsqueeze(1).to_broadcast([64, 8, 64])
                deccol = 0 if rev else 63
                for tl, nm in ((S32, "S32"), (Sb, "Sb"), (Sn, "Sn"), (Snb, "Snb")):
                    S.op("pool", lambda e, tl=tl: e.memset(tl[:], 0.0), reads=[nm], writes=[nm])
                blocks = list(range(NB))
                if rev:
                    blocks = blocks[::-1]
                seq = []
                for ib, bi in enumerate(blocks):
                    cs = list(range(CB))
                    if rev:
                        cs = cs[::-1]
                    for c in cs:
                        seq.append((ib, bi, c))

                def load_block(bi, slot, d=d):
                    t0 = bi * BT
                    S.dma("sp", "sld_q%d" % slot, QTs[slot][:], self.QT[:, :, t0:t0 + BT].rearrange("h d t -> d h t"),
                          writes=["sQT%d" % slot])
                    S.dma("sp", "sld_k%d" % slot, KTs[slot][:], self.KT[d][:, :, t0:t0 + BT].rearrange("h d t -> d h t"),
                          writes=["sKT%d" % slot])
                    S.dma("sp", "sld_kt%d" % slot, Ks[slot][:],
                          self.Kt[d][t0:t0 + BT].rearrange("(c p) h e -> p c h e", p=L), writes=["sK%d" % slot])
                    S.dma("sp", "sld_v%d" % slot, Vs[slot][:],
                          self.Vt[t0:t0 + BT].rearrange("(c p) h e -> p c h e", p=L), writes=["sV%d" % slot])
                    S.dma("sp", "sld_g%d" % slot, Gs[slot][:],
                          self.Gt[d][t0:t0 + BT].rearrange("(c p) h e -> p c h e", p=L), writes=["sG%d" % slot])
                    S.op("act", lambda e, slot=slot: e.copy(out=Gh[slot][:], in_=Gs[slot][:]), reads=["sG%d" % slot],
                         writes=["sGh%d" % slot])
                    S.op("pool", lambda e, slot=slot: e.tensor_tensor(out=Gl[slot][:], in0=Gs[slot][:], in1=Gh[slot][:],
                                                                     op=ALU.subtract), reads=["sG%d" % slot, "sGh%d" % slot],
                         writes=["sGl%d" % slot])

                def prep_round(k, r):
                    ib, bi, c = seq[k]
                    slot = ib % 2
                    p = k % 2
                    gh_n, gl_n = "sGh%d" % slot, "sGl%d" % slot
                    gh = Gh[slot][:, c, :, :]
                    gl = Gl[slot][:, c, :, :]
                    S.op("pe", lambda e, gh=gh, r=r: e.matmul(p_suf[:], lhsT=tsuf, rhs=gh[:, 4 * r:4 * r + 4, :],
                                                              start=True, stop=False), reads=[gh_n, "trib"], writes=["p_suf"])
                    S.op("pe", lambda e, gl=gl, r=r: e.matmul(p_suf[:], lhsT=tsuf, rhs=gl[:, 4 * r:4 * r + 4, :],
                                                              start=False, stop=True), reads=[gl_n, "trib"], writes=["p_suf"])
                    S.op("act", lambda e, p=p, r=r: e.activation(out=Es[p][:, 4 * r:4 * r + 4, :],
                                                                in_=p_suf[:].rearrange("q (h e) -> q h e", h=4), func=AF.Exp),
                         reads=["p_suf"], writes=[("Es", p, r)])
                    for h4 in range(4):
                        S.op("pe", lambda e, gh=gh, r=r, h4=h4: e.matmul(p_bm[:, h4, :, :], lhsT=gh[:, 4 * r + h4, :], rhs=tim,
                                                                        start=True, stop=False), reads=[gh_n, "trib"], writes=["p_bm"])
                        S.op("pe", lambda e, gl=gl, r=r, h4=h4: e.matmul(p_bm[:, h4, :, :], lhsT=gl[:, 4 * r + h4, :], rhs=tim,
                                                                        start=False, stop=True), reads=[gl_n, "trib"], writes=["p_bm"])
                    S.op("act", lambda e, p=p, r=r: e.activation(out=EE[p][:, 4 * r:4 * r + 4, :, :], in_=p_bm[:], func=AF.Exp),
                         reads=["p_bm"], writes=[("EE", p, r)])
                    S.op("act", lambda e, p=p, r=r: e.activation(out=Ek[p][:, 4 * r:4 * r + 4, :], in_=p_bm[:, :, 1, :],
                                                                func=AF.Exp, scale=-1.0), reads=["p_bm"], writes=[("Ek", p, r)])

                def prep_tail(k):
                    ib, bi, c = seq[k]
                    slot = ib % 2
                    p = k % 2
                    tsl = slice(c * L, (c + 1) * L)
                    ee = [("EE", p, 0), ("EE", p, 1)]
                    S.op("dve", lambda e, p=p, slot=slot, tsl=tsl: e.tensor_tensor(
                        out=Q1[p][:], in0=QTs[slot][:, :, tsl], in1=EE[p][:, :, 1, :], op=ALU.mult),
                        reads=["sQT%d" % slot] + ee, writes=[("Q1", p)])
                    S.op("pool", lambda e, p=p, slot=slot, tsl=tsl: e.tensor_tensor(
                        out=K1[p][:], in0=KTs[slot][:, :, tsl], in1=Ek[p][:], op=ALU.mult),
                        reads=["sKT%d" % slot, ("Ek", p, 0), ("Ek", p, 1)], writes=[("K1", p)])
                    S.op("pool", lambda e, p=p, slot=slot, tsl=tsl: e.tensor_tensor(
                        out=Q2[p][:], in0=QTs[slot][:, :, tsl], in1=EE[p][:, :, 0, :], op=ALU.mult),
                        reads=["sQT%d" % slot] + ee, writes=[("Q2", p)])
                    S.op("dve", lambda e, p=p, slot=slot, c=c: e.tensor_tensor(
                        out=K2[p][:], in0=Ks[slot][:, c, :, :], in1=Es[p][:], op=ALU.mult),
                        reads=["sK%d" % slot, ("Es", p, 0), ("Es", p, 1)], writes=[("K2", p)])

                def main_a(k):
                    ib, bi, c = seq[k]
                    slot = ib % 2
                    p = k % 2
                    gc = bi * CB + c
                    bidx = None
                    if not rev and gc > 0 and gc % BNDC == 0:
                        bidx = gc // BNDC
                    if rev and (gc + 1) % BNDC == 0 and gc + 1 < NCH:
                        bidx = (gc + 1) // BNDC
                    if bidx is not None:
                        S.op("dve", lambda e, bidx=bidx: e.tensor_scalar(
                            out=S32[:], in0=S32[:], scalar1=self.kf[:, bidx:bidx + 1], scalar2=None, op0=ALU.mult),
                            reads=["S32", "kf"], writes=["S32"])
                        S.op("pool", lambda e: e.tensor_copy(out=Sb[:], in_=S32[:]), reads=["S32"], writes=["Sb"])
                        if nden:
                            S.op("dve", lambda e, bidx=bidx: e.tensor_scalar(
                                out=Sn[:], in0=Sn[:], scalar1=self.kf[:, bidx:bidx + 1], scalar2=None, op0=ALU.mult),
                                reads=["Sn", "kf"], writes=["Sn"])
                            S.op("pool", lambda e: e.tensor_copy(out=Snb[:], in_=Sn[:]), reads=["Sn"], writes=["Snb"])
                    for h in range(8):
                        S.op("pe", lambda e, h=h, p=p: e.matmul(p_st[:, h, :], lhsT=K1[p][:, h, :], rhs=Q1[p][:, h, :],
                                                                start=True, stop=True), reads=[("K1", p), ("Q1", p)], writes=["p_st"])
                    S.op("dve", lambda e, p=p: e.tensor_tensor(out=PT[p][:], in0=p_st[:], in1=mskb, op=ALU.mult),
                         reads=["p_st", "tri"], writes=[("PT", p)])

                def main_b(k):
                    ib, bi, c = seq[k]
                    slot = ib % 2
                    p = k % 2
                    vn = "sV%d" % slot
                    on = "sO%d" % slot
                    for h in range(8):
                        S.op("pe", lambda e, h=h, p=p: e.matmul(p_o[:, h, :], lhsT=Q2[p][:, h, :], rhs=Sb[:, h, :], start=True, stop=False),
                             reads=[("Q2", p), "Sb"], writes=["p_o"])
                        S.op("pe", lambda e, h=h, p=p, slot=slot, c=c: e.matmul(p_o[:, h, :], lhsT=PT[p][:, h, :], rhs=Vs[slot][:, c, h, :],
                                                                               start=False, stop=True), reads=[("PT", p), vn], writes=["p_o"])
                    for h in range(nden):
                        S.op("pe", lambda e, h=h, p=p: e.matmul(p_sm[0:64, h:h + 1], lhsT=Q2[p][:, h, :], rhs=Snb[:, h:h + 1],
                                                                start=True, stop=False), reads=[("Q2", p), "Snb"], writes=["p_smd"])
                        S.op("pe", lambda e, h=h, p=p: e.matmul(p_sm[0:64, h:h + 1], lhsT=PT[p][:, h, :], rhs=onesb[:],
                                                                start=False, stop=True), reads=[("PT", p), "onesb"], writes=["p_smd"])
                    for h in range(8):
                        S.op("pe", lambda e, h=h, p=p, slot=slot, c=c: e.matmul(p_u[:, h, :], lhsT=K2[p][:, h, :], rhs=Vs[slot][:, c, h, :],
                                                                               start=True, stop=True), reads=[("K2", p), vn], writes=["p_u"])
                    for h in range(nden):
                        S.op("pe", lambda e, h=h, p=p: e.matmul(p_sm[:, 8 + h:9 + h], lhsT=K2[p][:, h, :], rhs=onesb[:],
                                                                start=True, stop=True), reads=[("K2", p), "onesb"], writes=["p_smu"])
                    if nden:
                        S.op("dve", lambda e: e.tensor_copy(out=dn[:, 0:4], in_=p_sm[0:64, 0:4]), reads=["p_smd"], writes=["dn"])
                        S.op("dve", lambda e: e.scalar_tensor_tensor(out=dn[:, 4:8], in0=dn[:, 0:4], scalar=-1.0, in1=dn[:, 0:4],
                                                                     op0=ALU.mult, op1=ALU.max), reads=["dn"], writes=["dn"])
                        S.op("dve", lambda e: e.tensor_scalar(out=dn[:, 4:8], in0=dn[:, 4:8], scalar1=1.0, scalar2=None, op0=ALU.max),
                             reads=["dn"], writes=["dn"])
                        S.op("dve", lambda e: e.reciprocal(out=dn[:, 0:4], in_=dn[:, 4:8]), reads=["dn"], writes=["dn"])
                        S.op("dve", lambda e, slot=slot, c=c: e.tensor_tensor(
                            out=Os[slot][:, c, 0:4, :], in0=p_o[:, 0:4, :], in1=dn[:, 0:4].unsqueeze(2).to_broadcast([64, 4, 128]),
                            op=ALU.mult), reads=["p_o", "dn"], writes=[(on, c, 0)])
                    else:
                        S.op("dve", lambda e, slot=slot, c=c: e.tensor_copy(out=Os[slot][:, c, 0:4, :], in_=p_o[:, 0:4, :]),
                             reads=["p_o"], writes=[(on, c, 0)])
                    S.op("act", lambda e, slot=slot, c=c: e.copy(out=Os[slot][:, c, 4:8, :], in_=p_o[:, 4:8, :]),
                         reads=["p_o"], writes=[(on, c, 1)])
                    ee = [("EE", p, 0), ("EE", p, 1)]
                    S.op("dve", lambda e, p=p: e.tensor_tensor(
                        out=S32[:], in0=S32[:], in1=EE[p][:, :, 0, deccol:deccol + 1].to_broadcast([128, 8, 128]), op=ALU.mult),
                        reads=["S32"] + ee, writes=["S32"])
                    S.op("dve", lambda e: e.tensor_tensor(out=S32[:], in0=S32[:], in1=p_u[:], op=ALU.add),
                         reads=["S32", "p_u"], writes=["S32"])
                    S.op("act", lambda e: e.copy(out=Sb[:], in_=S32[:]), reads=["S32"], writes=["Sb"])
                    if nden:
                        S.op("dve", lambda e, p=p: e.tensor_tensor(out=Sn[:], in0=Sn[:], in1=EE[p][:, 0:4, 0, deccol], op=ALU.mult),
                             reads=["Sn"] + ee, writes=["Sn"])
                        S.op("dve", lambda e: e.tensor_tensor(out=Sn[:], in0=Sn[:], in1=p_sm[:, 8:12], op=ALU.add),
                             reads=["Sn", "p_smu"], writes=["Sn"])
                        S.op("act", lambda e: e.copy(out=Snb[:], in_=Sn[:]), reads=["Sn"], writes=["Snb"])
                    last_c = 0 if rev else CB - 1
                    if c == last_c:
                        t0 = bi * BT
                        S.dma("sp", "sst%d" % slot, self.OD[d][t0:t0 + BT].rearrange("(c p) (h e) -> p c h e", p=L, h=8),
                              Os[slot][:], reads=[(on, c_, hf) for c_ in range(CB) for hf in range(2)], writes=[("dO", d, bi)])

                load_block(blocks[0], 0)
                if NB > 1:
                    load_block(blocks[1], 1)
                prep_round(0, 0)
                prep_round(0, 1)
                prep_tail(0)
                for k in range(len(seq)):
                    nxt = k + 1 < len(seq)
                    if nxt:
                        prep_round(k + 1, 0)
                    main_a(k)
                    if nxt:
                        prep_round(k + 1, 1)
                    main_b(k)
                    ib_k = seq[k][0]
                    if (k + 1) % CB == 0 and ib_k + 2 < NB:
                        load_block(blocks[ib_k + 2], ib_k % 2)
                    if nxt:
                        prep_tail(k + 1)
                S.flush()

    def epi_phase(self, x_in, x_out, even, w_out, g_vec, b_vec):
        S = self.S
        T = self.T
        NT = T // 128
        with ExitStack() as es:
            ident = self.make_identity(es)
            wo = self.sb(es, "ewo", [128, 8, 1024], BF16)
            xs = [self.sb(es, "ex%d" % i, [128, 1024], F32) for i in range(2)]
            of = [self.sb(es, "eof%d" % i, [128, 1024], F32) for i in range(2)]
            ob = [self.sb(es, "eob%d" % i, [128, 1024], F32) for i in range(2)]
            ga = [self.sb(es, "ega%d" % i, [128, 1024], F32) for i in range(2)]
            sq = self.sb(es, "esq", [128, 1024], F32)
            onT = self.sb(es, "eonT", [128, 8, 128], BF16)
            s1 = self.sb(es, "es1", [128, 8], F32)
            s2 = self.sb(es, "es2", [128, 8], F32)
            cfl = self.sb(es, "ecfl", [128, 8], F32)
            gb = self.sb(es, "egb", [128, 1024], F32)
            bb = self.sb(es, "ebb", [128, 1024], F32)
            stats = self.sb(es, "estats", [128, 2, 6], F32)
            mv = self.sb(es, "emv", [128, 2], F32)
            rstd = self.sb(es, "erstd", [128, 1], F32)
            nmr = self.sb(es, "enmr", [128, 1], F32)
            pt = [self.ps(es, "ept%d" % i, [128, 512]) for i in range(2)]
            py = [self.ps(es, "epy%d" % i, [128, 512]) for i in range(2)]
            S.dma("sp", "cst1", gb[:], g_vec.partition_broadcast(128), writes=["gb"])
            S.dma("sp", "cst2", bb[:], b_vec.partition_broadcast(128), writes=["bb"])
            lnh = (0, 4) if even else (4, 8)
            S.op("pool", lambda e: e.memset(cfl[:], 0.0), writes=["ecfl"])
            S.op("pool", lambda e: e.memset(cfl[:, lnh[0]:lnh[1]], 1.0 / 128.0), reads=["ecfl"], writes=["ecfl"])
            self.load_weight_bf16(w_out, wo, 1024, 1024, [_View(of[0][:]), _View(of[1][:])], "ewo")
            S.flush()
            xin_v = x_in.rearrange("(n p) d -> n p d", p=128)
            xout_v = x_out.rearrange("(n p) d -> n p d", p=128)

            def load(i):
                sl = i % 2
                S.dma("sp", "eld_x%d" % sl, xs[sl][:], xin_v[i], writes=["ex%d" % sl])
                S.dma("sp", "eld_f%d" % sl, of[sl][:], self.OD[0][i * 128:(i + 1) * 128], reads=[("dO", 0, i)], writes=["eof%d" % sl])
                S.dma("sp", "eld_b%d" % sl, ob[sl][:], self.OD[1][i * 128:(i + 1) * 128], reads=[("dO", 1, i)], writes=["eob%d" % sl])
                S.dma("sp", "eld_g%d" % sl, ga[sl][:], self.GATE[i * 128:(i + 1) * 128], reads=[("dGA", i)], writes=["ega%d" % sl])

            load(0)
            for i in range(NT):
                sl = i % 2
                if i + 1 < NT:
                    load(i + 1)
                o = of[sl]
                on_ = "eof%d" % sl
                o3 = o[:].rearrange("p (h e) -> p h e", h=8)
                S.op("dve", lambda e, o=o, sl=sl: e.tensor_tensor(out=o[:], in0=o[:], in1=ob[sl][:], op=ALU.add),
                     reads=[on_, "eob%d" % sl], writes=[on_])
                S.op("act", lambda e, o=o: e.activation(out=sq[:], in_=o[:], func=AF.Square), reads=[on_], writes=["esq"])
                S.op("dve", lambda e, o3=o3: e.tensor_reduce(out=s1[:], in_=o3, axis=AX.X, op=ALU.add), reads=[on_], writes=["es1"])
                S.op("dve", lambda e: e.tensor_reduce(out=s2[:], in_=sq[:].rearrange("p (h e) -> p h e", h=8), axis=AX.X, op=ALU.add),
                     reads=["esq"], writes=["es2"])
                S.op("dve", lambda e: e.tensor_tensor(out=s1[:], in0=s1[:], in1=cfl[:], op=ALU.mult), reads=["es1", "ecfl"], writes=["es1"])
                S.op("dve", lambda e: e.tensor_tensor(out=sq[:, 0:8], in0=s1[:], in1=s1[:], op=ALU.mult), reads=["es1"], writes=["esq"])
                S.op("dve", lambda e: e.scalar_tensor_tensor(out=s2[:], in0=s2[:], scalar=1.0 / 128.0, in1=sq[:, 0:8],
                                                             op0=ALU.mult, op1=ALU.subtract), reads=["es2", "esq"], writes=["es2"])
                S.op("act", lambda e: e.activation(out=s2[:], in_=s2[:], func=AF.Sqrt, bias=float(LN_EPS), scale=1.0),
                     reads=["es2"], writes=["es2"])
                S.op("dve", lambda e: e.reciprocal(out=s2[:], in_=s2[:]), reads=["es2"], writes=["es2"])
                S.op("dve", lambda e, o3=o3: e.tensor_tensor(out=o3, in0=o3, in1=s1[:].unsqueeze(2).to_broadcast([128, 8, 128]),
                                                            op=ALU.subtract), reads=[on_, "es1"], writes=[on_])
                S.op("pool", lambda e, o3=o3: e.tensor_tensor(out=o3, in0=o3, in1=s2[:].unsqueeze(2).to_broadcast([128, 8, 128]),
                                                             op=ALU.mult), reads=[on_, "es2"], writes=[on_])
                S.op("dve", lambda e, o=o, sl=sl: e.tensor_tensor(out=o[:], in0=o[:], in1=ga[sl][:], op=ALU.mult),
                     reads=[on_, "ega%d" % sl], writes=[on_])
                for half in range(2):
                    p = pt[half]
                    for k4 in range(4):
                        k = half * 4 + k4
                        S.op("pe", lambda e, p=p, k=k, k4=k4, o=o: e.transpose(out=p[:, k4 * 128:(k4 + 1) * 128],
                                                                             in_=o[:, k * 128:(k + 1) * 128], identity=ident[:]),
                             reads=[on_, "ident"], writes=["ept%d" % half])
                    if half == 0:
                        S.op("dve", lambda e, p=p: e.tensor_copy(out=onT[:, 0:4, :], in_=p[:].rearrange("p (k t) -> p k t", k=4)),
                             reads=["ept0"], writes=[("eonT", 0)])
                    else:
                        S.op("act", lambda e, p=p: e.copy(out=onT[:, 4:8, :], in_=p[:].rearrange("p (k t) -> p k t", k=4)),
                             reads=["ept1"], writes=[("eonT", 1)])
                xt = xs[sl]
                xtok = "ex%d" % sl
                for c in range(2):
                    for k in range(8):
                        S.op("pe", lambda e, c=c, k=k: e.matmul(py[c][:], lhsT=onT[:, k, :], rhs=wo[:, k, c * 512:(c + 1) * 512],
                                                                start=(k == 0), stop=(k == 7)),
                             reads=[("eonT", 0), ("eonT", 1), "ewo"], writes=["epy%d" % c])
                    S.op("dve", lambda e, c=c, xt=xt: e.scalar_tensor_tensor(
                        out=xt[:, c * 512:(c + 1) * 512], in0=xt[:, c * 512:(c + 1) * 512], scalar=ALPHA, in1=py[c][:],
                        op0=ALU.mult, op1=ALU.add), reads=["epy%d" % c, xtok], writes=[xtok])
                self.layer_norm_rows(xt[:], xtok, gb, bb, stats, mv, rstd, nmr, LN_EPS, pfx="e")
                S.dma("sp", "est%d" % sl, xout_v[i], xt[:], reads=[xtok], writes=[("xout", i)])
            S.flush()


class Builder(Builder_, MixerMixin):
    def setup_globals(self):
        nc = self.nc
        self.c_ident = self.dram_in("c_ident", [128, 128])
        self.c_shift = nc.dram_tensor("c_shift", [128, 8, 128], BF16, kind="ExternalInput").ap()
        self.c_tri = self.dram_in("c_tri", [64, 8, 64])
        self.c_rot = self.dram_in("c_rot", [self.T, 256])
        self.c_flags = self.dram_in("c_flags", [128, 4])
        self.kf = self.sb(self.es, "kf", [128, 4], F32)
        self.S.dma("sp", "kf", self.kf[:], self.c_flags, writes=["kf"])
        self.S.flush()
        self.alloc_scratch()


def host_consts():
    import ml_dtypes
    c = {}
    c["c_ident"] = np.eye(128, dtype=np.float32)
    sh = np.zeros((128, 8, 128), np.float32)
    u = np.arange(128)[:, None]
    t = np.arange(128)[None, :]
    for oi, o in enumerate((-2, -1, 1, 2)):
        sh[:, oi, :] = (u == t + o)
        if o < 0:
            sh[:, 4 + oi, :] = (u == 128 + t + o)
        else:
            sh[:, 4 + oi, :] = (u == t + o - 128)
    c["c_shift"] = sh.astype(ml_dtypes.bfloat16)
    tri = np.zeros((64, 8, 64), np.float32)
    u = np.arange(64)[:, None]
    t = np.arange(64)[None, :]
    tri[:, 0, :] = (u <= t)
    tri[:, 1, :] = ((u >= 32) & (u <= t)) * 1.0 - ((u > t) & (u <= 31)) * 1.0
    tri[:, 2, :] = (u > t)
    tri[:, 3, :] = (u <= t)
    tri[:, 4, :] = (u >= t)
    tri[:, 5, :] = ((u >= t) & (u <= 31)) * 1.0 - ((u >= 32) & (u < t)) * 1.0
    tri[:, 6, :] = (u < t)
    tri[:, 7, :] = (u >= t)
    c["c_tri"] = tri
    return c


def rot_table(T, seq_len):
    pos = (np.arange(T) % seq_len).astype(np.float32)
    inv = (1.0 / (10000.0 ** (np.arange(0, 128, 2, dtype=np.float32) / np.float32(128)))).astype(np.float32)
    ang = (pos[:, None] * inv[None, :]).astype(np.float32)
    cos, sin = np.cos(ang).astype(np.float32), np.sin(ang).astype(np.float32)
    sc = np.float32(128.0 ** -0.5)
    return np.concatenate([cos, sin, cos * sc, sin * sc], axis=1).astype(np.float32)


WNAMES = ["ffn1_w_in", "ffn1_w_out", "ffn2_w_in", "ffn2_w_out", "ln_g", "ln_b", "ev_w_in", "ev_gate_b", "ev_conv_w",
          "ev_gla_a2_w", "ev_gla_a2_b", "ev_norm_w", "ev_w_out", "od_w_in", "od_lb_logits", "od_norm_w", "od_w_out"]


def build_program(T, bnd, shapes, depth=DEPTH):
    B = Builder(T, bnd, depth)
    B.BND = bnd
    B.setup_globals()
    W = {n: B.dram_in(n, shapes[n]) for n in WNAMES}
    x = B.dram_in("x", [T, 1024])
    y = B.dram_out("y", [T, 1024])
    xa = B.dram_tmp("s_xa", [T, 1024])
    xb = B.dram_tmp("s_xb", [T, 1024])
    cur = x
    for l in range(depth):
        j = l // 2
        B.ffn_phase(cur, xa, W["ffn1_w_in"][l], W["ffn1_w_out"][l], W["ln_g"][l, 0], W["ln_b"][l, 0])
        if l % 2 == 0:
            P = dict(w_in=W["ev_w_in"][j], gate_b=W["ev_gate_b"][j], conv_w=W["ev_conv_w"][j], a2_w=W["ev_gla_a2_w"][j],
                     a2_b=W["ev_gla_a2_b"][j], norm_w=W["ev_norm_w"][j])
            wo = W["ev_w_out"][j]
        else:
            P = dict(w_in=W["od_w_in"][j], lb_logits=W["od_lb_logits"], norm_w=W["od_norm_w"][j], layer_idx=j)
            wo = W["od_w_out"][j]
        B.proj_phase(xa, l % 2 == 0, P)
        B.scan_phase(l % 2 == 0)
        B.epi_phase(xa, xb, l % 2 == 0, wo, W["ln_g"][l, 1], W["ln_b"][l, 1])
        dst = y if l == depth - 1 else xa
        B.ffn_phase(xb, dst, W["ffn2_w_in"][l], W["ffn2_w_out"][l], W["ln_g"][l, 2], W["ln_b"][l, 2])
        cur = xa
    B.es.close()
    return B


def kernel(**inputs):
    xp = np.asarray(inputs["x_prompt"], np.float32)
    xs = np.asarray(inputs["x_sample"], np.float32)
    T = 16384
    BND = 4096
    weights = {n: np.ascontiguousarray(np.asarray(inputs[n], np.float32)) for n in WNAMES}
    shapes = {n: weights[n].shape for n in WNAMES}
    B = build_program(T, BND, shapes)
    consts = host_consts()
    rot_p = rot_table(T, 16384)
    rot_s = rot_table(T, 4096)
    fl_p = np.ones((128, 4), np.float32)
    fl_s = np.zeros((128, 4), np.float32)
    work = {0: (xp[0], True), 1: (xs[0:4].reshape(T, 1024), False), 4: (xp[1], True), 5: (xs[4:8].reshape(T, 1024), False)}
    zweights = {n: np.zeros_like(weights[n]) for n in WNAMES}
    zx = np.zeros((T, 1024), np.float32)
    maps = []
    for c in range(N_CORES):
        if c in work:
            xc, is_prompt = work[c]
            m = dict(weights)
            m["x"] = np.ascontiguousarray(xc)
        else:
            is_prompt = False
            m = dict(zweights)
            m["x"] = zx
        m.update(consts)
        m["c_rot"] = rot_p if is_prompt else rot_s
        m["c_flags"] = fl_p if is_prompt else fl_s
        maps.append(m)
    res = run_bass_kernel_spmd(B.nc, maps, core_ids=list(range(N_CORES)))
    r = res.results
    y_prompt = np.stack([np.asarray(r[0]["y"]), np.asarray(r[4]["y"])], 0).astype(np.float32)
    y_sample = np.concatenate([np.asarray(r[1]["y"]).reshape(4, 4096, 1024),
                               np.asarray(r[5]["y"]).reshape(4, 4096, 1024)], 0).astype(np.float32)
    return (y_prompt, y_sample)
```
